# Optimizing a Trainium2 kernel written in Bass

```python
import math
import jax, jax.numpy as jnp
from jax import lax
import numpy as np

D_MODEL = 1024
BATCH = 16
SEQ = 2048
DEPTH = 4

D_MIX = D_MODEL
A_HEADS = 4
A_QK_DIM = 32
A_V_DIM = 64
B_HEADS = 4
B_DIM = 64
B_CONFIGS = ((128, 1), (512, 4), (2048, 16))
B_BLOCK = 64
C_HEADS = 8
C_NOPE = 64
C_ROPE = 32
C_V = 64
C_Q_RANK = 256
C_KV_RANK = 128
ROPE_BASE = 10000.0
D_FF = 2816
CONV_W = 3
Q_BLOCK = 128
EPS = 1e-6
NEG = -1e30

A_Q_COLS = A_HEADS * 2 * A_QK_DIM
A_K_COLS = A_HEADS * 2 * A_QK_DIM
A_V_COLS = A_HEADS * A_V_DIM
B_COLS = B_HEADS * B_DIM
IN_SPLITS = (A_Q_COLS, A_K_COLS, A_V_COLS, B_COLS, B_COLS, B_COLS,
             C_Q_RANK, C_KV_RANK, C_ROPE)
N_IN = sum(IN_SPLITS)

kernel_name = "hybrid_diff_dilated_mla_encoder"


def _alibi_slopes():
    n = A_HEADS + B_HEADS
    s = (2.0 ** (-8.0 * np.arange(1, n + 1) / n)).astype(np.float32)
    return jnp.asarray(s[0::2]), jnp.asarray(s[1::2])


def _rmsnorm(x, g):
    xf = x.astype(jnp.float32)
    y = xf * lax.rsqrt(jnp.mean(xf * xf, axis=-1, keepdims=True) + EPS)
    return (y * g.astype(jnp.float32)).astype(x.dtype)


def _heads(t, n_heads):
    b, s, _ = t.shape
    return t.reshape(b, s, n_heads, -1).transpose(0, 2, 1, 3)


def _merge_heads(t):
    b, h, s, d = t.shape
    return t.transpose(0, 2, 1, 3).reshape(b, s, h * d)


def _rope(t, cos, sin):
    half = t.shape[-1] // 2
    t1 = t[..., :half].astype(jnp.float32)
    t2 = t[..., half:].astype(jnp.float32)
    return jnp.concatenate([t1 * cos - t2 * sin, t2 * cos + t1 * sin], axis=-1).astype(t.dtype)


def _sweep_queries(block_fn, qs):
    b, h, s, _ = qs[0].shape
    nq = s // Q_BLOCK
    blocks = tuple(q.reshape(b, h, nq, Q_BLOCK, q.shape[-1]).transpose(2, 0, 1, 3, 4) for q in qs)
    out = lax.map(lambda a: block_fn(a[0], *a[1]), (jnp.arange(nq), blocks))
    return out.transpose(1, 2, 0, 3, 4).reshape(b, h, s, -1)


def _diff_attention(q1, q2, k1, k2, v, lam, slopes):
    s_len = k1.shape[2]
    scale = A_QK_DIM ** -0.5
    kpos = jnp.arange(s_len)

    def block(i, qb1, qb2):
        qpos = i * Q_BLOCK + jnp.arange(Q_BLOCK)
        dist = jnp.abs(qpos[:, None] - kpos[None, :]).astype(jnp.float32)
        bias = -slopes[:, None, None] * dist
        s1 = jnp.einsum('bhqd,bhkd->bhqk', qb1, k1).astype(jnp.float32) * scale + bias
        s2 = jnp.einsum('bhqd,bhkd->bhqk', qb2, k2).astype(jnp.float32) * scale + bias
        p = jax.nn.softmax(s1, axis=-1) - lam * jax.nn.softmax(s2, axis=-1)
        return jnp.einsum('bhqk,bhkd->bhqd', p.astype(v.dtype), v)

    return _sweep_queries(block, (q1, q2))


def _dilated_branch(q, k, v, window, dil, slopes):
    b, h, s_len, d = q.shape
    half = window // (2 * dil)
    L = s_len // dil
    nb = -(-L // B_BLOCK)
    lp = nb * B_BLOCK
    scale = d ** -0.5

    def to_sub(t):
        t = t.reshape(b, h, L, dil, d).transpose(0, 1, 3, 2, 4)
        return jnp.pad(t, ((0, 0), (0, 0), (0, 0), (0, lp - L), (0, 0)))

    def band(t):
        tb = t.reshape(b, h, dil, nb, B_BLOCK, d)
        tb = jnp.pad(tb, ((0, 0), (0, 0), (0, 0), (1, 1), (0, 0), (0, 0)))
        return jnp.concatenate([tb[:, :, :, :-2], tb[:, :, :, 1:-1], tb[:, :, :, 2:]], axis=4)

    qb = to_sub(q).reshape(b, h, dil, nb, B_BLOCK, d)
    kb = band(to_sub(k))
    vb = band(to_sub(v))
    sc = jnp.einsum('bhrnqd,bhrnkd->bhrnqk', qb, kb).astype(jnp.float32) * scale
    a_idx = jnp.arange(B_BLOCK)
    c_idx = jnp.arange(3 * B_BLOCK)
    rel = c_idx[None, :] - B_BLOCK - a_idx[:, None]
    kidx = (jnp.arange(nb)[:, None, None] - 1) * B_BLOCK + c_idx[None, None, :]
    valid = (jnp.abs(rel) <= half)[None] & (kidx >= 0) & (kidx < L)
    dist = (dil * jnp.abs(rel)).astype(jnp.float32)
    bias = -slopes[:, None, None, None, None] * dist
    sc = jnp.where(valid, sc + bias, NEG)
    m = jnp.max(sc, axis=-1, keepdims=True)
    p = jnp.exp(sc - m)
    den = jnp.sum(p, axis=-1, keepdims=True)
    o = jnp.einsum('bhrnqk,bhrnkd->bhrnqd', (p / den).astype(v.dtype), vb)
    lse = (m + jnp.log(den))[..., 0]
    o = o.reshape(b, h, dil, lp, d)[:, :, :, :L].transpose(0, 1, 3, 2, 4).reshape(b, h, s_len, d)
    lse = lse.reshape(b, h, dil, lp)[:, :, :, :L].transpose(0, 1, 3, 2).reshape(b, h, s_len)
    return o, lse


def _dilated_attention(q, k, v, slopes):
    outs, lses = [], []
    for window, dil in B_CONFIGS:
        o, lse = _dilated_branch(q, k, v, window, dil, slopes)
        outs.append(o)
        lses.append(lse)
    w = jax.nn.softmax(jnp.stack(lses, axis=0), axis=0)
    o = jnp.sum(w[..., None] * jnp.stack(outs, axis=0).astype(jnp.float32), axis=0)
    return o.astype(q.dtype)


def _mla_attention(q_nope, q_rope, k_nope, k_rope, v):
    scale = (C_NOPE + C_ROPE) ** -0.5

    def block(i, qn, qr):
        s = (jnp.einsum('bhqd,bhkd->bhqk', qn, k_nope)
             + jnp.einsum('bhqr,bkr->bhqk', qr, k_rope)).astype(jnp.float32) * scale
        p = jax.nn.softmax(s, axis=-1)
        return jnp.einsum('bhqk,bhkd->bhqd', p.astype(v.dtype), v)

    return _sweep_queries(block, (q_nope, q_rope))


def setup_inputs(seed: int = 0) -> dict:
    key = jax.random.key(seed)
    ks = jax.random.split(key, 20)
    f32 = jnp.float32

    def nrm(k, shape, scale):
        return jax.random.normal(k, shape, f32) * scale

    def gain(k, shape):
        return 1.0 + 0.01 * jax.random.normal(k, shape, f32)

    return {
        "x": jax.random.normal(ks[0], (BATCH, SEQ, D_MODEL), f32),
        "w_in": nrm(ks[1], (DEPTH, D_MODEL, N_IN), D_MODEL ** -0.5),
        "g_attn": gain(ks[2], (DEPTH, D_MODEL)),
        "a_lq1": nrm(ks[3], (DEPTH, A_QK_DIM), 0.1),
        "a_lk1": nrm(ks[4], (DEPTH, A_QK_DIM), 0.1),
        "a_lq2": nrm(ks[5], (DEPTH, A_QK_DIM), 0.1),
        "a_lk2": nrm(ks[6], (DEPTH, A_QK_DIM), 0.1),
        "a_subln": gain(ks[7], (DEPTH, A_V_DIM)),
        "c_g_q": gain(ks[8], (DEPTH, C_Q_RANK)),
        "c_w_uq": nrm(ks[9], (DEPTH, C_Q_RANK, C_HEADS * (C_NOPE + C_ROPE)), C_Q_RANK ** -0.5),
        "c_g_kv": gain(ks[10], (DEPTH, C_KV_RANK)),
        "c_w_ukv": nrm(ks[11], (DEPTH, C_KV_RANK, C_HEADS * (C_NOPE + C_V)), C_KV_RANK ** -0.5),
        "w_out": nrm(ks[12], (DEPTH, D_MIX, D_MODEL), D_MIX ** -0.5),
        "g_ffn": gain(ks[13], (DEPTH, D_MODEL)),
        "w_up": nrm(ks[14], (DEPTH, D_MODEL, 2 * D_FF), D_MODEL ** -0.5),
        "conv_w": nrm(ks[15], (DEPTH, CONV_W, 2 * D_FF), CONV_W ** -0.5),
        "conv_b": nrm(ks[16], (DEPTH, 2 * D_FF), 0.01),
        "w_down": nrm(ks[17], (DEPTH, D_FF, D_MODEL), D_FF ** -0.5),
        "g_final": gain(ks[18], (D_MODEL,)),
    }


def reference(x, w_in, g_attn, a_lq1, a_lk1, a_lq2, a_lk2, a_subln, c_g_q, c_w_uq,
              c_g_kv, c_w_ukv, w_out, g_ffn, w_up, conv_w, conv_b, w_down, g_final):
    b, s_len, _ = x.shape
    slopes_a, slopes_b = _alibi_slopes()
    pos = jnp.arange(s_len, dtype=jnp.float32)
    inv_freq = ROPE_BASE ** (-jnp.arange(0, C_ROPE, 2, dtype=jnp.float32) / C_ROPE)
    ang = pos[:, None] * inv_freq[None, :]
    cos, sin = jnp.cos(ang), jnp.sin(ang)
    split_idx = [int(i) for i in np.cumsum(IN_SPLITS)[:-1]]

    for l in range(DEPTH):
        h = _rmsnorm(x, g_attn[l])
        proj = h @ w_in[l]
        a_q, a_k, a_v, b_q, b_k, b_v, c_q, c_kv, c_kr = jnp.split(proj, split_idx, axis=-1)

        a_q = a_q.reshape(b, s_len, A_HEADS, 2, A_QK_DIM).transpose(0, 2, 3, 1, 4)
        a_k = a_k.reshape(b, s_len, A_HEADS, 2, A_QK_DIM).transpose(0, 2, 3, 1, 4)
        lam_init = 0.8 - 0.6 * math.exp(-0.3 * l)
        lam = (jnp.exp(jnp.sum(a_lq1[l].astype(jnp.float32) * a_lk1[l].astype(jnp.float32)))
               - jnp.exp(jnp.sum(a_lq2[l].astype(jnp.float32) * a_lk2[l].astype(jnp.float32)))
               + lam_init)
        a_o = _diff_attention(a_q[:, :, 0], a_q[:, :, 1], a_k[:, :, 0], a_k[:, :, 1],
                              _heads(a_v, A_HEADS), lam, slopes_a)
        a_o = _rmsnorm(a_o, a_subln[l]) * (1.0 - lam_init)
        a_out = _merge_heads(a_o)

        b_o = _dilated_attention(_heads(b_q, B_HEADS), _heads(b_k, B_HEADS),
                                 _heads(b_v, B_HEADS), slopes_b)
        b_out = _merge_heads(b_o)

        cq = _rmsnorm(c_q, c_g_q[l]) @ c_w_uq[l]
        cq = cq.reshape(b, s_len, C_HEADS, C_NOPE + C_ROPE)
        q_nope = cq[..., :C_NOPE].transpose(0, 2, 1, 3)
        q_rope = _rope(cq[..., C_NOPE:], cos[:, None, :], sin[:, None, :]).transpose(0, 2, 1, 3)
        ckv = (_rmsnorm(c_kv, c_g_kv[l]) @ c_w_ukv[l]).reshape(b, s_len, C_HEADS, C_NOPE + C_V)
        k_nope = ckv[..., :C_NOPE].transpose(0, 2, 1, 3)
        c_v = ckv[..., C_NOPE:].transpose(0, 2, 1, 3)
        k_rope = _rope(c_kr, cos, sin)
        c_out = _merge_heads(_mla_attention(q_nope, q_rope, k_nope, k_rope, c_v))

        mix = jnp.concatenate([a_out, b_out, c_out], axis=-1)
        x = x + mix @ w_out[l]

        h = _rmsnorm(x, g_ffn[l])
        u = h @ w_up[l]
        up = jnp.pad(u, ((0, 0), (1, 1), (0, 0)))
        cw = conv_w[l]
        u = up[:, :-2] * cw[0] + up[:, 1:-1] * cw[1] + up[:, 2:] * cw[2] + conv_b[l]
        gate, val = u[..., :D_FF], u[..., D_FF:]
        x = x + (jax.nn.silu(gate) * val) @ w_down[l]

    return _rmsnorm(x, g_final)
```

```python
import math
import numpy as np
import ml_dtypes
import concourse.bass as bass
import concourse.mybir as mybir
from concourse.bass_utils import run_bass_kernel_spmd

F32 = mybir.dt.float32
BF16 = mybir.dt.bfloat16
AF = mybir.ActivationFunctionType
ALU = mybir.AluOpType
AX = mybir.AxisListType

S = 2048
D = 1024
DEPTH = 4
NCORES = 8
NSEQ = 2
DFF = 2816
NJ = 22
EPS = 1e-6
SL_A = [2.0 ** -1, 2.0 ** -3, 2.0 ** -5, 2.0 ** -7]
SL_B = [2.0 ** -2, 2.0 ** -4, 2.0 ** -6, 2.0 ** -8]
SL_IDX_A = [0, 2, 4, 6]
SL_IDX_B = [1, 3, 5, 7]
OFFB = 1408
CBW = 2944
VL = 387
NV = DEPTH * VL + 8
GROUPS = [list(range(0, 12)), list(range(12, 22))]
SLOTS_PER_LAYER = 17 + (12 + 8) + (10 + 8)
C_ID, C_ONES, C_ED, C_ROPE, C_RA, C_RB, C_CB = 0, 128, 256, 1280, 5376, 11520, 17664
NC16 = C_CB + CBW
R_RING = 4
LOOK = 2


class Tr:
    def __init__(self):
        self.ops = []
        self.st = {}
        self.carry = {}

    def _m(self, d, idx):
        e = self.ops[idx][0]
        key = e if e != 'sp' else ('sp', idx)
        if d.get(key, -1) < idx:
            d[key] = idx

    @staticmethod
    def _k(k):
        return (k, None) if isinstance(k, str) else k

    def op(self, eng, fn, r=(), w=()):
        deps = {}
        for k in r:
            t, sub = self._k(k)
            for i in self.carry.get(t, {}).values():
                self._m(deps, i)
            s = self.st.get(t, {}).get(sub)
            if s is not None and s[0] is not None:
                self._m(deps, s[0])
        for k in w:
            t, sub = self._k(k)
            for i in self.carry.get(t, {}).values():
                self._m(deps, i)
            s = self.st.get(t, {}).get(sub)
            if s is not None:
                if s[0] is not None:
                    self._m(deps, s[0])
                for i in s[1].values():
                    self._m(deps, i)
        idx = len(self.ops)
        self.ops.append([eng, fn, sorted(deps.values()), False])
        for k in r:
            t, sub = self._k(k)
            s = self.st.setdefault(t, {}).setdefault(sub, [None, {}])
            key = eng if eng != 'sp' else ('sp', idx)
            s[1][key] = idx
        for k in w:
            t, sub = self._k(k)
            self.st.setdefault(t, {})[sub] = [idx, {}]
        return idx

    def retire(self, tiles):
        for t in tiles:
            c = self.carry.setdefault(t, {})
            for sub, s in self.st.get(t, {}).items():
                if s[0] is not None:
                    self._m(c, s[0])
                for i in s[1].values():
                    self._m(c, i)
            self.st[t] = {}

    def emit(self, nc, sems, dsems):
        ops = self.ops
        for o in ops:
            for d in o[2]:
                if not (ops[d][0] == 'pe' and o[0] == 'pe'):
                    ops[d][3] = True
        engobj = {'pe': nc.tensor, 'act': nc.scalar, 'dve': nc.vector, 'pool': nc.gpsimd, 'sp': nc.sync}
        cnt = {e: 0 for e in sems}
        seen = {e: {} for e in engobj}
        dcnt = [0] * len(dsems)
        drot = 0
        ev = [None] * len(ops)
        for i, o in enumerate(ops):
            eng = o[0]
            eo = engobj[eng]
            for d in o[2]:
                if ops[d][0] == 'pe' and eng == 'pe':
                    continue
                sid, sem, val = ev[d]
                if seen[eng].get(sid, 0) < val:
                    eo.wait_ge(sem, val)
                    seen[eng][sid] = val
            if eng == 'sp':
                j = drot
                drot = (drot + 1) % len(dsems)
                sid = ('d', j)
                if seen[eng].get(sid, 0) < dcnt[j]:
                    eo.wait_ge(dsems[j], dcnt[j])
                    seen[eng][sid] = dcnt[j]
                inst = o[1]()
                dcnt[j] += 16
                inst.then_inc(dsems[j], 16)
                ev[i] = (sid, dsems[j], dcnt[j])
            else:
                inst = o[1]()
                if o[3]:
                    cnt[eng] += 1
                    inst.then_inc(sems[eng], 1)
                    ev[i] = (eng, sems[eng], cnt[eng])
        for j in range(len(dsems)):
            if dcnt[j] > 0:
                nc.sync.wait_ge(dsems[j], dcnt[j])


class Builder:
    def __init__(self, depth=DEPTH, nseq=NSEQ, dbg=None):
        self.depth = depth
        self.nseq = nseq
        self.dbg = dbg
        self.tr = Tr()
        self.srot = 0
        self.arot = 0
        self.brot = 0
        self.prot = 0
        self.erot = 0
        self.cast_eng = 'dve'
        self.pv_lag = 2
        self.bank_pool = list(range(8))
        self.acc_pool = [4, 5, 6, 7]

    def xt(self, c, t0, n):
        return self.XT[:, c * S + t0: c * S + t0 + n]

    def ht(self, c, t0, n):
        return self.HT[:, c * S + t0: c * S + t0 + n]

    def hk(self, c, t0, n=1):
        return [('H%d' % c, tt) for tt in range(t0 // 512, (t0 + n - 1) // 512 + 1)]

    def g(self, i):
        return self.G[:, i * 2048:(i + 1) * 2048]

    def gf(self, i):
        return self.G[:, i * 2048:(i + 1) * 2048].bitcast(F32)

    def h(self, i):
        return self.HT[:, i * 2048:(i + 1) * 2048]

    def ps(self, b, n=512, p0=0, p1=128, c0=0):
        return self.PS[p0:p1, b * 512 + c0: b * 512 + c0 + n]

    def psb(self, b):
        return self.PS[:, b * 512:(b + 1) * 512].bitcast(BF16)

    def vec(self, l, off, n=1):
        return self.VEC[:, l * VL + off: l * VL + off + n]

    def ptk(self, pp):
        return self.PT_PAIRS[pp][1]

    def ptv(self, pp):
        return self.PT_PAIRS[pp][0]

    def set_pt(self, specs):
        self.PT_PAIRS = [(self.G[:, t * 2048 + c0: t * 2048 + c0 + 1024], ('G%d' % t, ('pt', c0))) for (t, c0) in specs]
        self.prot = 0

    def nbank(self):
        pool = self.bank_pool
        b = pool[self.brot % len(pool)]
        self.brot += 1
        return b

    def op(self, eng, fn, r=(), w=()):
        return self.tr.op(eng, fn, r, w)

    def mm(self, out, lhsT, rhs, start, stop, r, w, **kw):
        nc = self.nc
        return self.op('pe', lambda: nc.tensor.matmul(out, lhsT=lhsT, rhs=rhs, start=start, stop=stop, **kw), r, w)

    def act(self, out, in_, func, r, w, scale=None, bias=None):
        nc = self.nc
        kw = {}
        if scale is not None:
            kw['scale'] = scale
        if bias is not None:
            kw['bias'] = bias
        return self.op('act', lambda: nc.scalar.activation(out=out, in_=in_, func=func, **kw), r, w)

    def tt(self, eng, out, in0, in1, op, r, w):
        nc = self.nc
        e = nc.vector if eng == 'dve' else nc.gpsimd
        return self.op(eng, lambda: e.tensor_tensor(out=out, in0=in0, in1=in1, op=op), r, w)

    def ts(self, eng, out, in0, s1, op0, r, w, s2=None, op1=None):
        nc = self.nc
        e = nc.vector if eng == 'dve' else nc.gpsimd
        if op1 is None:
            return self.op(eng, lambda: e.tensor_scalar(out=out, in0=in0, scalar1=s1, scalar2=None, op0=op0), r, w)
        return self.op(eng, lambda: e.tensor_scalar(out=out, in0=in0, scalar1=s1, scalar2=s2, op0=op0, op1=op1), r, w)

    def stt(self, out, in0, scalar, in1, op0, op1, r, w):
        nc = self.nc
        return self.op('dve', lambda: nc.vector.scalar_tensor_tensor(out=out, in0=in0, scalar=scalar, in1=in1,
                                                                     op0=op0, op1=op1), r, w)

    def cp(self, eng, out, in_, r, w):
        nc = self.nc
        e = {'dve': nc.vector, 'pool': nc.gpsimd}[eng]
        return self.op(eng, lambda: e.tensor_copy(out=out, in_=in_), r, w)

    def dma(self, out, in_, r, w):
        nc = self.nc
        return self.op('sp', lambda: nc.sync.dma_start(out=out, in_=in_), r, w)

    def ws_init(self, sched):
        self.wsched = sched
        self.wnext = 0
        self.wcons = 0

    def wget(self):
        i = self.wcons
        self.wcons += 1
        lim = min(i + LOOK, len(self.wsched) - 1)
        nc = self.nc
        while self.wnext <= lim:
            j = self.wnext
            slot, n = self.wsched[j]
            st = j % 2
            self.dma(self.WST[:, st * 2048: st * 2048 + n], self.wst[slot, :, 0:n], [], [('WST', st)])
            ce = self.cast_eng
            if ce == 'act':
                self.act(self.WR[:, (j % R_RING) * 2048:(j % R_RING) * 2048 + n],
                         self.WST[:, st * 2048: st * 2048 + n], AF.Copy, [('WST', st)], [('WR', j % R_RING)])
            else:
                self.cp(ce, self.WR[:, (j % R_RING) * 2048:(j % R_RING) * 2048 + n],
                        self.WST[:, st * 2048: st * 2048 + n], [('WST', st)], [('WR', j % R_RING)])
            self.wnext += 1
        r = i % R_RING
        return self.WR[:, r * 2048:(r + 1) * 2048], ('WR', r)

    def rmsnorm_x(self, l, goff, final=False, seq=0):
        for tt in range(4):
            t0 = tt * 512
            sb = self.nbank()
            for c in range(8):
                q = (tt * 8 + c) % 3
                self.act(self.SQ[:, q * 512:(q + 1) * 512], self.xt(c, t0, 512), AF.Square,
                         [('XT', (c, tt))], [('SQ', q)])
                self.mm(self.ps(sb), self.ONES, self.SQ[:, q * 512:(q + 1) * 512], c == 0, c == 7,
                        [('SQ', q), 'C16'], ['PS%d' % sb])
            self.act(self.LNV[:, 0:512], self.ps(sb), AF.Ln, ['PS%d' % sb, 'EPS'], ['LNV'],
                     scale=1.0 / D, bias=self.EPSV[:, 0:1])
            rq = 0
            self.act(self.RSTD[:, rq * 512:(rq + 1) * 512], self.LNV[:, 0:512], AF.Exp, ['LNV'], [('RSTD', rq)],
                     scale=-0.5)
            for c in range(8):
                gcol = self.VEC[:, DEPTH * VL + c: DEPTH * VL + c + 1] if final else self.vec(l, goff + c)
                if not final:
                    self.stt(self.ht(c, t0, 512), self.xt(c, t0, 512), gcol, self.RSTD[:, rq * 512:(rq + 1) * 512],
                             ALU.mult, ALU.mult, [('XT', (c, tt)), ('RSTD', rq), 'VEC'], [('H%d' % c, tt)])
                else:
                    gi = (tt % 2) * 4 + c // 2
                    o = self.gf(gi)[:, (c % 2) * 512:(c % 2) * 512 + 512]
                    self.stt(o, self.xt(c, t0, 512), gcol, self.RSTD[:, rq * 512:(rq + 1) * 512],
                             ALU.mult, ALU.mult, [('XT', (c, tt)), ('RSTD', rq), 'VEC'], [('G%d' % gi, c % 2)])
                    self.dma(self.outT[seq, c * 128:(c + 1) * 128, t0:t0 + 512], o, [('G%d' % gi, c % 2)], [])

    def proj_fm(self, wap, wkey, wcol0, M, src, nk, evac):
        for tt in range(4):
            b = self.nbank()
            for c in range(nk):
                rhs, rk = src(c, tt)
                self.mm(self.ps(b, 512, 0, M), wap(c, wcol0, M), rhs, c == 0, c == nk - 1, [wkey] + rk, ['PS%d' % b])
            evac(tt, b)

    def evac_eng(self):
        self.erot += 1
        return 'act' if self.erot % 2 else 'dve'

    def copy_scaled(self, out, in_, scale, r, w, eng=None):
        eng = eng or self.evac_eng()
        if eng == 'act':
            self.act(out, in_, AF.Copy, r, w, scale=float(scale))
        else:
            self.ts('dve', out, in_, float(scale), ALU.mult, r, w)

    def attn_run(self, jobs, hooks=None):
        pend = []
        deferred = []
        flat = []
        for job in jobs:
            blocks = job['blocks']
            if len(job['lanes']) == 2:
                steps = [[(0, kc), (1, kc)] for kc in blocks]
            else:
                steps = [[(0, kc) for kc in blocks[i:i + 2]] for i in range(0, len(blocks), 2)]
            job['started'] = set()
            for si, st in enumerate(steps):
                flat.append((job, st, si == len(steps) - 1))
        stepno = 0
        for (job, st, last) in flat:
            sp = self.srot
            self.srot = (self.srot + 1) % 2
            pp = self.prot
            self.prot = (self.prot + 1) % len(self.PT_PAIRS)
            qt = job['qt']
            q0 = qt * 512
            ramp = job['ramp']
            skeys = ['PS%d' % (2 * sp), 'PS%d' % (2 * sp + 1)]
            info = []
            fold_info = []
            for bi, (ln, kc) in enumerate(st):
                L = job['lanes'][ln]
                bk = 2 * sp + bi
                k0 = kc * 128
                if not job.get('fold'):
                    kap, kk = L['kT'](kc)
                    qap, qk = L['qT'](qt)
                a = k0 + 128 - q0
                b = k0 - q0
                segs = []
                if ramp is not None:
                    if a < 512:
                        segs.append((max(0, a), 512, 0))
                    if b > 0:
                        segs.append((0, min(512, b), 1))
                kw = {}
                if L['base'] == 96:
                    kw['tile_position'] = (96, 0)
                if job.get('fold'):
                    if b >= 512:
                        fs = [(0, 512, 'L')]
                    elif b <= -128:
                        fs = [(0, 512, 'R')]
                    else:
                        fs = []
                        if b > 0:
                            fs.append((0, b, 'L'))
                        fs.append((b, b + 128, 'D'))
                        if b + 128 < 512:
                            fs.append((b + 128, 512, 'R'))
                    fold_info.append((L, bk, kc, fs, bi))
                    continue
                self.mm(self.ps(bk), kap, qap, True, len(segs) == 0, kk + qk, [skeys[bi]], **kw)
                info.append((L, bk, k0, segs, kw, bi))
            if fold_info:
                for si in range(3):
                    for (L, bk, kc, fs, bi) in fold_info:
                        if si < len(fs):
                            lo, hi, var = fs[si]
                            kap, kk = L['kTv'](kc, var)
                            qap, qk = L['qTv'](q0 + lo, hi - lo, var)
                            self.mm(self.ps(bk, hi - lo, 0, 128, lo), kap, qap, True, True, kk + qk, [skeys[bi]],
                                    skip_group_check=True)
            for (L, bk, k0, segs, kw, bi) in info:
                base, K = L['base'], L['K']
                for si, (lo, hi, sgn) in enumerate(segs):
                    rk_tab = ramp[1] if sgn else ramp[0]
                    self.mm(self.ps(bk, hi - lo, 0, 128, lo), rk_tab[base:base + K, k0:k0 + 128],
                            ramp[2][base:base + K, q0 + lo:q0 + hi], False, si == len(segs) - 1,
                            list(ramp[3]), [skeys[bi]], **kw)
            nb = len(st)
            scales = [job['lanes'][ln]['scale'] for (ln, kc) in st]
            ptt = self.ptv(pp)
            if all(sc == scales[0] for sc in scales):
                self.act(ptt[:, 0: nb * 512], self.PS[:, 2 * sp * 512: 2 * sp * 512 + nb * 512],
                         AF.Exp, skeys[:nb], [self.ptk(pp)], scale=float(scales[0]))
            else:
                for bi in range(nb):
                    self.act(ptt[:, bi * 512: (bi + 1) * 512], self.ps(2 * sp + bi),
                             AF.Exp, [skeys[bi]], [self.ptk(pp)], scale=float(scales[bi]))
            if job.get('mult_both') is not None and nb == 2:
                for (tab, tk, eng) in job['mult_both'](st[0][1], qt):
                    v = ptt[:, 0:1024].rearrange("p (b c) -> p b c", b=2)
                    self.tt(eng, v, v, tab.unsqueeze(1).to_broadcast([128, 2, 512]), ALU.mult,
                            [self.ptk(pp)] + tk, [self.ptk(pp)])
            for bi, (ln, kc) in enumerate(st):
                L = job['lanes'][ln]
                if L['mult'] is not None:
                    for (c0, n, tab, tk, eng) in L['mult'](kc, qt):
                        v = ptt[:, bi * 512 + c0: bi * 512 + c0 + n]
                        self.tt(eng, v, v, tab, ALU.mult, [self.ptk(pp)] + tk, [self.ptk(pp)])
            def pv(job=job, st=st, pp=pp, last=last):
                ptt_ = self.ptv(pp)
                for bi, (ln, kc) in enumerate(st):
                    L = job['lanes'][ln]
                    acc = L['acc']
                    vap, vk = L['vT'](kc)
                    for qc in range(4):
                        stf = acc not in job['started']
                        job['started'].add(acc)
                        lhs = ptt_[:, bi * 512 + qc * 128: bi * 512 + qc * 128 + 128]
                        self.mm(self.ps(acc, 65, 0, 128, qc * 65), lhs, vap, stf, last,
                                [self.ptk(pp)] + vk, ['PS%d' % acc], skip_group_check=True)
                if last:
                    for d in (job['fin'](job) or []):
                        deferred.append([3, d])
            pend.append(pv)
            if len(pend) > self.pv_lag:
                pend.pop(0)()
            if hooks and stepno in hooks:
                hooks[stepno]()
            stepno += 1
            for dd in deferred:
                dd[0] -= 1
            while deferred and deferred[0][0] <= 0:
                deferred.pop(0)[1]()
        while pend:
            pend.pop(0)()
        for dd in deferred:
            dd[1]()

    def acc_bank(self):
        b = self.acc_pool[self.arot % len(self.acc_pool)]
        self.arot += 1
        return b

    def transposes_to_mix(self, obuf, okey, mixtile, h2, qt, bank):
        nc = self.nc
        bk = bank
        r0 = 64 * h2
        pv = self.psb(bk)
        for qc in range(4):
            o = pv[r0:r0 + 64, qc * 128:(qc + 1) * 128]
            i_ = obuf[:, qc * 64:(qc + 1) * 64]
            self.op('pe', lambda o=o, i_=i_: nc.tensor.transpose(o, i_, self.IDENT), [okey, 'C16'], ['PS%d' % bk])
        self.cp('dve', self.g(mixtile)[r0:r0 + 64, qt * 512:(qt + 1) * 512], pv[r0:r0 + 64, 0:512],
                ['PS%d' % bk], [('G%d' % mixtile, (h2, qt))])

    def out_proj(self, wlist, mixtiles):
        nfc = len(mixtiles)
        for tt in range(4):
            for oc in range(8):
                b = self.nbank()
                for fc in range(nfc):
                    wap, wk = wlist[fc // 2]
                    lhs = wap[:, (fc % 2) * 1024 + oc * 128:(fc % 2) * 1024 + oc * 128 + 128]
                    mt = mixtiles[fc]
                    self.mm(self.ps(b), lhs, self.g(mt)[:, tt * 512:(tt + 1) * 512], fc == 0, fc == nfc - 1,
                            [wk] + [('G%d' % mt, (h2, tt)) for h2 in range(2)], ['PS%d' % b])
                self.tt('dve', self.xt(oc, tt * 512, 512), self.ps(b), self.xt(oc, tt * 512, 512), ALU.add,
                        ['PS%d' % b, ('XT', (oc, tt))], [('XT', (oc, tt))])

    def mixer_ab(self, l, which):
        nc = self.nc
        tr = self.tr
        isA = which == 'A'
        QT = [0, 1]
        KT = [2, 3]
        RT = [4, 5, 6]
        MIX = [8, 9]
        CBT = [10, 11]
        tr.retire(['G%d' % i for i in range(12)])
        if isA:
            self.set_pt([(7, 0), (7, 1024), (10, 0), (10, 1024)])
        else:
            self.set_pt([(7, 0), (7, 1024), (11, 1024)])
        rbase = C_RA if isA else C_RB
        for i in range(3):
            self.dma(self.g(RT[i]), self.c16[:, rbase + i * 2048: rbase + (i + 1) * 2048], [], ['G%d' % RT[i]])
        if not isA:
            self.dma(self.G[:, CBT[0] * 2048: CBT[0] * 2048 + CBW], self.c16[:, C_CB:C_CB + CBW], [], ['G%d' % CBT[0], 'G%d' % CBT[1]])
        slopes = SL_A if isA else SL_B
        qk_scale = (32.0 ** -0.5) if isA else (64.0 ** -0.5)

        def wfm(w):
            return lambda c, col0, M: w[:, (col0 // 128) * 1024 + c * 128: (col0 // 128) * 1024 + c * 128 + M]

        def src(c, tt):
            return self.ht(c, tt * 512, 512), [('H%d' % c, tt)]

        wq, wqk = self.wget()
        for i in range(2):
            def evq(tt, b, i=i):
                for hh in range(2):
                    hd = 2 * i + hh
                    self.copy_scaled(self.g(QT[i])[64 * hh:64 * hh + 64, tt * 512:(tt + 1) * 512],
                                     self.ps(b, 512, 64 * hh, 64 * hh + 64),
                                     qk_scale / (slopes[hd] if isA else slopes[2 * i]),
                                     ['PS%d' % b], [('G%d' % QT[i], (hh, tt))])
            self.proj_fm(wfm(wq), wqk, i * 128, 128, src, 8, evq)
        wk_, wkk = self.wget()
        for i in range(2):
            def evk(tt, b, i=i):
                self.copy_scaled(self.g(KT[i])[:, tt * 512:(tt + 1) * 512], self.ps(b), 1.0,
                                 ['PS%d' % b], [('G%d' % KT[i], tt)])
            self.proj_fm(wfm(wk_), wkk, i * 128, 128, src, 8, evk)
        wv, wvk = self.wget()
        for tc in range(16):
            b = self.nbank()
            for c in range(8):
                self.mm(self.ps(b, 256), self.ht(c, tc * 128, 128), wv[:, c * 256:(c + 1) * 256], c == 0, c == 7,
                        [wvk, ('H%d' % c, tc // 4)], ['PS%d' % b])
            ov = self.VT[:, tc * 260:(tc + 1) * 260].rearrange("p (h e) -> p h e", e=65)[:, :, 0:64]
            iv = self.ps(b, 256).rearrange("p (h e) -> p h e", e=64)
            eng = self.evac_eng()
            if eng == 'act':
                self.act(ov, iv, AF.Copy, ['PS%d' % b], [('VT', tc)])
            else:
                self.cp('dve', ov, iv, ['PS%d' % b], [('VT', tc)])

        ramp = (self.g(RT[0]), self.g(RT[1]), self.g(RT[2]), 'G%d' % RT[0])
        rkeys = ['G%d' % t for t in RT]
        jobs = []
        lam_init = 0.8 - 0.6 * math.exp(-0.3 * l)

        def mk_lane(hd, base, K, acc):
            ti = hd // 2
            sidx = (SL_IDX_A if isA else SL_IDX_B)[hd]

            def kT(kc):
                return self.g(KT[ti])[base:base + K, kc * 128:(kc + 1) * 128], [('G%d' % KT[ti], kc // 4)]

            def qT(qt_):
                return (self.g(QT[ti])[base:base + K, qt_ * 512:(qt_ + 1) * 512],
                        [('G%d' % QT[ti], ((base // 64), qt_))])

            def vT(kc):
                return self.VT[:, (kc * 4 + hd) * 65:(kc * 4 + hd) * 65 + 65], [('VT', kc)]

            def mult(kc, qt_):
                res = []
                dlt = qt_ * 512 - kc * 128
                b_ = -dlt
                if 0 <= b_ < 512:
                    res.append((b_, 128, self.C16[:, C_ED + sidx * 128: C_ED + (sidx + 1) * 128], ['C16'], 'pool'))
                return res
            return dict(K=K, base=base, kT=kT, qT=qT, vT=vT, acc=acc,
                        scale=(slopes[hd] if isA else slopes[2 * (hd // 2)]), mult=mult)

        def mult_both(kc, qt_):
            dlt = qt_ * 512 - kc * 128
            return [(self.G[:, CBT[0] * 2048 + dlt + OFFB: CBT[0] * 2048 + dlt + OFFB + 512],
                     ['G%d' % CBT[0], 'G%d' % CBT[1]], 'dve')]

        if isA:
            for hd in range(4):
                for qt in range(4):
                    accs = [self.acc_bank(), self.acc_bank()]
                    lanes = [mk_lane(hd, 64 * (hd % 2) + 32 * mp, 32, accs[mp]) for mp in range(2)]

                    def fin(job, hd=hd, qt=qt, accs=accs):
                        return [self.fin_ab(l, True, hd, qt, accs, MIX[hd // 2], lam_init)]
                    jobs.append(dict(lanes=lanes, qt=qt, blocks=list(range(16)),
                                     ramp=(ramp[0], ramp[1], ramp[2], rkeys), fin=fin))
        else:
            for hp in range(2):
                for qt in range(4):
                    accs = [self.acc_bank(), self.acc_bank()]
                    lanes = [mk_lane(2 * hp + hh, 64 * hh, 64, accs[hh]) for hh in range(2)]
                    blocks = [kc for kc in range(16) if -1408 <= qt * 512 - kc * 128 <= 1024]

                    def fin(job, hp=hp, qt=qt, accs=accs):
                        return [self.fin_ab(l, False, 2 * hp + hh, qt, [accs[hh]], MIX[hp], 0.0) for hh in range(2)]
                    jobs.append(dict(lanes=lanes, qt=qt, blocks=blocks, mult_both=mult_both,
                                     ramp=(ramp[0], ramp[1], ramp[2], rkeys), fin=fin))
        self.ramp_keys = rkeys
        self.attn_run(jobs)
        wo, wok = self.wget()
        self.out_proj([(wo, wok)], MIX)

    def mixer_a2(self, l):
        tr = self.tr
        QA = [0, 1]
        KR = [2, 3]
        KL = [4, 5]
        MIX = [8, 9]
        tr.retire(['G%d' % i for i in range(12)])
        self.set_pt([(7, 0), (7, 1024), (10, 0), (10, 1024)])
        lam_init = 0.8 - 0.6 * math.exp(-0.3 * l)
        qk_scale = 32.0 ** -0.5
        def src(c, tt):
            return self.ht(c, tt * 512, 512), [('H%d' % c, tt)]
        wv, wvk = self.wget()
        for tc in range(16):
            b = self.nbank()
            for c in range(8):
                self.mm(self.ps(b, 256), self.ht(c, tc * 128, 128), wv[:, c * 256:(c + 1) * 256], c == 0, c == 7,
                        [wvk, ('H%d' % c, tc // 4)], ['PS%d' % b])
            ov = self.VT[:, tc * 260:(tc + 1) * 260].rearrange("p (h e) -> p h e", e=65)[:, :, 0:64]
            iv = self.ps(b, 256).rearrange("p (h e) -> p h e", e=64)
            if self.evac_eng() == 'act':
                self.act(ov, iv, AF.Copy, ['PS%d' % b], [('VT', tc)])
            else:
                self.cp('dve', ov, iv, ['PS%d' % b], [('VT', tc)])
        for hp in range(2):
            for hh in range(2):
                for (tiles, coff) in ((QA, C_RA + 2 * S), (KR, C_RA), (KL, C_RA + S)):
                    for base in (0, 64):
                        self.dma(self.g(tiles[hh])[base + 32: base + 64, :], self.c16[0:32, coff:coff + S], [],
                                 [('G%d' % tiles[hh], ('aug', base))])
            wq, wqk = self.wget()
            for hh in range(2):
                hd = 2 * hp + hh
                for tt in range(4):
                    b = self.nbank()
                    for c in range(8):
                        rhs, rk = src(c, tt)
                        self.mm(self.ps(b), wq[:, hh * 1024 + c * 128: hh * 1024 + c * 128 + 128], rhs, c == 0, c == 7,
                                [wqk] + rk, ['PS%d' % b])
                    for base in (0, 64):
                        self.copy_scaled(self.g(QA[hh])[base:base + 32, tt * 512:(tt + 1) * 512],
                                         self.ps(b, 512, base, base + 32), qk_scale / SL_A[hd],
                                         ['PS%d' % b], [('G%d' % QA[hh], ('f', base, tt))], eng='dve')
            wk_, wkk = self.wget()
            for hh in range(2):
                for tt in range(4):
                    b = self.nbank()
                    for c in range(8):
                        rhs, rk = src(c, tt)
                        self.mm(self.ps(b), wk_[:, hh * 1024 + c * 128: hh * 1024 + c * 128 + 128], rhs, c == 0, c == 7,
                                [wkk] + rk, ['PS%d' % b])
                    for base in (0, 64):
                        for tiles in (KR, KL):
                            self.copy_scaled(self.g(tiles[hh])[base:base + 32, tt * 512:(tt + 1) * 512],
                                             self.ps(b, 512, base, base + 32), 1.0,
                                             ['PS%d' % b], [('G%d' % tiles[hh], ('f', base, tt))], eng='dve')
            jobs = []
            for hh in range(2):
                hd = 2 * hp + hh
                sidx = SL_IDX_A[hd]
                for qt in range(4):
                    accs = [self.acc_bank(), self.acc_bank()]
                    lanes = []
                    for mp in range(2):
                        base = 64 * mp

                        def kTv(kc, var, hh=hh, base=base):
                            t = (KL if var == 'L' else KR)[hh]
                            K = 32 if var == 'D' else 64
                            keys = [('G%d' % t, ('f', base, kc // 4))]
                            if var != 'D':
                                keys.append(('G%d' % t, ('aug', base)))
                            return self.g(t)[base:base + K, kc * 128:(kc + 1) * 128], keys

                        def qTv(c0, n, var, hh=hh, base=base):
                            K = 32 if var == 'D' else 64
                            keys = [('G%d' % QA[hh], ('f', base, tq)) for tq in range(c0 // 512, (c0 + n - 1) // 512 + 1)]
                            if var != 'D':
                                keys.append(('G%d' % QA[hh], ('aug', base)))
                            return self.g(QA[hh])[base:base + K, c0:c0 + n], keys

                        def vT(kc, hd=hd):
                            return self.VT[:, (kc * 4 + hd) * 65:(kc * 4 + hd) * 65 + 65], [('VT', kc)]

                        def mult(kc, qt_, sidx=sidx):
                            b_ = kc * 128 - qt_ * 512
                            if 0 <= b_ < 512:
                                return [(b_, 128, self.C16[:, C_ED + sidx * 128: C_ED + (sidx + 1) * 128], ['C16'], 'pool')]
                            return []
                        lanes.append(dict(K=64, base=base, kTv=kTv, qTv=qTv, vT=vT, acc=accs[mp], scale=SL_A[hd],
                                          mult=mult))

                    def fin(job, hd=hd, qt=qt, accs=accs, hp=hp):
                        return [self.fin_ab(l, True, hd, qt, accs, MIX[hp], lam_init)]
                    jobs.append(dict(lanes=lanes, qt=qt, blocks=list(range(16)), ramp=None, fold=True, fin=fin))
            self.attn_run(jobs)
        wo, wok = self.wget()
        self.out_proj([(wo, wok)], MIX)

    def fin_ab(self, l, isA, hd, qt, accs, mixtile, lam_init):
        nc = self.nc
        ob = self.OB[:, (self.obrot % 2) * 256:(self.obrot % 2) * 256 + 256]
        okey = ('OB', self.obrot % 2)
        self.obrot += 1
        ob3 = ob.rearrange("p (q e) -> p q e", e=64)

        def acc3(b):
            return self.PS[:, b * 512: b * 512 + 260].rearrange("p (q e) -> p q e", e=65)
        a0 = acc3(accs[0])
        R1 = self.SM[:, 0:4]
        self.op('dve', lambda: nc.vector.reciprocal(out=R1, in_=a0[:, :, 64]), ['PS%d' % accs[0]], [('SM', 0)])
        r1b = R1.unsqueeze(2).to_broadcast([128, 4, 64])
        if not isA:
            self.tt('dve', ob3, a0[:, :, 0:64], r1b, ALU.mult, ['PS%d' % accs[0], ('SM', 0)], [okey])
        else:
            a1 = acc3(accs[1])
            O1 = self.OF[:, 0:256].rearrange("p (q e) -> p q e", e=64)
            O2 = self.OF[:, 256:512].rearrange("p (q e) -> p q e", e=64)
            self.tt('dve', O1, a0[:, :, 0:64], r1b, ALU.mult, ['PS%d' % accs[0], ('SM', 0)], [('OF', 0)])
            R2 = self.SM[:, 4:8]
            self.op('dve', lambda: nc.vector.reciprocal(out=R2, in_=a1[:, :, 64]), ['PS%d' % accs[1]], [('SM', 1)])
            R2L = self.SM[:, 8:12]
            self.ts('dve', R2L, R2, self.LAMS[:, l:l + 1], ALU.mult, [('SM', 1), 'LAMS'], [('SM', 2)])
            r2b = R2L.unsqueeze(2).to_broadcast([128, 4, 64])
            self.tt('dve', O2, a1[:, :, 0:64], r2b, ALU.mult, ['PS%d' % accs[1], ('SM', 2)], [('OF', 1)])
            self.tt('dve', O1, O1, O2, ALU.add, [('OF', 0), ('OF', 1)], [('OF', 0)])
            self.tt('dve', O2, O1, O1, ALU.mult, [('OF', 0)], [('OF', 1)])
            SSQ = self.SM[:, 12:16]
            self.op('dve', lambda: nc.vector.tensor_reduce(out=SSQ, in_=O2, axis=AX.X, op=ALU.add),
                    [('OF', 1)], [('SM', 3)])
            LN_ = self.SM[:, 16:20]
            self.act(LN_, SSQ, AF.Ln, [('SM', 3), 'EPS'], [('SM', 4)], scale=1.0 / 64, bias=self.EPSV[:, 0:1])
            RS = self.SM[:, 20:24]
            self.act(RS, LN_, AF.Exp, [('SM', 4)], [('SM', 5)], scale=-0.5)
            rsb = RS.unsqueeze(2).to_broadcast([128, 4, 64])
            self.tt('dve', O1, O1, rsb, ALU.mult, [('OF', 0), ('SM', 5)], [('OF', 0)])
            sub = self.vec(l, 195, 64).unsqueeze(1).to_broadcast([128, 4, 64])
            self.stt(ob3, O1, float(1.0 - lam_init), sub, ALU.mult, ALU.mult, [('OF', 0), 'VEC'], [okey])

        def deferred():
            self.transposes_to_mix(ob, okey, mixtile, hd % 2, qt, accs[0])
        return deferred

    def mixer_c(self, l):
        nc = self.nc
        tr = self.tr
        CQN = [0, 1]
        CKVN = 2
        KROPE = 3
        MIX = [4, 5, 6, 7]
        ROPE = [9, 10]
        tr.retire(['G%d' % i for i in range(12)])
        self.set_pt([(8, 0), (8, 1024), (11, 0), (11, 1024)])
        wcq, wcqk = self.wget()
        wckv, wckvk = self.wget()
        for i in range(2):
            self.dma(self.g(ROPE[i]), self.c16[:, C_ROPE + i * 2048: C_ROPE + (i + 1) * 2048], [], ['G%d' % ROPE[i]])
        cosT = self.g(ROPE[0])
        sinT = self.g(ROPE[1])
        for tt in range(4):
            t0 = tt * 512
            bq = [self.nbank(), self.nbank()]
            bkv = self.nbank()
            bk1 = self.nbank()
            bk2 = self.nbank()
            for i in range(2):
                for c in range(8):
                    self.mm(self.ps(bq[i]), wcq[:, i * 1024 + c * 128: i * 1024 + c * 128 + 128], self.ht(c, t0, 512),
                            c == 0, c == 7, [wcqk, ('H%d' % c, tt)], ['PS%d' % bq[i]])
            for c in range(8):
                self.mm(self.ps(bkv), wckv[:, c * 128:(c + 1) * 128], self.ht(c, t0, 512), c == 0, c == 7,
                        [wckvk, ('H%d' % c, tt)], ['PS%d' % bkv])
            for c in range(8):
                self.mm(self.ps(bk1, 512, 64, 96), wckv[:, 1024 + c * 32: 1024 + (c + 1) * 32], self.ht(c, t0, 512),
                        c == 0, c == 7, [wckvk, ('H%d' % c, tt)], ['PS%d' % bk1])
            for c in range(8):
                self.mm(self.ps(bk2, 512, 64, 96), wckv[:, 1280 + c * 32: 1280 + (c + 1) * 32], self.ht(c, t0, 512),
                        c == 0, c == 7, [wckvk, ('H%d' % c, tt)], ['PS%d' % bk2])
            for (banks, nfeat, goff, dst) in ((bq, 256, 16, CQN), ([bkv], 128, 18, [CKVN])):
                sb = self.nbank()
                for i, bnk in enumerate(banks):
                    q = self.sqrot % 3
                    self.sqrot += 1
                    self.act(self.SQ[:, q * 512:(q + 1) * 512], self.ps(bnk), AF.Square, ['PS%d' % bnk], [('SQ', q)])
                    self.mm(self.ps(sb), self.ONES, self.SQ[:, q * 512:(q + 1) * 512], i == 0, i == len(banks) - 1,
                            [('SQ', q), 'C16'], ['PS%d' % sb])
                self.act(self.LNV[:, 0:512], self.ps(sb), AF.Ln, ['PS%d' % sb, 'EPS'], ['LNV'],
                         scale=1.0 / nfeat, bias=self.EPSV[:, 0:1])
                rq = 0
                self.rsrot += 1
                self.act(self.RSTD[:, rq * 512:(rq + 1) * 512], self.LNV[:, 0:512], AF.Exp, ['LNV'], [('RSTD', rq)],
                         scale=-0.5)
                for i, bnk in enumerate(banks):
                    self.stt(self.g(dst[i])[:, t0:t0 + 512], self.ps(bnk), self.vec(l, goff + i),
                             self.RSTD[:, rq * 512:(rq + 1) * 512], ALU.mult, ALU.mult,
                             ['PS%d' % bnk, ('RSTD', rq), 'VEC'], [('G%d' % dst[i], tt)])
            T1 = self.RT1[64:96, 0:512]
            T2 = self.RT1[64:96, 512:1024]
            self.tt('dve', T1, self.ps(bk1, 512, 64, 96), cosT[64:96, t0:t0 + 512], ALU.mult,
                    ['PS%d' % bk1, 'G%d' % ROPE[0]], [('RT1', 0)])
            self.tt('dve', T2, self.ps(bk2, 512, 64, 96), sinT[64:96, t0:t0 + 512], ALU.mult,
                    ['PS%d' % bk2, 'G%d' % ROPE[1]], [('RT1', 1)])
            self.tt('dve', self.g(KROPE)[64:96, t0:t0 + 512], T1, T2, ALU.add, [('RT1', 0), ('RT1', 1)],
                    [('G%d' % KROPE, tt)])
        tr.retire(['H%d' % i for i in range(8)])
        wuq, wuqk = self.wget()
        wukv, wukvk = self.wget()
        QC = [0, 1]
        KC = [2, 3]
        VC = 4
        nc_ = self.nc
        self.op('pool', lambda: nc_.gpsimd.memset(self.h(VC)[:, 0:1040], 1.0), [], ['H%d' % VC])
        self.op('pool', lambda: nc_.gpsimd.memset(self.h(VC + 1)[:, 0:1040], 1.0), [], ['H%d' % (VC + 1)])
        qscale = 96.0 ** -0.5
        self.acc_pool = [4, 5]
        self.bank_pool = [6, 7]

        def proj(hd):
            pb = hd % 2
            qc_t, kc_t = QC[pb], KC[pb]
            vct = VC + pb
            units = []

            def uq(tt):
                t0 = tt * 512
                ba = self.nbank()
                for c in range(2):
                    self.mm(self.ps(ba, 512, 0, 96), wuq[:, c * 768 + hd * 96: c * 768 + hd * 96 + 96],
                            self.g(CQN[c])[:, t0:t0 + 512], c == 0, c == 1, [wuqk, ('G%d' % CQN[c], tt)], ['PS%d' % ba])
                self.ts('dve', self.h(qc_t)[0:64, t0:t0 + 512], self.ps(ba, 512, 0, 64), qscale, ALU.mult,
                        ['PS%d' % ba], [('H%d' % qc_t, ('n', tt))])
                T1 = self.RT1[64:96, 0:512]
                T2 = self.RT1[64:96, 512:1024]
                self.stt(T1, self.ps(ba, 512, 64, 96), qscale, cosT[64:96, t0:t0 + 512], ALU.mult, ALU.mult,
                         ['PS%d' % ba, 'G%d' % ROPE[0]], [('RT1', 0)])
                bb = self.nbank()
                for c in range(2):
                    self.mm(self.ps(bb, 512, 64, 96), wuq[:, 1536 + c * 256 + hd * 32: 1536 + c * 256 + hd * 32 + 32],
                            self.g(CQN[c])[:, t0:t0 + 512], c == 0, c == 1, [wuqk, ('G%d' % CQN[c], tt)], ['PS%d' % bb])
                self.stt(T2, self.ps(bb, 512, 64, 96), qscale, sinT[64:96, t0:t0 + 512], ALU.mult, ALU.mult,
                         ['PS%d' % bb, 'G%d' % ROPE[1]], [('RT1', 1)])
                self.tt('pool', self.h(qc_t)[64:96, t0:t0 + 512], T1, T2, ALU.add, [('RT1', 0), ('RT1', 1)],
                        [('H%d' % qc_t, ('r', tt))])

            def uk(tt):
                t0 = tt * 512
                b = self.nbank()
                self.mm(self.ps(b, 512, 0, 64), wukv[:, hd * 128: hd * 128 + 64], self.g(CKVN)[:, t0:t0 + 512],
                        True, True, [wukvk, ('G%d' % CKVN, tt)], ['PS%d' % b])
                self.cp('dve', self.h(kc_t)[0:64, t0:t0 + 512], self.ps(b, 512, 0, 64), ['PS%d' % b],
                        [('H%d' % kc_t, ('n', tt))])
                self.cp('pool', self.h(kc_t)[64:96, t0:t0 + 512], self.g(KROPE)[64:96, t0:t0 + 512],
                        [('G%d' % KROPE, tt)], [('H%d' % kc_t, ('r', tt))])

            def uv(half):
                b = self.nbank()
                for t8 in range(8):
                    tc = half * 8 + t8
                    self.mm(self.ps(b, 64, 0, 128, t8 * 64), self.g(CKVN)[:, tc * 128:(tc + 1) * 128],
                            wukv[:, hd * 128 + 64: hd * 128 + 128], True, True,
                            [wukvk, ('G%d' % CKVN, tc // 4)], ['PS%d' % b], skip_group_check=True)
                ov = self.h(vct)[:, half * 520: half * 520 + 520].rearrange("p (t e) -> p t e", e=65)[:, :, 0:64]
                iv = self.ps(b).rearrange("p (t e) -> p t e", e=64)
                self.cp('dve', ov, iv, ['PS%d' % b, 'H%d' % vct], [('H%d' % vct, (pb, half))])

            for tt in range(4):
                units.append(lambda tt=tt: uq(tt))
            for tt in range(4):
                units.append(lambda tt=tt: uk(tt))
            for half in range(2):
                units.append(lambda half=half: uv(half))
            return units

        for u in proj(0):
            u()
        for hd in range(8):
            pb = hd % 2
            qc_t, kc_t = QC[pb], KC[pb]
            vct = VC + pb
            jobs = []
            for qt in range(4):
                acc = self.acc_bank()

                def kT(kc, kc_t=kc_t):
                    return (self.h(kc_t)[0:96, kc * 128:(kc + 1) * 128],
                            [('H%d' % kc_t, ('n', kc // 4)), ('H%d' % kc_t, ('r', kc // 4))])

                def qT(qt_, qc_t=qc_t):
                    return (self.h(qc_t)[0:96, qt_ * 512:(qt_ + 1) * 512],
                            [('H%d' % qc_t, ('n', qt_)), ('H%d' % qc_t, ('r', qt_))])

                def vT(kc, pb=pb, vct=vct):
                    return self.h(vct)[:, kc * 65: kc * 65 + 65], [('H%d' % vct, (pb, kc // 8)), 'H%d' % vct]

                def fin(job, hd=hd, qt=qt, acc=acc):
                    return [self.fin_ab(l, False, hd, qt, [acc], MIX[hd // 2], 0.0)]
                jobs.append(dict(lanes=[dict(K=96, base=0, kT=kT, qT=qT, vT=vT, acc=acc, scale=1.0, mult=None)],
                                 qt=qt, blocks=list(range(16)), ramp=None, fin=fin))
            hooks = {}
            if hd < 7:
                for ui, u in enumerate(proj(hd + 1)):
                    hooks[2 + 3 * ui] = u
            self.attn_run(jobs, hooks)
        self.acc_pool = [4, 5, 6, 7]
        self.bank_pool = list(range(8))
        wo1, wo1k = self.wget()
        wo2, wo2k = self.wget()
        self.out_proj([(wo1, wo1k), (wo2, wo2k)], MIX)

    def ffn(self, l):
        nc = self.nc
        tr = self.tr
        tr.retire(['G%d' % i for i in range(12)])
        self.cast_eng = 'act'
        ACT_T = 0
        CEN = [6, 7, 8, 9]
        SG = [10, 11]
        def part1(tt2, gi, jj):
            t0 = tt2 * 1024
            j = GROUPS[gi][jj]
            wu, wuk = self.wget()
            cbuf = jj % 2
            for gv in range(2):
                ch = j + 22 * gv
                bm = 2 * gv
                bh = 4 + gv
                for half in range(2):
                    for c in range(8):
                        self.mm(self.ps(bm + half), wu[:, gv * 1024 + c * 128: gv * 1024 + c * 128 + 128],
                                self.ht(c, t0 + half * 512, 512), c == 0, c == 7,
                                [wuk, ('H%d' % c, tt2 * 2 + half)], ['PS%d' % (bm + half)])
                tcol = t0 - 1 if tt2 == 1 else t0 + 1024
                for c in range(8):
                    self.mm(self.ps(bh, 2), wu[:, gv * 1024 + c * 128: gv * 1024 + c * 128 + 128],
                            self.ht(c, tcol - (1 if tt2 == 0 else 0), 2),
                            c == 0, c == 7, [wuk] + self.hk(c, tcol - (1 if tt2 == 0 else 0), 2), ['PS%d' % bh])
                hcol = 1 if tt2 == 0 else 0
                ct = CEN[cbuf * 2 + gv]
                cen = self.gf(ct)
                mainp = self.PS[:, bm * 512: bm * 512 + 1024]
                mkeys = ['PS%d' % bm, 'PS%d' % (bm + 1)]
                self.act(cen, mainp, AF.Identity, mkeys + ['VEC'], ['G%d' % ct],
                         scale=self.vec(l, 63 + ch), bias=self.vec(l, 151 + ch))
                self.stt(cen[:, 1:1024], mainp[:, 0:1023], self.vec(l, 19 + ch), cen[:, 1:1024], ALU.mult, ALU.add,
                         mkeys + ['VEC', 'G%d' % ct], ['G%d' % ct])
                self.stt(cen[:, 0:1023], mainp[:, 1:1024], self.vec(l, 107 + ch), cen[:, 0:1023], ALU.mult, ALU.add,
                         mkeys + ['VEC', 'G%d' % ct], ['G%d' % ct])
                if tt2 == 1:
                    self.stt(cen[:, 0:1], self.ps(bh, 1, 0, 128, hcol), self.vec(l, 19 + ch), cen[:, 0:1],
                             ALU.mult, ALU.add, ['PS%d' % bh, 'VEC', 'G%d' % ct], ['G%d' % ct])
                else:
                    self.stt(cen[:, 1023:1024], self.ps(bh, 1, 0, 128, hcol), self.vec(l, 107 + ch),
                             cen[:, 1023:1024], ALU.mult, ALU.add, ['PS%d' % bh, 'VEC', 'G%d' % ct], ['G%d' % ct])

        def part2(jj):
            cbuf = jj % 2
            cg = self.gf(CEN[cbuf * 2 + 0])
            cv = self.gf(CEN[cbuf * 2 + 1])
            sg = self.gf(SG[cbuf])
            self.act(sg, cg, AF.Silu, ['G%d' % CEN[cbuf * 2]], ['G%d' % SG[cbuf]])
            at = self.G[:, jj * 1024:(jj + 1) * 1024]
            self.tt('pool', at, cv, sg, ALU.mult, ['G%d' % CEN[cbuf * 2 + 1], 'G%d' % SG[cbuf]],
                    [('G%d' % (jj // 2), ('a', jj % 2))])

        def down(tt2, gi):
            n = len(GROUPS[gi])
            for oc in range(8):
                wd, wdk = self.wget()
                for half in range(2):
                    b = 6 + (oc * 2 + half) % 2
                    for jj in range(n):
                        self.mm(self.ps(b), wd[:, jj * 128:(jj + 1) * 128],
                                self.G[:, jj * 1024 + half * 512: jj * 1024 + half * 512 + 512],
                                jj == 0, jj == n - 1, [wdk, ('G%d' % (jj // 2), ('a', jj % 2))], ['PS%d' % b])
                    tt = tt2 * 2 + half
                    self.tt('dve', self.xt(oc, tt * 512, 512), self.ps(b), self.xt(oc, tt * 512, 512), ALU.add,
                            ['PS%d' % b, ('XT', (oc, tt))], [('XT', (oc, tt))])

        groups = [(tt2, gi) for tt2 in range(2) for gi in range(2)]
        pre = False
        for k, (tt2, gi) in enumerate(groups):
            for jj in range(len(GROUPS[gi])):
                if not (jj == 0 and pre):
                    part1(tt2, gi, jj)
                part2(jj)
            pre = False
            if k + 1 < len(groups):
                part1(groups[k + 1][0], groups[k + 1][1], 0)
                pre = True
            down(tt2, gi)

    def build(self):
        depth, nseq = self.depth, self.nseq
        nc = bass.Bass("TRN2", target_bir_lowering=False)
        self.nc = nc
        self.xTd = nc.dram_tensor("xT", [nseq, D, S], F32, kind="ExternalInput").ap()
        self.wst = nc.dram_tensor("wst", [depth * SLOTS_PER_LAYER, 128, 2048], F32, kind="ExternalInput").ap()
        self.vecd = nc.dram_tensor("vecs", [128, NV], F32, kind="ExternalInput").ap()
        self.c16 = nc.dram_tensor("c16", [128, NC16], BF16, kind="ExternalInput").ap()
        self.outT = nc.dram_tensor("outT", [nseq, D, S], F32, kind="ExternalOutput").ap()
        sched = []
        for b in range(nseq):
            for l in range(depth):
                base = l * SLOTS_PER_LAYER
                sizes = [2048, 2048, 2048, 2048, 2048, 2048, 2048, 2048, 2048, 2048, 2048, 1536, 2048, 1024, 2048, 2048]
                for i, n in enumerate(sizes):
                    sched.append((base + i, n))
                groups = [(tt2, gi) for tt2 in range(2) for gi in range(2)]
                for k, (tt2, gi) in enumerate(groups):
                    n = len(GROUPS[gi])
                    ubase = base + 16 + (20 if gi == 1 else 0)
                    for jj in range(n):
                        if not (jj == 0 and k > 0):
                            sched.append((ubase + jj, 2048))
                    if k + 1 < len(groups):
                        sched.append((base + 16 + (20 if groups[k + 1][1] == 1 else 0), 2048))
                    for oc in range(8):
                        sched.append((ubase + n + oc, n * 128))
        self.ws_init(sched)
        self.obrot = 0
        self.sqrot = 0
        self.rsrot = 0
        from contextlib import ExitStack
        with ExitStack() as es:
            def sb(name, shape, dt):
                return es.enter_context(nc.sbuf_tensor(name, shape, dt))
            XT = sb("XT", [128, 8 * S], F32)
            HT = sb("HT", [128, 8 * S], BF16)
            G = sb("G", [128, 12 * 2048], BF16)
            VT = sb("VT", [128, 16 * 4 * 65], BF16)
            WST = sb("WST", [128, 2 * 2048], F32)
            WR = sb("WR", [128, R_RING * 2048], BF16)
            VEC = sb("VEC", [128, NV], F32)
            C16 = sb("C16", [128, C_ROPE], BF16)
            EPSV = sb("EPSV", [128, 2], F32)
            LAMS = sb("LAMS", [128, 4], F32)
            LT = sb("LT", [128, 64], F32)
            SQ = sb("SQ", [128, 3 * 512], BF16)
            LNV = sb("LNV", [128, 512], F32)
            RSTD = sb("RSTD", [128, 512], F32)
            RT1 = sb("RT1", [128, 1024], F32)
            OF = sb("OF", [128, 512], F32)
            OB = sb("OB", [128, 512], BF16)
            SM = sb("SM", [128, 32], F32)
            PS = es.enter_context(nc.psum_tensor("PS", [128, 8 * 512], F32))
            s_pe = es.enter_context(nc.semaphore("s_pe"))
            s_act = es.enter_context(nc.semaphore("s_act"))
            s_dve = es.enter_context(nc.semaphore("s_dve"))
            s_pool = es.enter_context(nc.semaphore("s_pool"))
            dsl = [es.enter_context(nc.semaphore("d%d" % i)) for i in range(8)]
            block = es.enter_context(nc.Block())
            self.XT, self.HT, self.G, self.VT, self.WST, self.WR, self.VEC, self.C16 = XT, HT, G, VT, WST, WR, VEC, C16
            self.EPSV, self.LAMS, self.SQ, self.LNV, self.RSTD, self.RT1, self.OF, self.OB, self.SM, self.PS = \
                EPSV, LAMS, SQ, LNV, RSTD, RT1, OF, OB, SM, PS
            self.IDENT = C16[:, C_ID:C_ID + 128]
            self.ONES = C16[:, C_ONES:C_ONES + 128]

            self.dma(VEC[:, :], self.vecd[:, :], [], ['VEC'])
            self.dma(C16[:, :], self.c16[:, 0:C_ROPE], [], ['C16'])
            self.op('pool', lambda: nc.gpsimd.memset(EPSV[:, :], EPS), [], ['EPS'])
            self.op('pool', lambda: nc.gpsimd.memset(VT[:, :], 1.0), [], ['VT'])
            self.tr.retire(['VT'])
            for l in range(depth):
                lam_init = 0.8 - 0.6 * math.exp(-0.3 * l)
                for k in range(2):
                    a = self.vec(l, 259 + 64 * k, 32)
                    b_ = self.vec(l, 291 + 64 * k, 32)
                    self.tt('dve', LT[:, 0:32], a, b_, ALU.mult, ['VEC'], [('LT', 0)])
                    self.op('dve', lambda k=k: nc.vector.tensor_reduce(out=LT[:, 32 + k:33 + k], in_=LT[:, 0:32],
                                                                       axis=AX.X, op=ALU.add), [('LT', 0)], [('LT', 1 + k)])
                    self.act(LT[:, 34 + k:35 + k], LT[:, 32 + k:33 + k], AF.Exp, [('LT', 1 + k)], [('LT', 3 + k)])
                self.stt(LAMS[:, l:l + 1], LT[:, 35:36], float(-lam_init), LT[:, 34:35], ALU.add, ALU.subtract,
                         [('LT', 3), ('LT', 4)], ['LAMS'])

            for b in range(nseq):
                for c in range(8):
                    self.dma(XT[:, c * S:(c + 1) * S], self.xTd[b, c * 128:(c + 1) * 128, :],
                             [], [('XT', (c, tt)) for tt in range(4)])
                for l in range(depth):
                    self.tr.retire(['H%d' % i for i in range(8)])
                    self.cast_eng = 'dve'
                    self.rmsnorm_x(l, 0)
                    self.mixer_a2(l)
                    self.mixer_ab(l, 'B')
                    self.mixer_c(l)
                    self.tr.retire(['H%d' % i for i in range(8)])
                    self.rmsnorm_x(l, 8)
                    self.ffn(l)
                self.tr.retire(['G%d' % i for i in range(12)])
                self.rmsnorm_x(0, 0, final=True, seq=b)

            @block.sync
            def _(sync):
                self.tr.emit(nc, {'pe': s_pe, 'act': s_act, 'dve': s_dve, 'pool': s_pool},
                             dsl)
        return nc


def _bf16(a):
    return np.asarray(a, dtype=np.float32).astype(ml_dtypes.bfloat16)


def make_consts():
    c = np.zeros((128, NC16), dtype=np.float32)
    c[:, C_ID:C_ID + 128] = np.eye(128, dtype=np.float32)
    c[:, C_ONES:C_ONES + 128] = 1.0
    ki = np.arange(128)[:, None]
    qi = np.arange(128)[None, :]
    for s in range(8):
        slope = 2.0 ** -(s + 1)
        c[:, C_ED + s * 128: C_ED + (s + 1) * 128] = np.exp(-slope * np.abs(qi - ki))
    pos = np.arange(S, dtype=np.float32)
    inv_freq = (10000.0 ** (-np.arange(0, 32, 2, dtype=np.float32) / 32)).astype(np.float32)
    ang = (pos[:, None] * inv_freq[None, :]).astype(np.float32)
    cos = np.cos(ang).astype(np.float32).T
    sin = np.sin(ang).astype(np.float32).T
    c[64:80, C_ROPE:C_ROPE + S] = cos
    c[80:96, C_ROPE:C_ROPE + S] = cos
    c[64:80, C_ROPE + S:C_ROPE + 2 * S] = -sin
    c[80:96, C_ROPE + S:C_ROPE + 2 * S] = sin
    p = np.arange(S)
    hi = (64 * (p // 64)).astype(np.float32)
    lo = (p % 64).astype(np.float32)
    for (off, bases) in ((C_RA, (0, 32, 64, 96)), (C_RB, (0, 64))):
        for b in bases:
            c[b + 0, off:off + S] = hi
            c[b + 1, off:off + S] = lo
            c[b + 2, off:off + S] = -1.0
            c[b + 3, off:off + S] = -1.0
            if off == C_RB and b == 64:
                c[b:b + 4, off:off + S] *= 0.25
            c[b:b + 4, off + S:off + 2 * S] = -c[b:b + 4, off:off + S]
            c[b + 0, off + 2 * S:off + 3 * S] = 1.0
            c[b + 1, off + 2 * S:off + 3 * S] = 1.0
            c[b + 2, off + 2 * S:off + 3 * S] = hi
            c[b + 3, off + 2 * S:off + 3 * S] = lo
    m = np.arange(CBW)[None, :]
    d = m - np.arange(128)[:, None] - OFFB
    ad = np.abs(d)
    cb = (ad <= 64).astype(np.float32) + ((d % 4 == 0) & (ad <= 256)) + ((d % 16 == 0) & (ad <= 1024))
    c[:, C_CB:C_CB + CBW] = cb
    return _bf16(c)


def fm(w, M=None):
    k = w.shape[0] // 128
    return np.ascontiguousarray(w.reshape(k, 128, w.shape[1]).transpose(1, 0, 2))


def make_wstream(inp, depth):
    out = np.zeros((depth * SLOTS_PER_LAYER, 128, 2048), dtype=np.float32)

    def put(idx, arr):
        a = np.ascontiguousarray(arr, dtype=np.float32).reshape(128, -1)
        out[idx, :, :a.shape[1]] = a

    for l in range(depth):
        b = l * SLOTS_PER_LAYER
        w_in = np.asarray(inp['w_in'][l])
        w_out = np.asarray(inp['w_out'][l])

        def fm2(cols0):
            return np.stack([fm(w_in[:, cols0 + i * 128: cols0 + (i + 1) * 128]) for i in range(2)], axis=1)
        def fmA(cols0, hp):
            outs = []
            for hh in range(2):
                c0 = cols0 + (2 * hp + hh) * 64
                m0 = w_in[:, c0:c0 + 32]
                m1 = w_in[:, c0 + 32:c0 + 64]
                outs.append(fm(np.concatenate([m0, m1, m1, m0], axis=1)))
            return np.stack(outs, axis=1)
        put(b + 0, fm(w_in[:, 512:768]))
        put(b + 1, fmA(0, 0))
        put(b + 2, fmA(256, 0))
        put(b + 3, fmA(0, 1))
        put(b + 4, fmA(256, 1))
        put(b + 5, fm(w_out[0:256, :]))
        put(b + 6, fm2(768))
        put(b + 7, fm2(1024))
        put(b + 8, fm(w_in[:, 1280:1536]))
        put(b + 9, fm(w_out[256:512, :]))
        put(b + 10, fm2(1536))
        kr = w_in[:, 1920:1952]
        kr_sw = np.concatenate([kr[:, 16:32], kr[:, 0:16]], axis=1)
        put(b + 11, np.concatenate([fm(w_in[:, 1792:1920]).reshape(128, -1), fm(kr).reshape(128, -1),
                                    fm(kr_sw).reshape(128, -1)], axis=1))
        uq = np.asarray(inp['c_w_uq'][l])
        uq_h = uq.reshape(256, 8, 96)
        uqs = np.concatenate([uq_h[:, :, 80:96], uq_h[:, :, 64:80]], axis=2).reshape(256, 256)
        put(b + 12, np.concatenate([fm(uq).reshape(128, -1), fm(uqs).reshape(128, -1)], axis=1))
        put(b + 13, np.asarray(inp['c_w_ukv'][l]))
        put(b + 14, fm(w_out[512:768, :]))
        put(b + 15, fm(w_out[768:1024, :]))
        w_up = np.asarray(inp['w_up'][l])
        w_down = np.asarray(inp['w_down'][l])
        o = b + 16
        for grp in GROUPS:
            for j in grp:
                put(o, np.concatenate([fm(w_up[:, j * 128:(j + 1) * 128]).reshape(128, -1),
                                       fm(w_up[:, DFF + j * 128: DFF + (j + 1) * 128]).reshape(128, -1)], axis=1))
                o += 1
            for oc in range(8):
                rows = np.stack([w_down[j * 128:(j + 1) * 128, oc * 128:(oc + 1) * 128] for j in grp], axis=1)
                put(o, rows)
                o += 1
    return out


def make_vecs(inp, depth):
    v = np.zeros((128, NV), dtype=np.float32)

    def colz(a):
        a = np.asarray(a, dtype=np.float32)
        return a.reshape(-1, 128).T

    for l in range(depth):
        o = l * VL
        v[:, o + 0:o + 8] = colz(inp['g_attn'][l])
        v[:, o + 8:o + 16] = colz(inp['g_ffn'][l])
        v[:, o + 16:o + 18] = colz(inp['c_g_q'][l])
        v[:, o + 18:o + 19] = colz(inp['c_g_kv'][l])
        for k in range(3):
            v[:, o + 19 + 44 * k: o + 19 + 44 * (k + 1)] = colz(inp['conv_w'][l][k])
        v[:, o + 151:o + 195] = colz(inp['conv_b'][l])
        v[:, o + 195:o + 259] = np.broadcast_to(np.asarray(inp['a_subln'][l], dtype=np.float32)[None, :], (128, 64))
        for i, nm in enumerate(('a_lq1', 'a_lk1', 'a_lq2', 'a_lk2')):
            v[:, o + 259 + 32 * i: o + 259 + 32 * (i + 1)] = np.broadcast_to(
                np.asarray(inp[nm][l], dtype=np.float32)[None, :], (128, 32))
    v[:, DEPTH * VL: DEPTH * VL + 8] = colz(inp['g_final'])
    return v


_CACHE = {}


def kernel(**inputs):
    x = np.asarray(inputs['x'], dtype=np.float32)
    nb = x.shape[0]
    assert nb == NCORES * NSEQ
    wst = make_wstream(inputs, DEPTH)
    vecs = make_vecs(inputs, DEPTH)
    c16 = make_consts()
    if 'nc' not in _CACHE:
        _CACHE['nc'] = Builder().build()
    nc = _CACHE['nc']
    in_maps = []
    for i in range(NCORES):
        xT = np.ascontiguousarray(x[i * NSEQ:(i + 1) * NSEQ].transpose(0, 2, 1))
        in_maps.append({"xT": xT, "wst": wst, "vecs": vecs, "c16": c16})
    res = run_bass_kernel_spmd(nc, in_maps, core_ids=list(range(NCORES)))
    outs = [np.asarray(r["outT"]).transpose(0, 2, 1) for r in res.results]
    return np.ascontiguousarray(np.concatenate(outs, axis=0)).astype(np.float32)
```

```python
import math
import numpy as np
import ml_dtypes
import concourse.bass as bass
import concourse.mybir as mybir
from concourse.bass_utils import run_bass_kernel_spmd

F32 = mybir.dt.float32
BF16 = mybir.dt.bfloat16
AF = mybir.ActivationFunctionType
ALU = mybir.AluOpType
AX = mybir.AxisListType

S = 2048
D = 1024
DEPTH = 4
NCORES = 8
NSEQ = 2
DFF = 2816
NJ = 22
EPS = 1e-6
SL_A = [2.0 ** -1, 2.0 ** -3, 2.0 ** -5, 2.0 ** -7]
SL_B = [2.0 ** -2, 2.0 ** -4, 2.0 ** -6, 2.0 ** -8]
SL_IDX_A = [0, 2, 4, 6]
SL_IDX_B = [1, 3, 5, 7]
OFFB = 1408
CBW = 2944
VL = 387
NV = DEPTH * VL + 8
GROUPS = [list(range(0, 12)), list(range(12, 22))]
SLOTS_PER_LAYER = 17 + (12 + 8) + (10 + 8)
C_ID, C_ONES, C_ED, C_ROPE, C_RA, C_RB, C_CB = 0, 128, 256, 1280, 5376, 11520, 17664
NC16 = C_CB + CBW
R_RING = 4
LOOK = 2


class Tr:
    def __init__(self):
        self.ops = []
        self.st = {}
        self.carry = {}

    def _m(self, d, idx):
        e = self.ops[idx][0]
        key = e if e != 'sp' else ('sp', idx)
        if d.get(key, -1) < idx:
            d[key] = idx

    @staticmethod
    def _k(k):
        return (k, None) if isinstance(k, str) else k

    def op(self, eng, fn, r=(), w=()):
        deps = {}
        for k in r:
            t, sub = self._k(k)
            for i in self.carry.get(t, {}).values():
                self._m(deps, i)
            s = self.st.get(t, {}).get(sub)
            if s is not None and s[0] is not None:
                self._m(deps, s[0])
        for k in w:
            t, sub = self._k(k)
            for i in self.carry.get(t, {}).values():
                self._m(deps, i)
            s = self.st.get(t, {}).get(sub)
            if s is not None:
                if s[0] is not None:
                    self._m(deps, s[0])
                for i in s[1].values():
                    self._m(deps, i)
        idx = len(self.ops)
        self.ops.append([eng, fn, sorted(deps.values()), False])
        for k in r:
            t, sub = self._k(k)
            s = self.st.setdefault(t, {}).setdefault(sub, [None, {}])
            key = eng if eng != 'sp' else ('sp', idx)
            s[1][key] = idx
        for k in w:
            t, sub = self._k(k)
            self.st.setdefault(t, {})[sub] = [idx, {}]
        return idx

    def retire(self, tiles):
        for t in tiles:
            c = self.carry.setdefault(t, {})
            for sub, s in self.st.get(t, {}).items():
                if s[0] is not None:
                    self._m(c, s[0])
                for i in s[1].values():
                    self._m(c, i)
            self.st[t] = {}

    def emit(self, nc, sems, dsems):
        ops = self.ops
        for o in ops:
            for d in o[2]:
                if not (ops[d][0] == 'pe' and o[0] == 'pe'):
                    ops[d][3] = True
        engobj = {'pe': nc.tensor, 'act': nc.scalar, 'dve': nc.vector, 'pool': nc.gpsimd, 'sp': nc.sync}
        cnt = {e: 0 for e in sems}
        seen = {e: {} for e in engobj}
        dcnt = [0] * len(dsems)
        drot = 0
        ev = [None] * len(ops)
        for i, o in enumerate(ops):
            eng = o[0]
            eo = engobj[eng]
            for d in o[2]:
                if ops[d][0] == 'pe' and eng == 'pe':
                    continue
                sid, sem, val = ev[d]
                if seen[eng].get(sid, 0) < val:
                    eo.wait_ge(sem, val)
                    seen[eng][sid] = val
            if eng == 'sp':
                j = drot
                drot = (drot + 1) % len(dsems)
                sid = ('d', j)
                if seen[eng].get(sid, 0) < dcnt[j]:
                    eo.wait_ge(dsems[j], dcnt[j])
                    seen[eng][sid] = dcnt[j]
                inst = o[1]()
                dcnt[j] += 16
                inst.then_inc(dsems[j], 16)
                ev[i] = (sid, dsems[j], dcnt[j])
            else:
                inst = o[1]()
                if o[3]:
                    cnt[eng] += 1
                    inst.then_inc(sems[eng], 1)
                    ev[i] = (eng, sems[eng], cnt[eng])
        for j in range(len(dsems)):
            if dcnt[j] > 0:
                nc.sync.wait_ge(dsems[j], dcnt[j])


class Builder:
    def __init__(self, depth=DEPTH, nseq=NSEQ, dbg=None):
        self.depth = depth
        self.nseq = nseq
        self.dbg = dbg
        self.tr = Tr()
        self.srot = 0
        self.arot = 0
        self.brot = 0
        self.prot = 0
        self.erot = 0
        self.cast_eng = 'dve'
        self.pv_lag = 2
        self.bank_pool = list(range(8))
        self.acc_pool = [4, 5, 6, 7]

    def xt(self, c, t0, n):
        return self.XT[:, c * S + t0: c * S + t0 + n]

    def ht(self, c, t0, n):
        return self.HT[:, c * S + t0: c * S + t0 + n]

    def hk(self, c, t0, n=1):
        return [('H%d' % c, tt) for tt in range(t0 // 512, (t0 + n - 1) // 512 + 1)]

    def g(self, i):
        return self.G[:, i * 2048:(i + 1) * 2048]

    def gf(self, i):
        return self.G[:, i * 2048:(i + 1) * 2048].bitcast(F32)

    def h(self, i):
        return self.HT[:, i * 2048:(i + 1) * 2048]

    def ps(self, b, n=512, p0=0, p1=128, c0=0):
        return self.PS[p0:p1, b * 512 + c0: b * 512 + c0 + n]

    def psb(self, b):
        return self.PS[:, b * 512:(b + 1) * 512].bitcast(BF16)

    def vec(self, l, off, n=1):
        return self.VEC[:, l * VL + off: l * VL + off + n]

    def ptk(self, pp):
        return self.PT_PAIRS[pp][1]

    def ptv(self, pp):
        return self.PT_PAIRS[pp][0]

    def set_pt(self, specs):
        self.PT_PAIRS = [(self.G[:, t * 2048 + c0: t * 2048 + c0 + 1024], ('G%d' % t, ('pt', c0))) for (t, c0) in specs]
        self.prot = 0

    def nbank(self):
        pool = self.bank_pool
        b = pool[self.brot % len(pool)]
        self.brot += 1
        return b

    def op(self, eng, fn, r=(), w=()):
        return self.tr.op(eng, fn, r, w)

    def mm(self, out, lhsT, rhs, start, stop, r, w, **kw):
        nc = self.nc
        return self.op('pe', lambda: nc.tensor.matmul(out, lhsT=lhsT, rhs=rhs, start=start, stop=stop, **kw), r, w)

    def act(self, out, in_, func, r, w, scale=None, bias=None):
        nc = self.nc
        kw = {}
        if scale is not None:
            kw['scale'] = scale
        if bias is not None:
            kw['bias'] = bias
        return self.op('act', lambda: nc.scalar.activation(out=out, in_=in_, func=func, **kw), r, w)

    def tt(self, eng, out, in0, in1, op, r, w):
        nc = self.nc
        e = nc.vector if eng == 'dve' else nc.gpsimd
        return self.op(eng, lambda: e.tensor_tensor(out=out, in0=in0, in1=in1, op=op), r, w)

    def ts(self, eng, out, in0, s1, op0, r, w, s2=None, op1=None):
        nc = self.nc
        e = nc.vector if eng == 'dve' else nc.gpsimd
        if op1 is None:
            return self.op(eng, lambda: e.tensor_scalar(out=out, in0=in0, scalar1=s1, scalar2=None, op0=op0), r, w)
        return self.op(eng, lambda: e.tensor_scalar(out=out, in0=in0, scalar1=s1, scalar2=s2, op0=op0, op1=op1), r, w)

    def stt(self, out, in0, scalar, in1, op0, op1, r, w):
        nc = self.nc
        return self.op('dve', lambda: nc.vector.scalar_tensor_tensor(out=out, in0=in0, scalar=scalar, in1=in1,
                                                                     op0=op0, op1=op1), r, w)

    def cp(self, eng, out, in_, r, w):
        nc = self.nc
        e = {'dve': nc.vector, 'pool': nc.gpsimd}[eng]
        return self.op(eng, lambda: e.tensor_copy(out=out, in_=in_), r, w)

    def dma(self, out, in_, r, w):
        nc = self.nc
        return self.op('sp', lambda: nc.sync.dma_start(out=out, in_=in_), r, w)

    def ws_init(self, sched):
        self.wsched = sched
        self.wnext = 0
        self.wcons = 0

    def wget(self):
        i = self.wcons
        self.wcons += 1
        lim = min(i + LOOK, len(self.wsched) - 1)
        nc = self.nc
        while self.wnext <= lim:
            j = self.wnext
            slot, n = self.wsched[j]
            st = j % 2
            self.dma(self.WST[:, st * 2048: st * 2048 + n], self.wst[slot, :, 0:n], [], [('WST', st)])
            ce = self.cast_eng
            if ce == 'act':
                self.act(self.WR[:, (j % R_RING) * 2048:(j % R_RING) * 2048 + n],
                         self.WST[:, st * 2048: st * 2048 + n], AF.Copy, [('WST', st)], [('WR', j % R_RING)])
            else:
                self.cp(ce, self.WR[:, (j % R_RING) * 2048:(j % R_RING) * 2048 + n],
                        self.WST[:, st * 2048: st * 2048 + n], [('WST', st)], [('WR', j % R_RING)])
            self.wnext += 1
        r = i % R_RING
        return self.WR[:, r * 2048:(r + 1) * 2048], ('WR', r)

    def rmsnorm_x(self, l, goff, final=False, seq=0):
        for tt in range(4):
            t0 = tt * 512
            sb = self.nbank()
            for c in range(8):
                q = (tt * 8 + c) % 3
                self.act(self.SQ[:, q * 512:(q + 1) * 512], self.xt(c, t0, 512), AF.Square,
                         [('XT', (c, tt))], [('SQ', q)])
                self.mm(self.ps(sb), self.ONES, self.SQ[:, q * 512:(q + 1) * 512], c == 0, c == 7,
                        [('SQ', q), 'C16'], ['PS%d' % sb])
            self.act(self.LNV[:, 0:512], self.ps(sb), AF.Ln, ['PS%d' % sb, 'EPS'], ['LNV'],
                     scale=1.0 / D, bias=self.EPSV[:, 0:1])
            rq = 0
            self.act(self.RSTD[:, rq * 512:(rq + 1) * 512], self.LNV[:, 0:512], AF.Exp, ['LNV'], [('RSTD', rq)],
                     scale=-0.5)
            for c in range(8):
                gcol = self.VEC[:, DEPTH * VL + c: DEPTH * VL + c + 1] if final else self.vec(l, goff + c)
                if not final:
                    self.stt(self.ht(c, t0, 512), self.xt(c, t0, 512), gcol, self.RSTD[:, rq * 512:(rq + 1) * 512],
                             ALU.mult, ALU.mult, [('XT', (c, tt)), ('RSTD', rq), 'VEC'], [('H%d' % c, tt)])
                else:
                    gi = (tt % 2) * 4 + c // 2
                    o = self.gf(gi)[:, (c % 2) * 512:(c % 2) * 512 + 512]
                    self.stt(o, self.xt(c, t0, 512), gcol, self.RSTD[:, rq * 512:(rq + 1) * 512],
                             ALU.mult, ALU.mult, [('XT', (c, tt)), ('RSTD', rq), 'VEC'], [('G%d' % gi, c % 2)])
                    self.dma(self.outT[seq, c * 128:(c + 1) * 128, t0:t0 + 512], o, [('G%d' % gi, c % 2)], [])

    def proj_fm(self, wap, wkey, wcol0, M, src, nk, evac):
        for tt in range(4):
            b = self.nbank()
            for c in range(nk):
                rhs, rk = src(c, tt)
                self.mm(self.ps(b, 512, 0, M), wap(c, wcol0, M), rhs, c == 0, c == nk - 1, [wkey] + rk, ['PS%d' % b])
            evac(tt, b)

    def evac_eng(self):
        self.erot += 1
        return 'act' if self.erot % 2 else 'dve'

    def copy_scaled(self, out, in_, scale, r, w, eng=None):
        eng = eng or self.evac_eng()
        if eng == 'act':
            self.act(out, in_, AF.Copy, r, w, scale=float(scale))
        else:
            self.ts('dve', out, in_, float(scale), ALU.mult, r, w)

    def attn_run(self, jobs, hooks=None):
        pend = []
        deferred = []
        flat = []
        for job in jobs:
            blocks = job['blocks']
            if len(job['lanes']) == 2:
                steps = [[(0, kc), (1, kc)] for kc in blocks]
            else:
                steps = [[(0, kc) for kc in blocks[i:i + 2]] for i in range(0, len(blocks), 2)]
            job['started'] = set()
            for si, st in enumerate(steps):
                flat.append((job, st, si == len(steps) - 1))
        stepno = 0
        for (job, st, last) in flat:
            sp = self.srot
            self.srot = (self.srot + 1) % 2
            pp = self.prot
            self.prot = (self.prot + 1) % len(self.PT_PAIRS)
            qt = job['qt']
            q0 = qt * 512
            ramp = job['ramp']
            skeys = ['PS%d' % (2 * sp), 'PS%d' % (2 * sp + 1)]
            info = []
            fold_info = []
            for bi, (ln, kc) in enumerate(st):
                L = job['lanes'][ln]
                bk = 2 * sp + bi
                k0 = kc * 128
                if not job.get('fold'):
                    kap, kk = L['kT'](kc)
                    qap, qk = L['qT'](qt)
                a = k0 + 128 - q0
                b = k0 - q0
                segs = []
                if ramp is not None:
                    if a < 512:
                        segs.append((max(0, a), 512, 0))
                    if b > 0:
                        segs.append((0, min(512, b), 1))
                kw = {}
                if L['base'] == 96:
                    kw['tile_position'] = (96, 0)
                if job.get('fold'):
                    if b >= 512:
                        fs = [(0, 512, 'L')]
                    elif b <= -128:
                        fs = [(0, 512, 'R')]
                    else:
                        fs = []
                        if b > 0:
                            fs.append((0, b, 'L'))
                        fs.append((b, b + 128, 'D'))
                        if b + 128 < 512:
                            fs.append((b + 128, 512, 'R'))
                    fold_info.append((L, bk, kc, fs, bi))
                    continue
                self.mm(self.ps(bk), kap, qap, True, len(segs) == 0, kk + qk, [skeys[bi]], **kw)
                info.append((L, bk, k0, segs, kw, bi))
            if fold_info:
                for si in range(3):
                    for (L, bk, kc, fs, bi) in fold_info:
                        if si < len(fs):
                            lo, hi, var = fs[si]
                            kap, kk = L['kTv'](kc, var)
                            qap, qk = L['qTv'](q0 + lo, hi - lo, var)
                            self.mm(self.ps(bk, hi - lo, 0, 128, lo), kap, qap, True, True, kk + qk, [skeys[bi]],
                                    skip_group_check=True)
            for (L, bk, k0, segs, kw, bi) in info:
                base, K = L['base'], L['K']
                for si, (lo, hi, sgn) in enumerate(segs):
                    rk_tab = ramp[1] if sgn else ramp[0]
                    self.mm(self.ps(bk, hi - lo, 0, 128, lo), rk_tab[base:base + K, k0:k0 + 128],
                            ramp[2][base:base + K, q0 + lo:q0 + hi], False, si == len(segs) - 1,
                            list(ramp[3]), [skeys[bi]], **kw)
            nb = len(st)
            scales = [job['lanes'][ln]['scale'] for (ln, kc) in st]
            ptt = self.ptv(pp)
            if all(sc == scales[0] for sc in scales):
                self.act(ptt[:, 0: nb * 512], self.PS[:, 2 * sp * 512: 2 * sp * 512 + nb * 512],
                         AF.Exp, skeys[:nb], [self.ptk(pp)], scale=float(scales[0]))
            else:
                for bi in range(nb):
                    self.act(ptt[:, bi * 512: (bi + 1) * 512], self.ps(2 * sp + bi),
                             AF.Exp, [skeys[bi]], [self.ptk(pp)], scale=float(scales[bi]))
            if job.get('mult_both') is not None and nb == 2:
                for (tab, tk, eng) in job['mult_both'](st[0][1], qt):
                    v = ptt[:, 0:1024].rearrange("p (b c) -> p b c", b=2)
                    self.tt(eng, v, v, tab.unsqueeze(1).to_broadcast([128, 2, 512]), ALU.mult,
                            [self.ptk(pp)] + tk, [self.ptk(pp)])
            for bi, (ln, kc) in enumerate(st):
                L = job['lanes'][ln]
                if L['mult'] is not None:
                    for (c0, n, tab, tk, eng) in L['mult'](kc, qt):
                        v = ptt[:, bi * 512 + c0: bi * 512 + c0 + n]
                        self.tt(eng, v, v, tab, ALU.mult, [self.ptk(pp)] + tk, [self.ptk(pp)])
            def pv(job=job, st=st, pp=pp, last=last):
                ptt_ = self.ptv(pp)
                for bi, (ln, kc) in enumerate(st):
                    L = job['lanes'][ln]
                    acc = L['acc']
                    vap, vk = L['vT'](kc)
                    for qc in range(4):
                        stf = acc not in job['started']
                        job['started'].add(acc)
                        lhs = ptt_[:, bi * 512 + qc * 128: bi * 512 + qc * 128 + 128]
                        self.mm(self.ps(acc, 65, 0, 128, qc * 65), lhs, vap, stf, last,
                                [self.ptk(pp)] + vk, ['PS%d' % acc], skip_group_check=True)
                if last:
                    for d in (job['fin'](job) or []):
                        deferred.append([3, d])
            pend.append(pv)
            if len(pend) > self.pv_lag:
                pend.pop(0)()
            if hooks and stepno in hooks:
                hooks[stepno]()
            stepno += 1
            for dd in deferred:
                dd[0] -= 1
            while deferred and deferred[0][0] <= 0:
                deferred.pop(0)[1]()
        while pend:
            pend.pop(0)()
        for dd in deferred:
            dd[1]()

    def acc_bank(self):
        b = self.acc_pool[self.arot % len(self.acc_pool)]
        self.arot += 1
        return b

    def transposes_to_mix(self, obuf, okey, mixtile, h2, qt, bank):
        nc = self.nc
        bk = bank
        r0 = 64 * h2
        pv = self.psb(bk)
        for qc in range(4):
            o = pv[r0:r0 + 64, qc * 128:(qc + 1) * 128]
            i_ = obuf[:, qc * 64:(qc + 1) * 64]
            self.op('pe', lambda o=o, i_=i_: nc.tensor.transpose(o, i_, self.IDENT), [okey, 'C16'], ['PS%d' % bk])
        self.cp('dve', self.g(mixtile)[r0:r0 + 64, qt * 512:(qt + 1) * 512], pv[r0:r0 + 64, 0:512],
                ['PS%d' % bk], [('G%d' % mixtile, (h2, qt))])

    def out_proj(self, wlist, mixtiles):
        nfc = len(mixtiles)
        for tt in range(4):
            for oc in range(8):
                b = self.nbank()
                for fc in range(nfc):
                    wap, wk = wlist[fc // 2]
                    lhs = wap[:, (fc % 2) * 1024 + oc * 128:(fc % 2) * 1024 + oc * 128 + 128]
                    mt = mixtiles[fc]
                    self.mm(self.ps(b), lhs, self.g(mt)[:, tt * 512:(tt + 1) * 512], fc == 0, fc == nfc - 1,
                            [wk] + [('G%d' % mt, (h2, tt)) for h2 in range(2)], ['PS%d' % b])
                self.tt('dve', self.xt(oc, tt * 512, 512), self.ps(b), self.xt(oc, tt * 512, 512), ALU.add,
                        ['PS%d' % b, ('XT', (oc, tt))], [('XT', (oc, tt))])

    def mixer_ab(self, l, which):
        nc = self.nc
        tr = self.tr
        isA = which == 'A'
        QT = [0, 1]
        KT = [2, 3]
        RT = [4, 5, 6]
        MIX = [8, 9]
        CBT = [10, 11]
        tr.retire(['G%d' % i for i in range(12)])
        if isA:
            self.set_pt([(7, 0), (7, 1024), (10, 0), (10, 1024)])
        else:
            self.set_pt([(7, 0), (7, 1024), (11, 1024)])
        self.pv_lag = 2
        rbase = C_RA if isA else C_RB
        for i in range(3):
            self.dma(self.g(RT[i]), self.c16[:, rbase + i * 2048: rbase + (i + 1) * 2048], [], ['G%d' % RT[i]])
        if not isA:
            self.dma(self.G[:, CBT[0] * 2048: CBT[0] * 2048 + CBW], self.c16[:, C_CB:C_CB + CBW], [], ['G%d' % CBT[0], 'G%d' % CBT[1]])
        slopes = SL_A if isA else SL_B
        qk_scale = (32.0 ** -0.5) if isA else (64.0 ** -0.5)

        def wfm(w):
            return lambda c, col0, M: w[:, (col0 // 128) * 1024 + c * 128: (col0 // 128) * 1024 + c * 128 + M]

        def src(c, tt):
            return self.ht(c, tt * 512, 512), [('H%d' % c, tt)]

        wq, wqk = self.wget()
        for i in range(2):
            def evq(tt, b, i=i):
                for hh in range(2):
                    hd = 2 * i + hh
                    self.copy_scaled(self.g(QT[i])[64 * hh:64 * hh + 64, tt * 512:(tt + 1) * 512],
                                     self.ps(b, 512, 64 * hh, 64 * hh + 64),
                                     qk_scale / (slopes[hd] if isA else slopes[2 * i]),
                                     ['PS%d' % b], [('G%d' % QT[i], (hh, tt))])
            self.proj_fm(wfm(wq), wqk, i * 128, 128, src, 8, evq)
        wk_, wkk = self.wget()
        for i in range(2):
            def evk(tt, b, i=i):
                self.copy_scaled(self.g(KT[i])[:, tt * 512:(tt + 1) * 512], self.ps(b), 1.0,
                                 ['PS%d' % b], [('G%d' % KT[i], tt)])
            self.proj_fm(wfm(wk_), wkk, i * 128, 128, src, 8, evk)
        wv, wvk = self.wget()
        for tc in range(16):
            b = self.nbank()
            for c in range(8):
                self.mm(self.ps(b, 256), self.ht(c, tc * 128, 128), wv[:, c * 256:(c + 1) * 256], c == 0, c == 7,
                        [wvk, ('H%d' % c, tc // 4)], ['PS%d' % b])
            ov = self.VT[:, tc * 260:(tc + 1) * 260].rearrange("p (h e) -> p h e", e=65)[:, :, 0:64]
            iv = self.ps(b, 256).rearrange("p (h e) -> p h e", e=64)
            eng = self.evac_eng()
            if eng == 'act':
                self.act(ov, iv, AF.Copy, ['PS%d' % b], [('VT', tc)])
            else:
                self.cp('dve', ov, iv, ['PS%d' % b], [('VT', tc)])

        ramp = (self.g(RT[0]), self.g(RT[1]), self.g(RT[2]), 'G%d' % RT[0])
        rkeys = ['G%d' % t for t in RT]
        jobs = []
        lam_init = 0.8 - 0.6 * math.exp(-0.3 * l)

        def mk_lane(hd, base, K, acc):
            ti = hd // 2
            sidx = (SL_IDX_A if isA else SL_IDX_B)[hd]

            def kT(kc):
                return self.g(KT[ti])[base:base + K, kc * 128:(kc + 1) * 128], [('G%d' % KT[ti], kc // 4)]

            def qT(qt_):
                return (self.g(QT[ti])[base:base + K, qt_ * 512:(qt_ + 1) * 512],
                        [('G%d' % QT[ti], ((base // 64), qt_))])

            def vT(kc):
                return self.VT[:, (kc * 4 + hd) * 65:(kc * 4 + hd) * 65 + 65], [('VT', kc)]

            def mult(kc, qt_):
                res = []
                dlt = qt_ * 512 - kc * 128
                b_ = -dlt
                if 0 <= b_ < 512:
                    res.append((b_, 128, self.C16[:, C_ED + sidx * 128: C_ED + (sidx + 1) * 128], ['C16'], 'pool'))
                return res
            return dict(K=K, base=base, kT=kT, qT=qT, vT=vT, acc=acc,
                        scale=(slopes[hd] if isA else slopes[2 * (hd // 2)]), mult=mult)

        def mult_both(kc, qt_):
            dlt = qt_ * 512 - kc * 128
            return [(self.G[:, CBT[0] * 2048 + dlt + OFFB: CBT[0] * 2048 + dlt + OFFB + 512],
                     ['G%d' % CBT[0], 'G%d' % CBT[1]], 'dve')]

        if isA:
            for hd in range(4):
                for qt in range(4):
                    accs = [self.acc_bank(), self.acc_bank()]
                    lanes = [mk_lane(hd, 64 * (hd % 2) + 32 * mp, 32, accs[mp]) for mp in range(2)]

                    def fin(job, hd=hd, qt=qt, accs=accs):
                        return [self.fin_ab(l, True, hd, qt, accs, MIX[hd // 2], lam_init)]
                    jobs.append(dict(lanes=lanes, qt=qt, blocks=list(range(16)),
                                     ramp=(ramp[0], ramp[1], ramp[2], rkeys), fin=fin))
        else:
            for hp in range(2):
                for qt in range(4):
                    accs = [self.acc_bank(), self.acc_bank()]
                    lanes = [mk_lane(2 * hp + hh, 64 * hh, 64, accs[hh]) for hh in range(2)]
                    blocks = [kc for kc in range(16) if -1408 <= qt * 512 - kc * 128 <= 1024]

                    def fin(job, hp=hp, qt=qt, accs=accs):
                        return [self.fin_ab(l, False, 2 * hp + hh, qt, [accs[hh]], MIX[hp], 0.0) for hh in range(2)]
                    jobs.append(dict(lanes=lanes, qt=qt, blocks=blocks, mult_both=mult_both,
                                     ramp=(ramp[0], ramp[1], ramp[2], rkeys), fin=fin))
        self.ramp_keys = rkeys
        self.attn_run(jobs)
        wo, wok = self.wget()
        self.out_proj([(wo, wok)], MIX)

    def mixer_a2(self, l):
        tr = self.tr
        QA = [0, 1]
        KR = [2, 3]
        KL = [4, 5]
        MIX = [8, 9]
        tr.retire(['G%d' % i for i in range(12)])
        self.set_pt([(7, 0), (7, 1024), (10, 0), (10, 1024)])
        self.pv_lag = 3
        lam_init = 0.8 - 0.6 * math.exp(-0.3 * l)
        qk_scale = 32.0 ** -0.5
        def src(c, tt):
            return self.ht(c, tt * 512, 512), [('H%d' % c, tt)]
        wv, wvk = self.wget()
        for tc in range(16):
            b = self.nbank()
            for c in range(8):
                self.mm(self.ps(b, 256), self.ht(c, tc * 128, 128), wv[:, c * 256:(c + 1) * 256], c == 0, c == 7,
                        [wvk, ('H%d' % c, tc // 4)], ['PS%d' % b])
            ov = self.VT[:, tc * 260:(tc + 1) * 260].rearrange("p (h e) -> p h e", e=65)[:, :, 0:64]
            iv = self.ps(b, 256).rearrange("p (h e) -> p h e", e=64)
            if self.evac_eng() == 'act':
                self.act(ov, iv, AF.Copy, ['PS%d' % b], [('VT', tc)])
            else:
                self.cp('dve', ov, iv, ['PS%d' % b], [('VT', tc)])
        for hp in range(2):
            for hh in range(2):
                for (tiles, coff) in ((QA, C_RA + 2 * S), (KR, C_RA), (KL, C_RA + S)):
                    for base in (0, 64):
                        self.dma(self.g(tiles[hh])[base + 32: base + 64, :], self.c16[0:32, coff:coff + S], [],
                                 [('G%d' % tiles[hh], ('aug', base))])
            wq, wqk = self.wget()
            for hh in range(2):
                hd = 2 * hp + hh
                for tt in range(4):
                    b = self.nbank()
                    for c in range(8):
                        rhs, rk = src(c, tt)
                        self.mm(self.ps(b), wq[:, hh * 1024 + c * 128: hh * 1024 + c * 128 + 128], rhs, c == 0, c == 7,
                                [wqk] + rk, ['PS%d' % b])
                    for base in (0, 64):
                        self.copy_scaled(self.g(QA[hh])[base:base + 32, tt * 512:(tt + 1) * 512],
                                         self.ps(b, 512, base, base + 32), qk_scale / SL_A[hd],
                                         ['PS%d' % b], [('G%d' % QA[hh], ('f', base, tt))], eng='dve')
            wk_, wkk = self.wget()
            for hh in range(2):
                for tt in range(4):
                    b = self.nbank()
                    for c in range(8):
                        rhs, rk = src(c, tt)
                        self.mm(self.ps(b), wk_[:, hh * 1024 + c * 128: hh * 1024 + c * 128 + 128], rhs, c == 0, c == 7,
                                [wkk] + rk, ['PS%d' % b])
                    for base in (0, 64):
                        for tiles in (KR, KL):
                            self.copy_scaled(self.g(tiles[hh])[base:base + 32, tt * 512:(tt + 1) * 512],
                                             self.ps(b, 512, base, base + 32), 1.0,
                                             ['PS%d' % b], [('G%d' % tiles[hh], ('f', base, tt))], eng='dve')
            jobs = []
            for hh in range(2):
                hd = 2 * hp + hh
                sidx = SL_IDX_A[hd]
                for qt in range(4):
                    accs = [self.acc_bank(), self.acc_bank()]
                    lanes = []
                    for mp in range(2):
                        base = 64 * mp

                        def kTv(kc, var, hh=hh, base=base):
                            t = (KL if var == 'L' else KR)[hh]
                            K = 32 if var == 'D' else 64
                            keys = [('G%d' % t, ('f', base, kc // 4))]
                            if var != 'D':
                                keys.append(('G%d' % t, ('aug', base)))
                            return self.g(t)[base:base + K, kc * 128:(kc + 1) * 128], keys

                        def qTv(c0, n, var, hh=hh, base=base):
                            K = 32 if var == 'D' else 64
                            keys = [('G%d' % QA[hh], ('f', base, tq)) for tq in range(c0 // 512, (c0 + n - 1) // 512 + 1)]
                            if var != 'D':
                                keys.append(('G%d' % QA[hh], ('aug', base)))
                            return self.g(QA[hh])[base:base + K, c0:c0 + n], keys

                        def vT(kc, hd=hd):
                            return self.VT[:, (kc * 4 + hd) * 65:(kc * 4 + hd) * 65 + 65], [('VT', kc)]

                        def mult(kc, qt_, sidx=sidx):
                            b_ = kc * 128 - qt_ * 512
                            if 0 <= b_ < 512:
                                return [(b_, 128, self.C16[:, C_ED + sidx * 128: C_ED + (sidx + 1) * 128], ['C16'], 'pool')]
                            return []
                        lanes.append(dict(K=64, base=base, kTv=kTv, qTv=qTv, vT=vT, acc=accs[mp], scale=SL_A[hd],
                                          mult=mult))

                    def fin(job, hd=hd, qt=qt, accs=accs, hp=hp):
                        return [self.fin_ab(l, True, hd, qt, accs, MIX[hp], lam_init)]
                    jobs.append(dict(lanes=lanes, qt=qt, blocks=list(range(16)), ramp=None, fold=True, fin=fin))
            self.attn_run(jobs)
        wo, wok = self.wget()
        self.out_proj([(wo, wok)], MIX)

    def fin_ab(self, l, isA, hd, qt, accs, mixtile, lam_init):
        nc = self.nc
        ob = self.OB[:, (self.obrot % 2) * 256:(self.obrot % 2) * 256 + 256]
        okey = ('OB', self.obrot % 2)
        self.obrot += 1
        ob3 = ob.rearrange("p (q e) -> p q e", e=64)

        def acc3(b):
            return self.PS[:, b * 512: b * 512 + 260].rearrange("p (q e) -> p q e", e=65)
        a0 = acc3(accs[0])
        R1 = self.SM[:, 0:4]
        self.op('dve', lambda: nc.vector.reciprocal(out=R1, in_=a0[:, :, 64]), ['PS%d' % accs[0]], [('SM', 0)])
        r1b = R1.unsqueeze(2).to_broadcast([128, 4, 64])
        if not isA:
            self.tt('dve', ob3, a0[:, :, 0:64], r1b, ALU.mult, ['PS%d' % accs[0], ('SM', 0)], [okey])
        else:
            a1 = acc3(accs[1])
            O1 = self.OF[:, 0:256].rearrange("p (q e) -> p q e", e=64)
            O2 = self.OF[:, 256:512].rearrange("p (q e) -> p q e", e=64)
            self.tt('dve', O1, a0[:, :, 0:64], r1b, ALU.mult, ['PS%d' % accs[0], ('SM', 0)], [('OF', 0)])
            R2 = self.SM[:, 4:8]
            self.op('dve', lambda: nc.vector.reciprocal(out=R2, in_=a1[:, :, 64]), ['PS%d' % accs[1]], [('SM', 1)])
            R2L = self.SM[:, 8:12]
            self.ts('dve', R2L, R2, self.LAMS[:, l:l + 1], ALU.mult, [('SM', 1), 'LAMS'], [('SM', 2)])
            r2b = R2L.unsqueeze(2).to_broadcast([128, 4, 64])
            self.tt('dve', O2, a1[:, :, 0:64], r2b, ALU.mult, ['PS%d' % accs[1], ('SM', 2)], [('OF', 1)])
            self.tt('dve', O1, O1, O2, ALU.add, [('OF', 0), ('OF', 1)], [('OF', 0)])
            self.tt('dve', O2, O1, O1, ALU.mult, [('OF', 0)], [('OF', 1)])
            SSQ = self.SM[:, 12:16]
            self.op('dve', lambda: nc.vector.tensor_reduce(out=SSQ, in_=O2, axis=AX.X, op=ALU.add),
                    [('OF', 1)], [('SM', 3)])
            LN_ = self.SM[:, 16:20]
            self.act(LN_, SSQ, AF.Ln, [('SM', 3), 'EPS'], [('SM', 4)], scale=1.0 / 64, bias=self.EPSV[:, 0:1])
            RS = self.SM[:, 20:24]
            self.act(RS, LN_, AF.Exp, [('SM', 4)], [('SM', 5)], scale=-0.5)
            rsb = RS.unsqueeze(2).to_broadcast([128, 4, 64])
            self.tt('dve', O1, O1, rsb, ALU.mult, [('OF', 0), ('SM', 5)], [('OF', 0)])
            sub = self.vec(l, 195, 64).unsqueeze(1).to_broadcast([128, 4, 64])
            self.stt(ob3, O1, float(1.0 - lam_init), sub, ALU.mult, ALU.mult, [('OF', 0), 'VEC'], [okey])

        def deferred():
            self.transposes_to_mix(ob, okey, mixtile, hd % 2, qt, accs[0])
        return deferred

    def mixer_c(self, l):
        nc = self.nc
        tr = self.tr
        CQN = [0, 1]
        CKVN = 2
        KROPE = 3
        MIX = [4, 5, 6, 7]
        ROPE = [9, 10]
        tr.retire(['G%d' % i for i in range(12)])
        self.set_pt([(8, 0), (8, 1024), (11, 0), (11, 1024)])
        self.pv_lag = 3
        wcq, wcqk = self.wget()
        wckv, wckvk = self.wget()
        for i in range(2):
            self.dma(self.g(ROPE[i]), self.c16[:, C_ROPE + i * 2048: C_ROPE + (i + 1) * 2048], [], ['G%d' % ROPE[i]])
        cosT = self.g(ROPE[0])
        sinT = self.g(ROPE[1])
        for tt in range(4):
            t0 = tt * 512
            bq = [self.nbank(), self.nbank()]
            bkv = self.nbank()
            bk1 = self.nbank()
            bk2 = self.nbank()
            for i in range(2):
                for c in range(8):
                    self.mm(self.ps(bq[i]), wcq[:, i * 1024 + c * 128: i * 1024 + c * 128 + 128], self.ht(c, t0, 512),
                            c == 0, c == 7, [wcqk, ('H%d' % c, tt)], ['PS%d' % bq[i]])
            for c in range(8):
                self.mm(self.ps(bkv), wckv[:, c * 128:(c + 1) * 128], self.ht(c, t0, 512), c == 0, c == 7,
                        [wckvk, ('H%d' % c, tt)], ['PS%d' % bkv])
            for c in range(8):
                self.mm(self.ps(bk1, 512, 64, 96), wckv[:, 1024 + c * 32: 1024 + (c + 1) * 32], self.ht(c, t0, 512),
                        c == 0, c == 7, [wckvk, ('H%d' % c, tt)], ['PS%d' % bk1])
            for c in range(8):
                self.mm(self.ps(bk2, 512, 64, 96), wckv[:, 1280 + c * 32: 1280 + (c + 1) * 32], self.ht(c, t0, 512),
                        c == 0, c == 7, [wckvk, ('H%d' % c, tt)], ['PS%d' % bk2])
            for (banks, nfeat, goff, dst) in ((bq, 256, 16, CQN), ([bkv], 128, 18, [CKVN])):
                sb = self.nbank()
                for i, bnk in enumerate(banks):
                    q = self.sqrot % 3
                    self.sqrot += 1
                    self.act(self.SQ[:, q * 512:(q + 1) * 512], self.ps(bnk), AF.Square, ['PS%d' % bnk], [('SQ', q)])
                    self.mm(self.ps(sb), self.ONES, self.SQ[:, q * 512:(q + 1) * 512], i == 0, i == len(banks) - 1,
                            [('SQ', q), 'C16'], ['PS%d' % sb])
                self.act(self.LNV[:, 0:512], self.ps(sb), AF.Ln, ['PS%d' % sb, 'EPS'], ['LNV'],
                         scale=1.0 / nfeat, bias=self.EPSV[:, 0:1])
                rq = 0
                self.rsrot += 1
                self.act(self.RSTD[:, rq * 512:(rq + 1) * 512], self.LNV[:, 0:512], AF.Exp, ['LNV'], [('RSTD', rq)],
                         scale=-0.5)
                for i, bnk in enumerate(banks):
                    self.stt(self.g(dst[i])[:, t0:t0 + 512], self.ps(bnk), self.vec(l, goff + i),
                             self.RSTD[:, rq * 512:(rq + 1) * 512], ALU.mult, ALU.mult,
                             ['PS%d' % bnk, ('RSTD', rq), 'VEC'], [('G%d' % dst[i], tt)])
            T1 = self.RT1[64:96, 0:512]
            T2 = self.RT1[64:96, 512:1024]
            self.tt('dve', T1, self.ps(bk1, 512, 64, 96), cosT[64:96, t0:t0 + 512], ALU.mult,
                    ['PS%d' % bk1, 'G%d' % ROPE[0]], [('RT1', 0)])
            self.tt('dve', T2, self.ps(bk2, 512, 64, 96), sinT[64:96, t0:t0 + 512], ALU.mult,
                    ['PS%d' % bk2, 'G%d' % ROPE[1]], [('RT1', 1)])
            self.tt('dve', self.g(KROPE)[64:96, t0:t0 + 512], T1, T2, ALU.add, [('RT1', 0), ('RT1', 1)],
                    [('G%d' % KROPE, tt)])
        tr.retire(['H%d' % i for i in range(8)])
        wuq, wuqk = self.wget()
        wukv, wukvk = self.wget()
        QC = [0, 1]
        KC = [2, 3]
        VC = 4
        nc_ = self.nc
        self.op('pool', lambda: nc_.gpsimd.memset(self.h(VC)[:, 0:1040], 1.0), [], ['H%d' % VC])
        self.op('pool', lambda: nc_.gpsimd.memset(self.h(VC + 1)[:, 0:1040], 1.0), [], ['H%d' % (VC + 1)])
        qscale = 96.0 ** -0.5
        self.acc_pool = [4, 5]
        self.bank_pool = [6, 7]

        def proj(hd):
            pb = hd % 2
            qc_t, kc_t = QC[pb], KC[pb]
            vct = VC + pb
            units = []

            def uq(tt):
                t0 = tt * 512
                ba = self.nbank()
                for c in range(2):
                    self.mm(self.ps(ba, 512, 0, 96), wuq[:, c * 768 + hd * 96: c * 768 + hd * 96 + 96],
                            self.g(CQN[c])[:, t0:t0 + 512], c == 0, c == 1, [wuqk, ('G%d' % CQN[c], tt)], ['PS%d' % ba])
                self.ts('dve', self.h(qc_t)[0:64, t0:t0 + 512], self.ps(ba, 512, 0, 64), qscale, ALU.mult,
                        ['PS%d' % ba], [('H%d' % qc_t, ('n', tt))])
                T1 = self.RT1[64:96, 0:512]
                T2 = self.RT1[64:96, 512:1024]
                self.stt(T1, self.ps(ba, 512, 64, 96), qscale, cosT[64:96, t0:t0 + 512], ALU.mult, ALU.mult,
                         ['PS%d' % ba, 'G%d' % ROPE[0]], [('RT1', 0)])
                bb = self.nbank()
                for c in range(2):
                    self.mm(self.ps(bb, 512, 64, 96), wuq[:, 1536 + c * 256 + hd * 32: 1536 + c * 256 + hd * 32 + 32],
                            self.g(CQN[c])[:, t0:t0 + 512], c == 0, c == 1, [wuqk, ('G%d' % CQN[c], tt)], ['PS%d' % bb])
                self.stt(T2, self.ps(bb, 512, 64, 96), qscale, sinT[64:96, t0:t0 + 512], ALU.mult, ALU.mult,
                         ['PS%d' % bb, 'G%d' % ROPE[1]], [('RT1', 1)])
                self.tt('pool', self.h(qc_t)[64:96, t0:t0 + 512], T1, T2, ALU.add, [('RT1', 0), ('RT1', 1)],
                        [('H%d' % qc_t, ('r', tt))])

            def uk(tt):
                t0 = tt * 512
                b = self.nbank()
                self.mm(self.ps(b, 512, 0, 64), wukv[:, hd * 128: hd * 128 + 64], self.g(CKVN)[:, t0:t0 + 512],
                        True, True, [wukvk, ('G%d' % CKVN, tt)], ['PS%d' % b])
                self.cp('dve', self.h(kc_t)[0:64, t0:t0 + 512], self.ps(b, 512, 0, 64), ['PS%d' % b],
                        [('H%d' % kc_t, ('n', tt))])
                self.cp('pool', self.h(kc_t)[64:96, t0:t0 + 512], self.g(KROPE)[64:96, t0:t0 + 512],
                        [('G%d' % KROPE, tt)], [('H%d' % kc_t, ('r', tt))])

            def uv(half):
                b = self.nbank()
                for t8 in range(8):
                    tc = half * 8 + t8
                    self.mm(self.ps(b, 64, 0, 128, t8 * 64), self.g(CKVN)[:, tc * 128:(tc + 1) * 128],
                            wukv[:, hd * 128 + 64: hd * 128 + 128], True, True,
                            [wukvk, ('G%d' % CKVN, tc // 4)], ['PS%d' % b], skip_group_check=True)
                ov = self.h(vct)[:, half * 520: half * 520 + 520].rearrange("p (t e) -> p t e", e=65)[:, :, 0:64]
                iv = self.ps(b).rearrange("p (t e) -> p t e", e=64)
                self.cp('dve', ov, iv, ['PS%d' % b, 'H%d' % vct], [('H%d' % vct, (pb, half))])

            for tt in range(4):
                units.append(lambda tt=tt: uq(tt))
            for tt in range(4):
                units.append(lambda tt=tt: uk(tt))
            for half in range(2):
                units.append(lambda half=half: uv(half))
            return units

        for u in proj(0):
            u()
        for hd in range(8):
            pb = hd % 2
            qc_t, kc_t = QC[pb], KC[pb]
            vct = VC + pb
            jobs = []
            for qt in range(4):
                acc = self.acc_bank()

                def kT(kc, kc_t=kc_t):
                    return (self.h(kc_t)[0:96, kc * 128:(kc + 1) * 128],
                            [('H%d' % kc_t, ('n', kc // 4)), ('H%d' % kc_t, ('r', kc // 4))])

                def qT(qt_, qc_t=qc_t):
                    return (self.h(qc_t)[0:96, qt_ * 512:(qt_ + 1) * 512],
                            [('H%d' % qc_t, ('n', qt_)), ('H%d' % qc_t, ('r', qt_))])

                def vT(kc, pb=pb, vct=vct):
                    return self.h(vct)[:, kc * 65: kc * 65 + 65], [('H%d' % vct, (pb, kc // 8)), 'H%d' % vct]

                def fin(job, hd=hd, qt=qt, acc=acc):
                    return [self.fin_ab(l, False, hd, qt, [acc], MIX[hd // 2], 0.0)]
                jobs.append(dict(lanes=[dict(K=96, base=0, kT=kT, qT=qT, vT=vT, acc=acc, scale=1.0, mult=None)],
                                 qt=qt, blocks=list(range(16)), ramp=None, fin=fin))
            hooks = {}
            if hd < 7:
                for ui, u in enumerate(proj(hd + 1)):
                    hooks[2 + 3 * ui] = u
            self.attn_run(jobs, hooks)
        self.acc_pool = [4, 5, 6, 7]
        self.bank_pool = list(range(8))
        wo1, wo1k = self.wget()
        wo2, wo2k = self.wget()
        self.out_proj([(wo1, wo1k), (wo2, wo2k)], MIX)

    def ffn(self, l):
        nc = self.nc
        tr = self.tr
        tr.retire(['G%d' % i for i in range(12)])
        self.cast_eng = 'act'
        ACT_T = 0
        CEN = [6, 7, 8, 9]
        SG = [10, 11]
        def part1(tt2, gi, jj):
            t0 = tt2 * 1024
            j = GROUPS[gi][jj]
            wu, wuk = self.wget()
            cbuf = jj % 2
            for gv in range(2):
                ch = j + 22 * gv
                bm = 2 * gv
                bh = 4 + gv
                for half in range(2):
                    for c in range(8):
                        self.mm(self.ps(bm + half), wu[:, gv * 1024 + c * 128: gv * 1024 + c * 128 + 128],
                                self.ht(c, t0 + half * 512, 512), c == 0, c == 7,
                                [wuk, ('H%d' % c, tt2 * 2 + half)], ['PS%d' % (bm + half)])
                tcol = t0 - 1 if tt2 == 1 else t0 + 1024
                for c in range(8):
                    self.mm(self.ps(bh, 2), wu[:, gv * 1024 + c * 128: gv * 1024 + c * 128 + 128],
                            self.ht(c, tcol - (1 if tt2 == 0 else 0), 2),
                            c == 0, c == 7, [wuk] + self.hk(c, tcol - (1 if tt2 == 0 else 0), 2), ['PS%d' % bh])
                hcol = 1 if tt2 == 0 else 0
                ct = CEN[cbuf * 2 + gv]
                cen = self.gf(ct)
                mainp = self.PS[:, bm * 512: bm * 512 + 1024]
                mkeys = ['PS%d' % bm, 'PS%d' % (bm + 1)]
                self.act(cen, mainp, AF.Identity, mkeys + ['VEC'], ['G%d' % ct],
                         scale=self.vec(l, 63 + ch), bias=self.vec(l, 151 + ch))
                self.stt(cen[:, 1:1024], mainp[:, 0:1023], self.vec(l, 19 + ch), cen[:, 1:1024], ALU.mult, ALU.add,
                         mkeys + ['VEC', 'G%d' % ct], ['G%d' % ct])
                self.stt(cen[:, 0:1023], mainp[:, 1:1024], self.vec(l, 107 + ch), cen[:, 0:1023], ALU.mult, ALU.add,
                         mkeys + ['VEC', 'G%d' % ct], ['G%d' % ct])
                if tt2 == 1:
                    self.stt(cen[:, 0:1], self.ps(bh, 1, 0, 128, hcol), self.vec(l, 19 + ch), cen[:, 0:1],
                             ALU.mult, ALU.add, ['PS%d' % bh, 'VEC', 'G%d' % ct], ['G%d' % ct])
                else:
                    self.stt(cen[:, 1023:1024], self.ps(bh, 1, 0, 128, hcol), self.vec(l, 107 + ch),
                             cen[:, 1023:1024], ALU.mult, ALU.add, ['PS%d' % bh, 'VEC', 'G%d' % ct], ['G%d' % ct])

        def part2(jj):
            cbuf = jj % 2
            cg = self.gf(CEN[cbuf * 2 + 0])
            cv = self.gf(CEN[cbuf * 2 + 1])
            sg = self.gf(SG[cbuf])
            self.act(sg, cg, AF.Silu, ['G%d' % CEN[cbuf * 2]], ['G%d' % SG[cbuf]])
            at = self.G[:, jj * 1024:(jj + 1) * 1024]
            self.tt('pool', at, cv, sg, ALU.mult, ['G%d' % CEN[cbuf * 2 + 1], 'G%d' % SG[cbuf]],
                    [('G%d' % (jj // 2), ('a', jj % 2))])

        def down(tt2, gi):
            n = len(GROUPS[gi])
            for oc in range(8):
                wd, wdk = self.wget()
                for half in range(2):
                    b = 6 + (oc * 2 + half) % 2
                    for jj in range(n):
                        self.mm(self.ps(b), wd[:, jj * 128:(jj + 1) * 128],
                                self.G[:, jj * 1024 + half * 512: jj * 1024 + half * 512 + 512],
                                jj == 0, jj == n - 1, [wdk, ('G%d' % (jj // 2), ('a', jj % 2))], ['PS%d' % b])
                    tt = tt2 * 2 + half
                    self.tt('dve', self.xt(oc, tt * 512, 512), self.ps(b), self.xt(oc, tt * 512, 512), ALU.add,
                            ['PS%d' % b, ('XT', (oc, tt))], [('XT', (oc, tt))])

        groups = [(tt2, gi) for tt2 in range(2) for gi in range(2)]
        pre = False
        for k, (tt2, gi) in enumerate(groups):
            for jj in range(len(GROUPS[gi])):
                if not (jj == 0 and pre):
                    part1(tt2, gi, jj)
                part2(jj)
            pre = False
            if k + 1 < len(groups):
                part1(groups[k + 1][0], groups[k + 1][1], 0)
                pre = True
            down(tt2, gi)

    def build(self):
        depth, nseq = self.depth, self.nseq
        nc = bass.Bass("TRN2", target_bir_lowering=False)
        self.nc = nc
        self.xTd = nc.dram_tensor("xT", [nseq, D, S], F32, kind="ExternalInput").ap()
        self.wst = nc.dram_tensor("wst", [depth * SLOTS_PER_LAYER, 128, 2048], F32, kind="ExternalInput").ap()
        self.vecd = nc.dram_tensor("vecs", [128, NV], F32, kind="ExternalInput").ap()
        self.c16 = nc.dram_tensor("c16", [128, NC16], BF16, kind="ExternalInput").ap()
        self.outT = nc.dram_tensor("outT", [nseq, D, S], F32, kind="ExternalOutput").ap()
        sched = []
        for b in range(nseq):
            for l in range(depth):
                base = l * SLOTS_PER_LAYER
                sizes = [2048, 2048, 2048, 2048, 2048, 2048, 2048, 2048, 2048, 2048, 2048, 1536, 2048, 1024, 2048, 2048]
                for i, n in enumerate(sizes):
                    sched.append((base + i, n))
                groups = [(tt2, gi) for tt2 in range(2) for gi in range(2)]
                for k, (tt2, gi) in enumerate(groups):
                    n = len(GROUPS[gi])
                    ubase = base + 16 + (20 if gi == 1 else 0)
                    for jj in range(n):
                        if not (jj == 0 and k > 0):
                            sched.append((ubase + jj, 2048))
                    if k + 1 < len(groups):
                        sched.append((base + 16 + (20 if groups[k + 1][1] == 1 else 0), 2048))
                    for oc in range(8):
                        sched.append((ubase + n + oc, n * 128))
        self.ws_init(sched)
        self.obrot = 0
        self.sqrot = 0
        self.rsrot = 0
        from contextlib import ExitStack
        with ExitStack() as es:
            def sb(name, shape, dt):
                return es.enter_context(nc.sbuf_tensor(name, shape, dt))
            XT = sb("XT", [128, 8 * S], F32)
            HT = sb("HT", [128, 8 * S], BF16)
            G = sb("G", [128, 12 * 2048], BF16)
            VT = sb("VT", [128, 16 * 4 * 65], BF16)
            WST = sb("WST", [128, 2 * 2048], F32)
            WR = sb("WR", [128, R_RING * 2048], BF16)
            VEC = sb("VEC", [128, NV], F32)
            C16 = sb("C16", [128, C_ROPE], BF16)
            EPSV = sb("EPSV", [128, 2], F32)
            LAMS = sb("LAMS", [128, 4], F32)
            LT = sb("LT", [128, 64], F32)
            SQ = sb("SQ", [128, 3 * 512], BF16)
            LNV = sb("LNV", [128, 512], F32)
            RSTD = sb("RSTD", [128, 512], F32)
            RT1 = sb("RT1", [128, 1024], F32)
            OF = sb("OF", [128, 512], F32)
            OB = sb("OB", [128, 512], BF16)
            SM = sb("SM", [128, 32], F32)
            PS = es.enter_context(nc.psum_tensor("PS", [128, 8 * 512], F32))
            s_pe = es.enter_context(nc.semaphore("s_pe"))
            s_act = es.enter_context(nc.semaphore("s_act"))
            s_dve = es.enter_context(nc.semaphore("s_dve"))
            s_pool = es.enter_context(nc.semaphore("s_pool"))
            dsl = [es.enter_context(nc.semaphore("d%d" % i)) for i in range(8)]
            block = es.enter_context(nc.Block())
            self.XT, self.HT, self.G, self.VT, self.WST, self.WR, self.VEC, self.C16 = XT, HT, G, VT, WST, WR, VEC, C16
            self.EPSV, self.LAMS, self.SQ, self.LNV, self.RSTD, self.RT1, self.OF, self.OB, self.SM, self.PS = \
                EPSV, LAMS, SQ, LNV, RSTD, RT1, OF, OB, SM, PS
            self.IDENT = C16[:, C_ID:C_ID + 128]
            self.ONES = C16[:, C_ONES:C_ONES + 128]

            self.dma(VEC[:, :], self.vecd[:, :], [], ['VEC'])
            self.dma(C16[:, :], self.c16[:, 0:C_ROPE], [], ['C16'])
            self.op('pool', lambda: nc.gpsimd.memset(EPSV[:, :], EPS), [], ['EPS'])
            self.op('pool', lambda: nc.gpsimd.memset(VT[:, :], 1.0), [], ['VT'])
            self.tr.retire(['VT'])
            for l in range(depth):
                lam_init = 0.8 - 0.6 * math.exp(-0.3 * l)
                for k in range(2):
                    a = self.vec(l, 259 + 64 * k, 32)
                    b_ = self.vec(l, 291 + 64 * k, 32)
                    self.tt('dve', LT[:, 0:32], a, b_, ALU.mult, ['VEC'], [('LT', 0)])
                    self.op('dve', lambda k=k: nc.vector.tensor_reduce(out=LT[:, 32 + k:33 + k], in_=LT[:, 0:32],
                                                                       axis=AX.X, op=ALU.add), [('LT', 0)], [('LT', 1 + k)])
                    self.act(LT[:, 34 + k:35 + k], LT[:, 32 + k:33 + k], AF.Exp, [('LT', 1 + k)], [('LT', 3 + k)])
                self.stt(LAMS[:, l:l + 1], LT[:, 35:36], float(-lam_init), LT[:, 34:35], ALU.add, ALU.subtract,
                         [('LT', 3), ('LT', 4)], ['LAMS'])

            for b in range(nseq):
                for c in range(8):
                    self.dma(XT[:, c * S:(c + 1) * S], self.xTd[b, c * 128:(c + 1) * 128, :],
                             [], [('XT', (c, tt)) for tt in range(4)])
                for l in range(depth):
                    self.tr.retire(['H%d' % i for i in range(8)])
                    self.cast_eng = 'dve'
                    self.rmsnorm_x(l, 0)
                    self.mixer_a2(l)
                    self.mixer_ab(l, 'B')
                    self.mixer_c(l)
                    self.tr.retire(['H%d' % i for i in range(8)])
                    self.rmsnorm_x(l, 8)
                    self.ffn(l)
                self.tr.retire(['G%d' % i for i in range(12)])
                self.rmsnorm_x(0, 0, final=True, seq=b)

            @block.sync
            def _(sync):
                self.tr.emit(nc, {'pe': s_pe, 'act': s_act, 'dve': s_dve, 'pool': s_pool},
                             dsl)
        return nc


def _bf16(a):
    return np.asarray(a, dtype=np.float32).astype(ml_dtypes.bfloat16)


def make_consts():
    c = np.zeros((128, NC16), dtype=np.float32)
    c[:, C_ID:C_ID + 128] = np.eye(128, dtype=np.float32)
    c[:, C_ONES:C_ONES + 128] = 1.0
    ki = np.arange(128)[:, None]
    qi = np.arange(128)[None, :]
    for s in range(8):
        slope = 2.0 ** -(s + 1)
        c[:, C_ED + s * 128: C_ED + (s + 1) * 128] = np.exp(-slope * np.abs(qi - ki))
    pos = np.arange(S, dtype=np.float32)
    inv_freq = (10000.0 ** (-np.arange(0, 32, 2, dtype=np.float32) / 32)).astype(np.float32)
    ang = (pos[:, None] * inv_freq[None, :]).astype(np.float32)
    cos = np.cos(ang).astype(np.float32).T
    sin = np.sin(ang).astype(np.float32).T
    c[64:80, C_ROPE:C_ROPE + S] = cos
    c[80:96, C_ROPE:C_ROPE + S] = cos
    c[64:80, C_ROPE + S:C_ROPE + 2 * S] = -sin
    c[80:96, C_ROPE + S:C_ROPE + 2 * S] = sin
    p = np.arange(S)
    hi = (64 * (p // 64)).astype(np.float32)
    lo = (p % 64).astype(np.float32)
    for (off, bases) in ((C_RA, (0, 32, 64, 96)), (C_RB, (0, 64))):
        for b in bases:
            c[b + 0, off:off + S] = hi
            c[b + 1, off:off + S] = lo
            c[b + 2, off:off + S] = -1.0
            c[b + 3, off:off + S] = -1.0
            if off == C_RB and b == 64:
                c[b:b + 4, off:off + S] *= 0.25
            c[b:b + 4, off + S:off + 2 * S] = -c[b:b + 4, off:off + S]
            c[b + 0, off + 2 * S:off + 3 * S] = 1.0
            c[b + 1, off + 2 * S:off + 3 * S] = 1.0
            c[b + 2, off + 2 * S:off + 3 * S] = hi
            c[b + 3, off + 2 * S:off + 3 * S] = lo
    m = np.arange(CBW)[None, :]
    d = m - np.arange(128)[:, None] - OFFB
    ad = np.abs(d)
    cb = (ad <= 64).astype(np.float32) + ((d % 4 == 0) & (ad <= 256)) + ((d % 16 == 0) & (ad <= 1024))
    c[:, C_CB:C_CB + CBW] = cb
    return _bf16(c)


def fm(w, M=None):
    k = w.shape[0] // 128
    return np.ascontiguousarray(w.reshape(k, 128, w.shape[1]).transpose(1, 0, 2))


def make_wstream(inp, depth):
    out = np.zeros((depth * SLOTS_PER_LAYER, 128, 2048), dtype=np.float32)

    def put(idx, arr):
        a = np.ascontiguousarray(arr, dtype=np.float32).reshape(128, -1)
        out[idx, :, :a.shape[1]] = a

    for l in range(depth):
        b = l * SLOTS_PER_LAYER
        w_in = np.asarray(inp['w_in'][l])
        w_out = np.asarray(inp['w_out'][l])

        def fm2(cols0):
            return np.stack([fm(w_in[:, cols0 + i * 128: cols0 + (i + 1) * 128]) for i in range(2)], axis=1)
        def fmA(cols0, hp):
            outs = []
            for hh in range(2):
                c0 = cols0 + (2 * hp + hh) * 64
                m0 = w_in[:, c0:c0 + 32]
                m1 = w_in[:, c0 + 32:c0 + 64]
                outs.append(fm(np.concatenate([m0, m1, m1, m0], axis=1)))
            return np.stack(outs, axis=1)
        put(b + 0, fm(w_in[:, 512:768]))
        put(b + 1, fmA(0, 0))
        put(b + 2, fmA(256, 0))
        put(b + 3, fmA(0, 1))
        put(b + 4, fmA(256, 1))
        put(b + 5, fm(w_out[0:256, :]))
        put(b + 6, fm2(768))
        put(b + 7, fm2(1024))
        put(b + 8, fm(w_in[:, 1280:1536]))
        put(b + 9, fm(w_out[256:512, :]))
        put(b + 10, fm2(1536))
        kr = w_in[:, 1920:1952]
        kr_sw = np.concatenate([kr[:, 16:32], kr[:, 0:16]], axis=1)
        put(b + 11, np.concatenate([fm(w_in[:, 1792:1920]).reshape(128, -1), fm(kr).reshape(128, -1),
                                    fm(kr_sw).reshape(128, -1)], axis=1))
        uq = np.asarray(inp['c_w_uq'][l])
        uq_h = uq.reshape(256, 8, 96)
        uqs = np.concatenate([uq_h[:, :, 80:96], uq_h[:, :, 64:80]], axis=2).reshape(256, 256)
        put(b + 12, np.concatenate([fm(uq).reshape(128, -1), fm(uqs).reshape(128, -1)], axis=1))
        put(b + 13, np.asarray(inp['c_w_ukv'][l]))
        put(b + 14, fm(w_out[512:768, :]))
        put(b + 15, fm(w_out[768:1024, :]))
        w_up = np.asarray(inp['w_up'][l])
        w_down = np.asarray(inp['w_down'][l])
        o = b + 16
        for grp in GROUPS:
            for j in grp:
                put(o, np.concatenate([fm(w_up[:, j * 128:(j + 1) * 128]).reshape(128, -1),
                                       fm(w_up[:, DFF + j * 128: DFF + (j + 1) * 128]).reshape(128, -1)], axis=1))
                o += 1
            for oc in range(8):
                rows = np.stack([w_down[j * 128:(j + 1) * 128, oc * 128:(oc + 1) * 128] for j in grp], axis=1)
                put(o, rows)
                o += 1
    return out


def make_vecs(inp, depth):
    v = np.zeros((128, NV), dtype=np.float32)

    def colz(a):
        a = np.asarray(a, dtype=np.float32)
        return a.reshape(-1, 128).T

    for l in range(depth):
        o = l * VL
        v[:, o + 0:o + 8] = colz(inp['g_attn'][l])
        v[:, o + 8:o + 16] = colz(inp['g_ffn'][l])
        v[:, o + 16:o + 18] = colz(inp['c_g_q'][l])
        v[:, o + 18:o + 19] = colz(inp['c_g_kv'][l])
        for k in range(3):
            v[:, o + 19 + 44 * k: o + 19 + 44 * (k + 1)] = colz(inp['conv_w'][l][k])
        v[:, o + 151:o + 195] = colz(inp['conv_b'][l])
        v[:, o + 195:o + 259] = np.broadcast_to(np.asarray(inp['a_subln'][l], dtype=np.float32)[None, :], (128, 64))
        for i, nm in enumerate(('a_lq1', 'a_lk1', 'a_lq2', 'a_lk2')):
            v[:, o + 259 + 32 * i: o + 259 + 32 * (i + 1)] = np.broadcast_to(
                np.asarray(inp[nm][l], dtype=np.float32)[None, :], (128, 32))
    v[:, DEPTH * VL: DEPTH * VL + 8] = colz(inp['g_final'])
    return v


_CACHE = {}


def kernel(**inputs):
    x = np.asarray(inputs['x'], dtype=np.float32)
    nb = x.shape[0]
    assert nb == NCORES * NSEQ
    wst = make_wstream(inputs, DEPTH)
    vecs = make_vecs(inputs, DEPTH)
    c16 = make_consts()
    if 'nc' not in _CACHE:
        _CACHE['nc'] = Builder().build()
    nc = _CACHE['nc']
    in_maps = []
    for i in range(NCORES):
        xT = np.ascontiguousarray(x[i * NSEQ:(i + 1) * NSEQ].transpose(0, 2, 1))
        in_maps.append({"xT": xT, "wst": wst, "vecs": vecs, "c16": c16})
    res = run_bass_kernel_spmd(nc, in_maps, core_ids=list(range(NCORES)))
    outs = [np.asarray(r["outT"]).transpose(0, 2, 1) for r in res.results]
    return np.ascontiguousarray(np.concatenate(outs, axis=0)).astype(np.float32)
```

```python
import math
import numpy as np
import ml_dtypes
import concourse.bass as bass
import concourse.mybir as mybir
from concourse.bass_utils import run_bass_kernel_spmd

F32 = mybir.dt.float32
BF16 = mybir.dt.bfloat16
AF = mybir.ActivationFunctionType
ALU = mybir.AluOpType
AX = mybir.AxisListType

S = 2048
D = 1024
DEPTH = 4
NCORES = 8
NSEQ = 2
DFF = 2816
NJ = 22
EPS = 1e-6
SL_A = [2.0 ** -1, 2.0 ** -3, 2.0 ** -5, 2.0 ** -7]
SL_B = [2.0 ** -2, 2.0 ** -4, 2.0 ** -6, 2.0 ** -8]
SL_IDX_A = [0, 2, 4, 6]
SL_IDX_B = [1, 3, 5, 7]
OFFB = 1408
CBW = 2944
VL = 387
NV = DEPTH * VL + 8
GROUPS = [list(range(0, 12)), list(range(12, 22))]
SLOTS_PER_LAYER = 17 + (12 + 8) + (10 + 8)
C_ID, C_ONES, C_ED, C_ROPE, C_RA, C_RB, C_CB = 0, 128, 256, 1280, 5376, 11520, 17664
NC16 = C_CB + CBW
R_RING = 4
LOOK = 2


class Tr:
    def __init__(self):
        self.ops = []
        self.st = {}
        self.carry = {}

    def _m(self, d, idx):
        e = self.ops[idx][0]
        key = e if e != 'sp' else ('sp', idx)
        if d.get(key, -1) < idx:
            d[key] = idx

    @staticmethod
    def _k(k):
        return (k, None) if isinstance(k, str) else k

    def op(self, eng, fn, r=(), w=()):
        deps = {}
        for k in r:
            t, sub = self._k(k)
            for i in self.carry.get(t, {}).values():
                self._m(deps, i)
            s = self.st.get(t, {}).get(sub)
            if s is not None and s[0] is not None:
                self._m(deps, s[0])
        for k in w:
            t, sub = self._k(k)
            for i in self.carry.get(t, {}).values():
                self._m(deps, i)
            s = self.st.get(t, {}).get(sub)
            if s is not None:
                if s[0] is not None:
                    self._m(deps, s[0])
                for i in s[1].values():
                    self._m(deps, i)
        idx = len(self.ops)
        self.ops.append([eng, fn, sorted(deps.values()), False])
        for k in r:
            t, sub = self._k(k)
            s = self.st.setdefault(t, {}).setdefault(sub, [None, {}])
            key = eng if eng != 'sp' else ('sp', idx)
            s[1][key] = idx
        for k in w:
            t, sub = self._k(k)
            self.st.setdefault(t, {})[sub] = [idx, {}]
        return idx

    def retire(self, tiles):
        for t in tiles:
            c = self.carry.setdefault(t, {})
            for sub, s in self.st.get(t, {}).items():
                if s[0] is not None:
                    self._m(c, s[0])
                for i in s[1].values():
                    self._m(c, i)
            self.st[t] = {}

    def emit(self, nc, sems, dsems):
        ops = self.ops
        for o in ops:
            for d in o[2]:
                if not (ops[d][0] == 'pe' and o[0] == 'pe'):
                    ops[d][3] = True
        engobj = {'pe': nc.tensor, 'act': nc.scalar, 'dve': nc.vector, 'pool': nc.gpsimd, 'sp': nc.sync}
        cnt = {e: 0 for e in sems}
        seen = {e: {} for e in engobj}
        dcnt = [0] * len(dsems)
        drot = 0
        ev = [None] * len(ops)
        for i, o in enumerate(ops):
            eng = o[0]
            eo = engobj[eng]
            for d in o[2]:
                if ops[d][0] == 'pe' and eng == 'pe':
                    continue
                sid, sem, val = ev[d]
                if seen[eng].get(sid, 0) < val:
                    eo.wait_ge(sem, val)
                    seen[eng][sid] = val
            if eng == 'sp':
                j = drot
                drot = (drot + 1) % len(dsems)
                sid = ('d', j)
                if seen[eng].get(sid, 0) < dcnt[j]:
                    eo.wait_ge(dsems[j], dcnt[j])
                    seen[eng][sid] = dcnt[j]
                inst = o[1]()
                dcnt[j] += 16
                inst.then_inc(dsems[j], 16)
                ev[i] = (sid, dsems[j], dcnt[j])
            else:
                inst = o[1]()
                if o[3]:
                    cnt[eng] += 1
                    inst.then_inc(sems[eng], 1)
                    ev[i] = (eng, sems[eng], cnt[eng])
        for j in range(len(dsems)):
            if dcnt[j] > 0:
                nc.sync.wait_ge(dsems[j], dcnt[j])


class Builder:
    def __init__(self, depth=DEPTH, nseq=NSEQ, dbg=None):
        self.depth = depth
        self.nseq = nseq
        self.dbg = dbg
        self.tr = Tr()
        self.srot = 0
        self.arot = 0
        self.brot = 0
        self.prot = 0
        self.erot = 0
        self.cast_eng = 'dve'
        self.pv_lag = 2
        self.bank_pool = list(range(8))
        self.acc_pool = [4, 5, 6, 7]

    def xt(self, c, t0, n):
        return self.XT[:, c * S + t0: c * S + t0 + n]

    def ht(self, c, t0, n):
        return self.HT[:, c * S + t0: c * S + t0 + n]

    def hk(self, c, t0, n=1):
        return [('H%d' % c, tt) for tt in range(t0 // 512, (t0 + n - 1) // 512 + 1)]

    def g(self, i):
        return self.G[:, i * 2048:(i + 1) * 2048]

    def gf(self, i):
        return self.G[:, i * 2048:(i + 1) * 2048].bitcast(F32)

    def h(self, i):
        return self.HT[:, i * 2048:(i + 1) * 2048]

    def ps(self, b, n=512, p0=0, p1=128, c0=0):
        return self.PS[p0:p1, b * 512 + c0: b * 512 + c0 + n]

    def psb(self, b):
        return self.PS[:, b * 512:(b + 1) * 512].bitcast(BF16)

    def vec(self, l, off, n=1):
        return self.VEC[:, l * VL + off: l * VL + off + n]

    def ptk(self, pp):
        return self.PT_PAIRS[pp][1]

    def ptv(self, pp):
        return self.PT_PAIRS[pp][0]

    def set_pt(self, specs):
        self.PT_PAIRS = [(self.G[:, t * 2048 + c0: t * 2048 + c0 + 1024], ('G%d' % t, ('pt', c0))) for (t, c0) in specs]
        self.prot = 0

    def nbank(self):
        pool = self.bank_pool
        b = pool[self.brot % len(pool)]
        self.brot += 1
        return b

    def op(self, eng, fn, r=(), w=()):
        return self.tr.op(eng, fn, r, w)

    def mm(self, out, lhsT, rhs, start, stop, r, w, **kw):
        nc = self.nc
        return self.op('pe', lambda: nc.tensor.matmul(out, lhsT=lhsT, rhs=rhs, start=start, stop=stop, **kw), r, w)

    def act(self, out, in_, func, r, w, scale=None, bias=None):
        nc = self.nc
        kw = {}
        if scale is not None:
            kw['scale'] = scale
        if bias is not None:
            kw['bias'] = bias
        return self.op('act', lambda: nc.scalar.activation(out=out, in_=in_, func=func, **kw), r, w)

    def tt(self, eng, out, in0, in1, op, r, w):
        nc = self.nc
        e = nc.vector if eng == 'dve' else nc.gpsimd
        return self.op(eng, lambda: e.tensor_tensor(out=out, in0=in0, in1=in1, op=op), r, w)

    def ts(self, eng, out, in0, s1, op0, r, w, s2=None, op1=None):
        nc = self.nc
        e = nc.vector if eng == 'dve' else nc.gpsimd
        if op1 is None:
            return self.op(eng, lambda: e.tensor_scalar(out=out, in0=in0, scalar1=s1, scalar2=None, op0=op0), r, w)
        return self.op(eng, lambda: e.tensor_scalar(out=out, in0=in0, scalar1=s1, scalar2=s2, op0=op0, op1=op1), r, w)

    def stt(self, out, in0, scalar, in1, op0, op1, r, w):
        nc = self.nc
        return self.op('dve', lambda: nc.vector.scalar_tensor_tensor(out=out, in0=in0, scalar=scalar, in1=in1,
                                                                     op0=op0, op1=op1), r, w)

    def cp(self, eng, out, in_, r, w):
        nc = self.nc
        e = {'dve': nc.vector, 'pool': nc.gpsimd}[eng]
        return self.op(eng, lambda: e.tensor_copy(out=out, in_=in_), r, w)

    def dma(self, out, in_, r, w):
        nc = self.nc
        return self.op('sp', lambda: nc.sync.dma_start(out=out, in_=in_), r, w)

    def ws_init(self, sched):
        self.wsched = sched
        self.wnext = 0
        self.wcons = 0

    def wget(self):
        i = self.wcons
        self.wcons += 1
        lim = min(i + LOOK, len(self.wsched) - 1)
        nc = self.nc
        while self.wnext <= lim:
            j = self.wnext
            slot, n = self.wsched[j]
            st = j % 2
            self.dma(self.WST[:, st * 2048: st * 2048 + n], self.wst[slot, :, 0:n], [], [('WST', st)])
            ce = self.cast_eng
            if ce == 'act':
                self.act(self.WR[:, (j % R_RING) * 2048:(j % R_RING) * 2048 + n],
                         self.WST[:, st * 2048: st * 2048 + n], AF.Copy, [('WST', st)], [('WR', j % R_RING)])
            else:
                self.cp(ce, self.WR[:, (j % R_RING) * 2048:(j % R_RING) * 2048 + n],
                        self.WST[:, st * 2048: st * 2048 + n], [('WST', st)], [('WR', j % R_RING)])
            self.wnext += 1
        r = i % R_RING
        return self.WR[:, r * 2048:(r + 1) * 2048], ('WR', r)

    def rmsnorm_x(self, l, goff, final=False, seq=0):
        for tt in range(4):
            t0 = tt * 512
            sb = self.nbank()
            for c in range(8):
                q = (tt * 8 + c) % 3
                self.act(self.SQ[:, q * 512:(q + 1) * 512], self.xt(c, t0, 512), AF.Square,
                         [('XT', (c, tt))], [('SQ', q)])
                self.mm(self.ps(sb), self.ONES, self.SQ[:, q * 512:(q + 1) * 512], c == 0, c == 7,
                        [('SQ', q), 'C16'], ['PS%d' % sb])
            self.act(self.LNV[:, 0:512], self.ps(sb), AF.Ln, ['PS%d' % sb, 'EPS'], ['LNV'],
                     scale=1.0 / D, bias=self.EPSV[:, 0:1])
            rq = 0
            self.act(self.RSTD[:, rq * 512:(rq + 1) * 512], self.LNV[:, 0:512], AF.Exp, ['LNV'], [('RSTD', rq)],
                     scale=-0.5)
            for c in range(8):
                gcol = self.VEC[:, DEPTH * VL + c: DEPTH * VL + c + 1] if final else self.vec(l, goff + c)
                if not final:
                    self.stt(self.ht(c, t0, 512), self.xt(c, t0, 512), gcol, self.RSTD[:, rq * 512:(rq + 1) * 512],
                             ALU.mult, ALU.mult, [('XT', (c, tt)), ('RSTD', rq), 'VEC'], [('H%d' % c, tt)])
                else:
                    gi = (tt % 2) * 4 + c // 2
                    o = self.gf(gi)[:, (c % 2) * 512:(c % 2) * 512 + 512]
                    self.stt(o, self.xt(c, t0, 512), gcol, self.RSTD[:, rq * 512:(rq + 1) * 512],
                             ALU.mult, ALU.mult, [('XT', (c, tt)), ('RSTD', rq), 'VEC'], [('G%d' % gi, c % 2)])
                    self.dma(self.outT[seq, c * 128:(c + 1) * 128, t0:t0 + 512], o, [('G%d' % gi, c % 2)], [])

    def proj_fm(self, wap, wkey, wcol0, M, src, nk, evac):
        for tt in range(4):
            b = self.nbank()
            for c in range(nk):
                rhs, rk = src(c, tt)
                self.mm(self.ps(b, 512, 0, M), wap(c, wcol0, M), rhs, c == 0, c == nk - 1, [wkey] + rk, ['PS%d' % b])
            evac(tt, b)

    def evac_eng(self):
        self.erot += 1
        return 'act' if self.erot % 2 else 'dve'

    def copy_scaled(self, out, in_, scale, r, w, eng=None):
        eng = eng or self.evac_eng()
        if eng == 'act':
            self.act(out, in_, AF.Copy, r, w, scale=float(scale))
        else:
            self.ts('dve', out, in_, float(scale), ALU.mult, r, w)

    def attn_run(self, jobs, hooks=None):
        pend = []
        deferred = []
        flat = []
        for job in jobs:
            blocks = job['blocks']
            if len(job['lanes']) == 2:
                steps = [[(0, kc), (1, kc)] for kc in blocks]
            else:
                steps = [[(0, kc) for kc in blocks[i:i + 2]] for i in range(0, len(blocks), 2)]
            job['started'] = set()
            for si, st in enumerate(steps):
                flat.append((job, st, si == len(steps) - 1))
        stepno = 0
        for (job, st, last) in flat:
            sp = self.srot
            self.srot = (self.srot + 1) % 2
            pp = self.prot
            self.prot = (self.prot + 1) % len(self.PT_PAIRS)
            qt = job['qt']
            q0 = qt * 512
            ramp = job['ramp']
            skeys = ['PS%d' % (2 * sp), 'PS%d' % (2 * sp + 1)]
            info = []
            fold_info = []
            for bi, (ln, kc) in enumerate(st):
                L = job['lanes'][ln]
                bk = 2 * sp + bi
                k0 = kc * 128
                if not job.get('fold'):
                    kap, kk = L['kT'](kc)
                    qap, qk = L['qT'](qt)
                a = k0 + 128 - q0
                b = k0 - q0
                segs = []
                if ramp is not None:
                    if a < 512:
                        segs.append((max(0, a), 512, 0))
                    if b > 0:
                        segs.append((0, min(512, b), 1))
                kw = {}
                if L['base'] == 96:
                    kw['tile_position'] = (96, 0)
                if job.get('fold'):
                    if b >= 512:
                        fs = [(0, 512, 'L')]
                    elif b <= -128:
                        fs = [(0, 512, 'R')]
                    else:
                        fs = []
                        if b > 0:
                            fs.append((0, b, 'L'))
                        fs.append((b, b + 128, 'D'))
                        if b + 128 < 512:
                            fs.append((b + 128, 512, 'R'))
                    fold_info.append((L, bk, kc, fs, bi))
                    continue
                self.mm(self.ps(bk), kap, qap, True, len(segs) == 0, kk + qk, [skeys[bi]], **kw)
                info.append((L, bk, k0, segs, kw, bi))
            if fold_info:
                for si in range(3):
                    for (L, bk, kc, fs, bi) in fold_info:
                        if si < len(fs):
                            lo, hi, var = fs[si]
                            kap, kk = L['kTv'](kc, var)
                            qap, qk = L['qTv'](q0 + lo, hi - lo, var)
                            self.mm(self.ps(bk, hi - lo, 0, 128, lo), kap, qap, True, True, kk + qk, [skeys[bi]],
                                    skip_group_check=True)
            for (L, bk, k0, segs, kw, bi) in info:
                base, K = L['base'], L['K']
                for si, (lo, hi, sgn) in enumerate(segs):
                    rk_tab = ramp[1] if sgn else ramp[0]
                    self.mm(self.ps(bk, hi - lo, 0, 128, lo), rk_tab[base:base + K, k0:k0 + 128],
                            ramp[2][base:base + K, q0 + lo:q0 + hi], False, si == len(segs) - 1,
                            list(ramp[3]), [skeys[bi]], **kw)
            nb = len(st)
            scales = [job['lanes'][ln]['scale'] for (ln, kc) in st]
            ptt = self.ptv(pp)
            if all(sc == scales[0] for sc in scales):
                self.act(ptt[:, 0: nb * 512], self.PS[:, 2 * sp * 512: 2 * sp * 512 + nb * 512],
                         AF.Exp, skeys[:nb], [self.ptk(pp)], scale=float(scales[0]))
            else:
                for bi in range(nb):
                    self.act(ptt[:, bi * 512: (bi + 1) * 512], self.ps(2 * sp + bi),
                             AF.Exp, [skeys[bi]], [self.ptk(pp)], scale=float(scales[bi]))
            if job.get('mult_both') is not None and nb == 2:
                for (tab, tk, eng) in job['mult_both'](st[0][1], qt):
                    v = ptt[:, 0:1024].rearrange("p (b c) -> p b c", b=2)
                    self.tt(eng, v, v, tab.unsqueeze(1).to_broadcast([128, 2, 512]), ALU.mult,
                            [self.ptk(pp)] + tk, [self.ptk(pp)])
            for bi, (ln, kc) in enumerate(st):
                L = job['lanes'][ln]
                if L['mult'] is not None:
                    for (c0, n, tab, tk, eng) in L['mult'](kc, qt):
                        v = ptt[:, bi * 512 + c0: bi * 512 + c0 + n]
                        self.tt(eng, v, v, tab, ALU.mult, [self.ptk(pp)] + tk, [self.ptk(pp)])
            def pv(job=job, st=st, pp=pp, last=last):
                ptt_ = self.ptv(pp)
                for bi, (ln, kc) in enumerate(st):
                    L = job['lanes'][ln]
                    acc = L['acc']
                    vap, vk = L['vT'](kc)
                    for qc in range(4):
                        stf = acc not in job['started']
                        job['started'].add(acc)
                        lhs = ptt_[:, bi * 512 + qc * 128: bi * 512 + qc * 128 + 128]
                        self.mm(self.ps(acc, 65, 0, 128, qc * 65), lhs, vap, stf, last,
                                [self.ptk(pp)] + vk, ['PS%d' % acc], skip_group_check=True)
                if last:
                    for d in (job['fin'](job) or []):
                        deferred.append([3, d])
            pend.append(pv)
            if len(pend) > self.pv_lag:
                pend.pop(0)()
            if hooks and stepno in hooks:
                hooks[stepno]()
            stepno += 1
            for dd in deferred:
                dd[0] -= 1
            while deferred and deferred[0][0] <= 0:
                deferred.pop(0)[1]()
        while pend:
            pend.pop(0)()
        for dd in deferred:
            dd[1]()

    def acc_bank(self):
        b = self.acc_pool[self.arot % len(self.acc_pool)]
        self.arot += 1
        return b

    def transposes_to_mix(self, obuf, okey, mixtile, h2, qt, bank):
        nc = self.nc
        bk = bank
        r0 = 64 * h2
        pv = self.psb(bk)
        for qc in range(4):
            o = pv[r0:r0 + 64, qc * 128:(qc + 1) * 128]
            i_ = obuf[:, qc * 64:(qc + 1) * 64]
            self.op('pe', lambda o=o, i_=i_: nc.tensor.transpose(o, i_, self.IDENT), [okey, 'C16'], ['PS%d' % bk])
        self.cp('dve', self.g(mixtile)[r0:r0 + 64, qt * 512:(qt + 1) * 512], pv[r0:r0 + 64, 0:512],
                ['PS%d' % bk], [('G%d' % mixtile, (h2, qt))])

    def out_proj(self, wlist, mixtiles):
        nfc = len(mixtiles)
        for tt in range(4):
            for oc in range(8):
                b = self.nbank()
                for fc in range(nfc):
                    wap, wk = wlist[fc // 2]
                    lhs = wap[:, (fc % 2) * 1024 + oc * 128:(fc % 2) * 1024 + oc * 128 + 128]
                    mt = mixtiles[fc]
                    self.mm(self.ps(b), lhs, self.g(mt)[:, tt * 512:(tt + 1) * 512], fc == 0, fc == nfc - 1,
                            [wk] + [('G%d' % mt, (h2, tt)) for h2 in range(2)], ['PS%d' % b])
                self.tt('dve', self.xt(oc, tt * 512, 512), self.ps(b), self.xt(oc, tt * 512, 512), ALU.add,
                        ['PS%d' % b, ('XT', (oc, tt))], [('XT', (oc, tt))])

    def mixer_ab(self, l, which, pre=None):
        nc = self.nc
        tr = self.tr
        isA = which == 'A'
        QT = [0, 1]
        KT = [2, 3]
        RT = [4, 5, 6]
        MIX = [8, 9]
        CBT = [10, 11]
        tr.retire(['G%d' % i for i in range(12)])
        if isA:
            self.set_pt([(7, 0), (7, 1024), (10, 0), (10, 1024)])
        else:
            self.set_pt([(7, 0), (7, 1024), (11, 1024)])
        rbase = C_RA if isA else C_RB
        for i in range(3):
            self.dma(self.g(RT[i]), self.c16[:, rbase + i * 2048: rbase + (i + 1) * 2048], [], ['G%d' % RT[i]])
        if not isA:
            self.dma(self.G[:, CBT[0] * 2048: CBT[0] * 2048 + CBW], self.c16[:, C_CB:C_CB + CBW], [], ['G%d' % CBT[0], 'G%d' % CBT[1]])
        slopes = SL_A if isA else SL_B
        qk_scale = (32.0 ** -0.5) if isA else (64.0 ** -0.5)

        def wfm(w):
            return lambda c, col0, M: w[:, (col0 // 128) * 1024 + c * 128: (col0 // 128) * 1024 + c * 128 + M]

        def src(c, tt):
            return self.ht(c, tt * 512, 512), [('H%d' % c, tt)]

        wq, wqk = self.wget()
        for i in range(2):
            def evq(tt, b, i=i):
                for hh in range(2):
                    hd = 2 * i + hh
                    self.copy_scaled(self.g(QT[i])[64 * hh:64 * hh + 64, tt * 512:(tt + 1) * 512],
                                     self.ps(b, 512, 64 * hh, 64 * hh + 64),
                                     qk_scale / (slopes[hd] if isA else slopes[2 * i]),
                                     ['PS%d' % b], [('G%d' % QT[i], (hh, tt))])
            self.proj_fm(wfm(wq), wqk, i * 128, 128, src, 8, evq)
        if pre is not None:
            pre()
        wk_, wkk = self.wget()
        for i in range(2):
            def evk(tt, b, i=i):
                self.copy_scaled(self.g(KT[i])[:, tt * 512:(tt + 1) * 512], self.ps(b), 1.0,
                                 ['PS%d' % b], [('G%d' % KT[i], tt)])
            self.proj_fm(wfm(wk_), wkk, i * 128, 128, src, 8, evk)
        wv, wvk = self.wget()
        for tc in range(16):
            b = self.nbank()
            for c in range(8):
                self.mm(self.ps(b, 256), self.ht(c, tc * 128, 128), wv[:, c * 256:(c + 1) * 256], c == 0, c == 7,
                        [wvk, ('H%d' % c, tc // 4)], ['PS%d' % b])
            ov = self.VT[:, tc * 260:(tc + 1) * 260].rearrange("p (h e) -> p h e", e=65)[:, :, 0:64]
            iv = self.ps(b, 256).rearrange("p (h e) -> p h e", e=64)
            eng = self.evac_eng()
            if eng == 'act':
                self.act(ov, iv, AF.Copy, ['PS%d' % b], [('VT', tc)])
            else:
                self.cp('dve', ov, iv, ['PS%d' % b], [('VT', tc)])

        ramp = (self.g(RT[0]), self.g(RT[1]), self.g(RT[2]), 'G%d' % RT[0])
        rkeys = ['G%d' % t for t in RT]
        jobs = []
        lam_init = 0.8 - 0.6 * math.exp(-0.3 * l)

        def mk_lane(hd, base, K, acc):
            ti = hd // 2
            sidx = (SL_IDX_A if isA else SL_IDX_B)[hd]

            def kT(kc):
                return self.g(KT[ti])[base:base + K, kc * 128:(kc + 1) * 128], [('G%d' % KT[ti], kc // 4)]

            def qT(qt_):
                return (self.g(QT[ti])[base:base + K, qt_ * 512:(qt_ + 1) * 512],
                        [('G%d' % QT[ti], ((base // 64), qt_))])

            def vT(kc):
                return self.VT[:, (kc * 4 + hd) * 65:(kc * 4 + hd) * 65 + 65], [('VT', kc)]

            def mult(kc, qt_):
                res = []
                dlt = qt_ * 512 - kc * 128
                b_ = -dlt
                if 0 <= b_ < 512:
                    res.append((b_, 128, self.C16[:, C_ED + sidx * 128: C_ED + (sidx + 1) * 128], ['C16'], 'pool'))
                return res
            return dict(K=K, base=base, kT=kT, qT=qT, vT=vT, acc=acc,
                        scale=(slopes[hd] if isA else slopes[2 * (hd // 2)]), mult=mult)

        def mult_both(kc, qt_):
            dlt = qt_ * 512 - kc * 128
            return [(self.G[:, CBT[0] * 2048 + dlt + OFFB: CBT[0] * 2048 + dlt + OFFB + 512],
                     ['G%d' % CBT[0], 'G%d' % CBT[1]], 'dve')]

        if isA:
            for hd in range(4):
                for qt in range(4):
                    accs = [self.acc_bank(), self.acc_bank()]
                    lanes = [mk_lane(hd, 64 * (hd % 2) + 32 * mp, 32, accs[mp]) for mp in range(2)]

                    def fin(job, hd=hd, qt=qt, accs=accs):
                        return [self.fin_ab(l, True, hd, qt, accs, MIX[hd // 2], lam_init)]
                    jobs.append(dict(lanes=lanes, qt=qt, blocks=list(range(16)),
                                     ramp=(ramp[0], ramp[1], ramp[2], rkeys), fin=fin))
        else:
            for hp in range(2):
                for qt in range(4):
                    accs = [self.acc_bank(), self.acc_bank()]
                    lanes = [mk_lane(2 * hp + hh, 64 * hh, 64, accs[hh]) for hh in range(2)]
                    blocks = [kc for kc in range(16) if -1408 <= qt * 512 - kc * 128 <= 1024]

                    def fin(job, hp=hp, qt=qt, accs=accs):
                        return [self.fin_ab(l, False, 2 * hp + hh, qt, [accs[hh]], MIX[hp], 0.0) for hh in range(2)]
                    jobs.append(dict(lanes=lanes, qt=qt, blocks=blocks, mult_both=mult_both,
                                     ramp=(ramp[0], ramp[1], ramp[2], rkeys), fin=fin))
        self.ramp_keys = rkeys
        self.attn_run(jobs)
        wo, wok = self.wget()
        self.out_proj([(wo, wok)], MIX)

    def mixer_a2(self, l):
        tr = self.tr
        QA = [0, 1]
        KR = [2, 3]
        KL = [4, 5]
        MIX = [8, 9]
        tr.retire(['G%d' % i for i in range(12)])
        self.set_pt([(7, 0), (7, 1024), (10, 0), (10, 1024)])
        lam_init = 0.8 - 0.6 * math.exp(-0.3 * l)
        qk_scale = 32.0 ** -0.5
        def src(c, tt):
            return self.ht(c, tt * 512, 512), [('H%d' % c, tt)]
        wv, wvk = self.wget()
        for tc in range(16):
            b = self.nbank()
            for c in range(8):
                self.mm(self.ps(b, 256), self.ht(c, tc * 128, 128), wv[:, c * 256:(c + 1) * 256], c == 0, c == 7,
                        [wvk, ('H%d' % c, tc // 4)], ['PS%d' % b])
            ov = self.VT[:, tc * 260:(tc + 1) * 260].rearrange("p (h e) -> p h e", e=65)[:, :, 0:64]
            iv = self.ps(b, 256).rearrange("p (h e) -> p h e", e=64)
            if self.evac_eng() == 'act':
                self.act(ov, iv, AF.Copy, ['PS%d' % b], [('VT', tc)])
            else:
                self.cp('dve', ov, iv, ['PS%d' % b], [('VT', tc)])
        for hp in range(2):
            for hh in range(2):
                for (tiles, coff) in ((QA, C_RA + 2 * S), (KR, C_RA), (KL, C_RA + S)):
                    for base in (0, 64):
                        self.dma(self.g(tiles[hh])[base + 32: base + 64, :], self.c16[0:32, coff:coff + S], [],
                                 [('G%d' % tiles[hh], ('aug', base))])
            wq, wqk = self.wget()
            for hh in range(2):
                hd = 2 * hp + hh
                for tt in range(4):
                    b = self.nbank()
                    for c in range(8):
                        rhs, rk = src(c, tt)
                        self.mm(self.ps(b), wq[:, hh * 1024 + c * 128: hh * 1024 + c * 128 + 128], rhs, c == 0, c == 7,
                                [wqk] + rk, ['PS%d' % b])
                    for base in (0, 64):
                        self.copy_scaled(self.g(QA[hh])[base:base + 32, tt * 512:(tt + 1) * 512],
                                         self.ps(b, 512, base, base + 32), qk_scale / SL_A[hd],
                                         ['PS%d' % b], [('G%d' % QA[hh], ('f', base, tt))], eng='dve')
            wk_, wkk = self.wget()
            for hh in range(2):
                for tt in range(4):
                    b = self.nbank()
                    for c in range(8):
                        rhs, rk = src(c, tt)
                        self.mm(self.ps(b), wk_[:, hh * 1024 + c * 128: hh * 1024 + c * 128 + 128], rhs, c == 0, c == 7,
                                [wkk] + rk, ['PS%d' % b])
                    for base in (0, 64):
                        for tiles in (KR, KL):
                            self.copy_scaled(self.g(tiles[hh])[base:base + 32, tt * 512:(tt + 1) * 512],
                                             self.ps(b, 512, base, base + 32), 1.0,
                                             ['PS%d' % b], [('G%d' % tiles[hh], ('f', base, tt))], eng='dve')
            jobs = []
            for hh in range(2):
                hd = 2 * hp + hh
                sidx = SL_IDX_A[hd]
                for qt in range(4):
                    accs = [self.acc_bank(), self.acc_bank()]
                    lanes = []
                    for mp in range(2):
                        base = 64 * mp

                        def kTv(kc, var, hh=hh, base=base):
                            t = (KL if var == 'L' else KR)[hh]
                            K = 32 if var == 'D' else 64
                            keys = [('G%d' % t, ('f', base, kc // 4))]
                            if var != 'D':
                                keys.append(('G%d' % t, ('aug', base)))
                            return self.g(t)[base:base + K, kc * 128:(kc + 1) * 128], keys

                        def qTv(c0, n, var, hh=hh, base=base):
                            K = 32 if var == 'D' else 64
                            keys = [('G%d' % QA[hh], ('f', base, tq)) for tq in range(c0 // 512, (c0 + n - 1) // 512 + 1)]
                            if var != 'D':
                                keys.append(('G%d' % QA[hh], ('aug', base)))
                            return self.g(QA[hh])[base:base + K, c0:c0 + n], keys

                        def vT(kc, hd=hd):
                            return self.VT[:, (kc * 4 + hd) * 65:(kc * 4 + hd) * 65 + 65], [('VT', kc)]

                        def mult(kc, qt_, sidx=sidx):
                            b_ = kc * 128 - qt_ * 512
                            if 0 <= b_ < 512:
                                return [(b_, 128, self.C16[:, C_ED + sidx * 128: C_ED + (sidx + 1) * 128], ['C16'], 'pool')]
                            return []
                        lanes.append(dict(K=64, base=base, kTv=kTv, qTv=qTv, vT=vT, acc=accs[mp], scale=SL_A[hd],
                                          mult=mult))

                    def fin(job, hd=hd, qt=qt, accs=accs, hp=hp):
                        return [self.fin_ab(l, True, hd, qt, accs, MIX[hp], lam_init)]
                    jobs.append(dict(lanes=lanes, qt=qt, blocks=list(range(16)), ramp=None, fold=True, fin=fin))
            self.attn_run(jobs)

        def finish():
            wo, wok = self.wget()
            self.out_proj([(wo, wok)], MIX)
        return finish

    def fin_ab(self, l, isA, hd, qt, accs, mixtile, lam_init):
        nc = self.nc
        ob = self.OB[:, (self.obrot % 2) * 256:(self.obrot % 2) * 256 + 256]
        okey = ('OB', self.obrot % 2)
        self.obrot += 1
        ob3 = ob.rearrange("p (q e) -> p q e", e=64)

        def acc3(b):
            return self.PS[:, b * 512: b * 512 + 260].rearrange("p (q e) -> p q e", e=65)
        a0 = acc3(accs[0])
        R1 = self.SM[:, 0:4]
        self.op('dve', lambda: nc.vector.reciprocal(out=R1, in_=a0[:, :, 64]), ['PS%d' % accs[0]], [('SM', 0)])
        r1b = R1.unsqueeze(2).to_broadcast([128, 4, 64])
        if not isA:
            self.tt('dve', ob3, a0[:, :, 0:64], r1b, ALU.mult, ['PS%d' % accs[0], ('SM', 0)], [okey])
        else:
            a1 = acc3(accs[1])
            O1 = self.OF[:, 0:256].rearrange("p (q e) -> p q e", e=64)
            O2 = self.OF[:, 256:512].rearrange("p (q e) -> p q e", e=64)
            self.tt('dve', O1, a0[:, :, 0:64], r1b, ALU.mult, ['PS%d' % accs[0], ('SM', 0)], [('OF', 0)])
            R2 = self.SM[:, 4:8]
            self.op('dve', lambda: nc.vector.reciprocal(out=R2, in_=a1[:, :, 64]), ['PS%d' % accs[1]], [('SM', 1)])
            R2L = self.SM[:, 8:12]
            self.ts('dve', R2L, R2, self.LAMS[:, l:l + 1], ALU.mult, [('SM', 1), 'LAMS'], [('SM', 2)])
            r2b = R2L.unsqueeze(2).to_broadcast([128, 4, 64])
            self.tt('dve', O2, a1[:, :, 0:64], r2b, ALU.mult, ['PS%d' % accs[1], ('SM', 2)], [('OF', 1)])
            self.tt('dve', O1, O1, O2, ALU.add, [('OF', 0), ('OF', 1)], [('OF', 0)])
            self.tt('dve', O2, O1, O1, ALU.mult, [('OF', 0)], [('OF', 1)])
            SSQ = self.SM[:, 12:16]
            self.op('dve', lambda: nc.vector.tensor_reduce(out=SSQ, in_=O2, axis=AX.X, op=ALU.add),
                    [('OF', 1)], [('SM', 3)])
            LN_ = self.SM[:, 16:20]
            self.act(LN_, SSQ, AF.Ln, [('SM', 3), 'EPS'], [('SM', 4)], scale=1.0 / 64, bias=self.EPSV[:, 0:1])
            RS = self.SM[:, 20:24]
            self.act(RS, LN_, AF.Exp, [('SM', 4)], [('SM', 5)], scale=-0.5)
            rsb = RS.unsqueeze(2).to_broadcast([128, 4, 64])
            self.tt('dve', O1, O1, rsb, ALU.mult, [('OF', 0), ('SM', 5)], [('OF', 0)])
            sub = self.vec(l, 195, 64).unsqueeze(1).to_broadcast([128, 4, 64])
            self.stt(ob3, O1, float(1.0 - lam_init), sub, ALU.mult, ALU.mult, [('OF', 0), 'VEC'], [okey])

        def deferred():
            self.transposes_to_mix(ob, okey, mixtile, hd % 2, qt, accs[0])
        return deferred

    def mixer_c(self, l):
        nc = self.nc
        tr = self.tr
        CQN = [0, 1]
        CKVN = 2
        KROPE = 3
        MIX = [4, 5, 6, 7]
        ROPE = [9, 10]
        tr.retire(['G%d' % i for i in range(12)])
        self.set_pt([(8, 0), (8, 1024), (11, 0), (11, 1024)])
        wcq, wcqk = self.wget()
        wckv, wckvk = self.wget()
        for i in range(2):
            self.dma(self.g(ROPE[i]), self.c16[:, C_ROPE + i * 2048: C_ROPE + (i + 1) * 2048], [], ['G%d' % ROPE[i]])
        cosT = self.g(ROPE[0])
        sinT = self.g(ROPE[1])
        for tt in range(4):
            t0 = tt * 512
            bq = [self.nbank(), self.nbank()]
            bkv = self.nbank()
            bk1 = self.nbank()
            bk2 = self.nbank()
            for i in range(2):
                for c in range(8):
                    self.mm(self.ps(bq[i]), wcq[:, i * 1024 + c * 128: i * 1024 + c * 128 + 128], self.ht(c, t0, 512),
                            c == 0, c == 7, [wcqk, ('H%d' % c, tt)], ['PS%d' % bq[i]])
            for c in range(8):
                self.mm(self.ps(bkv), wckv[:, c * 128:(c + 1) * 128], self.ht(c, t0, 512), c == 0, c == 7,
                        [wckvk, ('H%d' % c, tt)], ['PS%d' % bkv])
            for c in range(8):
                self.mm(self.ps(bk1, 512, 64, 96), wckv[:, 1024 + c * 32: 1024 + (c + 1) * 32], self.ht(c, t0, 512),
                        c == 0, c == 7, [wckvk, ('H%d' % c, tt)], ['PS%d' % bk1])
            for c in range(8):
                self.mm(self.ps(bk2, 512, 64, 96), wckv[:, 1280 + c * 32: 1280 + (c + 1) * 32], self.ht(c, t0, 512),
                        c == 0, c == 7, [wckvk, ('H%d' % c, tt)], ['PS%d' % bk2])
            for (banks, nfeat, goff, dst) in ((bq, 256, 16, CQN), ([bkv], 128, 18, [CKVN])):
                sb = self.nbank()
                for i, bnk in enumerate(banks):
                    q = self.sqrot % 3
                    self.sqrot += 1
                    self.act(self.SQ[:, q * 512:(q + 1) * 512], self.ps(bnk), AF.Square, ['PS%d' % bnk], [('SQ', q)])
                    self.mm(self.ps(sb), self.ONES, self.SQ[:, q * 512:(q + 1) * 512], i == 0, i == len(banks) - 1,
                            [('SQ', q), 'C16'], ['PS%d' % sb])
                self.act(self.LNV[:, 0:512], self.ps(sb), AF.Ln, ['PS%d' % sb, 'EPS'], ['LNV'],
                         scale=1.0 / nfeat, bias=self.EPSV[:, 0:1])
                rq = 0
                self.rsrot += 1
                self.act(self.RSTD[:, rq * 512:(rq + 1) * 512], self.LNV[:, 0:512], AF.Exp, ['LNV'], [('RSTD', rq)],
                         scale=-0.5)
                for i, bnk in enumerate(banks):
                    self.stt(self.g(dst[i])[:, t0:t0 + 512], self.ps(bnk), self.vec(l, goff + i),
                             self.RSTD[:, rq * 512:(rq + 1) * 512], ALU.mult, ALU.mult,
                             ['PS%d' % bnk, ('RSTD', rq), 'VEC'], [('G%d' % dst[i], tt)])
            T1 = self.RT1[64:96, 0:512]
            T2 = self.RT1[64:96, 512:1024]
            self.tt('dve', T1, self.ps(bk1, 512, 64, 96), cosT[64:96, t0:t0 + 512], ALU.mult,
                    ['PS%d' % bk1, 'G%d' % ROPE[0]], [('RT1', 0)])
            self.tt('dve', T2, self.ps(bk2, 512, 64, 96), sinT[64:96, t0:t0 + 512], ALU.mult,
                    ['PS%d' % bk2, 'G%d' % ROPE[1]], [('RT1', 1)])
            self.tt('dve', self.g(KROPE)[64:96, t0:t0 + 512], T1, T2, ALU.add, [('RT1', 0), ('RT1', 1)],
                    [('G%d' % KROPE, tt)])
        tr.retire(['H%d' % i for i in range(8)])
        wuq, wuqk = self.wget()
        wukv, wukvk = self.wget()
        QC = [0, 1]
        KC = [2, 3]
        VC = 4
        nc_ = self.nc
        self.op('pool', lambda: nc_.gpsimd.memset(self.h(VC)[:, 0:1040], 1.0), [], ['H%d' % VC])
        self.op('pool', lambda: nc_.gpsimd.memset(self.h(VC + 1)[:, 0:1040], 1.0), [], ['H%d' % (VC + 1)])
        qscale = 96.0 ** -0.5
        self.acc_pool = [4, 5]
        self.bank_pool = [6, 7]

        def proj(hd):
            pb = hd % 2
            qc_t, kc_t = QC[pb], KC[pb]
            vct = VC + pb
            units = []

            def uq(tt):
                t0 = tt * 512
                ba = self.nbank()
                for c in range(2):
                    self.mm(self.ps(ba, 512, 0, 96), wuq[:, c * 768 + hd * 96: c * 768 + hd * 96 + 96],
                            self.g(CQN[c])[:, t0:t0 + 512], c == 0, c == 1, [wuqk, ('G%d' % CQN[c], tt)], ['PS%d' % ba])
                self.ts('dve', self.h(qc_t)[0:64, t0:t0 + 512], self.ps(ba, 512, 0, 64), qscale, ALU.mult,
                        ['PS%d' % ba], [('H%d' % qc_t, ('n', tt))])
                T1 = self.RT1[64:96, 0:512]
                T2 = self.RT1[64:96, 512:1024]
                self.stt(T1, self.ps(ba, 512, 64, 96), qscale, cosT[64:96, t0:t0 + 512], ALU.mult, ALU.mult,
                         ['PS%d' % ba, 'G%d' % ROPE[0]], [('RT1', 0)])
                bb = self.nbank()
                for c in range(2):
                    self.mm(self.ps(bb, 512, 64, 96), wuq[:, 1536 + c * 256 + hd * 32: 1536 + c * 256 + hd * 32 + 32],
                            self.g(CQN[c])[:, t0:t0 + 512], c == 0, c == 1, [wuqk, ('G%d' % CQN[c], tt)], ['PS%d' % bb])
                self.stt(T2, self.ps(bb, 512, 64, 96), qscale, sinT[64:96, t0:t0 + 512], ALU.mult, ALU.mult,
                         ['PS%d' % bb, 'G%d' % ROPE[1]], [('RT1', 1)])
                self.tt('pool', self.h(qc_t)[64:96, t0:t0 + 512], T1, T2, ALU.add, [('RT1', 0), ('RT1', 1)],
                        [('H%d' % qc_t, ('r', tt))])

            def uk(tt):
                t0 = tt * 512
                b = self.nbank()
                self.mm(self.ps(b, 512, 0, 64), wukv[:, hd * 128: hd * 128 + 64], self.g(CKVN)[:, t0:t0 + 512],
                        True, True, [wukvk, ('G%d' % CKVN, tt)], ['PS%d' % b])
                self.cp('dve', self.h(kc_t)[0:64, t0:t0 + 512], self.ps(b, 512, 0, 64), ['PS%d' % b],
                        [('H%d' % kc_t, ('n', tt))])
                self.cp('pool', self.h(kc_t)[64:96, t0:t0 + 512], self.g(KROPE)[64:96, t0:t0 + 512],
                        [('G%d' % KROPE, tt)], [('H%d' % kc_t, ('r', tt))])

            def uv(half):
                b = self.nbank()
                for t8 in range(8):
                    tc = half * 8 + t8
                    self.mm(self.ps(b, 64, 0, 128, t8 * 64), self.g(CKVN)[:, tc * 128:(tc + 1) * 128],
                            wukv[:, hd * 128 + 64: hd * 128 + 128], True, True,
                            [wukvk, ('G%d' % CKVN, tc // 4)], ['PS%d' % b], skip_group_check=True)
                ov = self.h(vct)[:, half * 520: half * 520 + 520].rearrange("p (t e) -> p t e", e=65)[:, :, 0:64]
                iv = self.ps(b).rearrange("p (t e) -> p t e", e=64)
                self.cp('dve', ov, iv, ['PS%d' % b, 'H%d' % vct], [('H%d' % vct, (pb, half))])

            for tt in range(4):
                units.append(lambda tt=tt: uq(tt))
            for tt in range(4):
                units.append(lambda tt=tt: uk(tt))
            for half in range(2):
                units.append(lambda half=half: uv(half))
            return units

        for u in proj(0):
            u()
        for hd in range(8):
            pb = hd % 2
            qc_t, kc_t = QC[pb], KC[pb]
            vct = VC + pb
            jobs = []
            for qt in range(4):
                acc = self.acc_bank()

                def kT(kc, kc_t=kc_t):
                    return (self.h(kc_t)[0:96, kc * 128:(kc + 1) * 128],
                            [('H%d' % kc_t, ('n', kc // 4)), ('H%d' % kc_t, ('r', kc // 4))])

                def qT(qt_, qc_t=qc_t):
                    return (self.h(qc_t)[0:96, qt_ * 512:(qt_ + 1) * 512],
                            [('H%d' % qc_t, ('n', qt_)), ('H%d' % qc_t, ('r', qt_))])

                def vT(kc, pb=pb, vct=vct):
                    return self.h(vct)[:, kc * 65: kc * 65 + 65], [('H%d' % vct, (pb, kc // 8)), 'H%d' % vct]

                def fin(job, hd=hd, qt=qt, acc=acc):
                    return [self.fin_ab(l, False, hd, qt, [acc], MIX[hd // 2], 0.0)]
                jobs.append(dict(lanes=[dict(K=96, base=0, kT=kT, qT=qT, vT=vT, acc=acc, scale=1.0, mult=None)],
                                 qt=qt, blocks=list(range(16)), ramp=None, fin=fin))
            hooks = {}
            if hd < 7:
                for ui, u in enumerate(proj(hd + 1)):
                    hooks[2 + 3 * ui] = u
            self.attn_run(jobs, hooks)
        self.acc_pool = [4, 5, 6, 7]
        self.bank_pool = list(range(8))
        wo1, wo1k = self.wget()
        wo2, wo2k = self.wget()
        self.out_proj([(wo1, wo1k), (wo2, wo2k)], MIX)

    def ffn(self, l):
        nc = self.nc
        tr = self.tr
        tr.retire(['G%d' % i for i in range(12)])
        self.cast_eng = 'act'
        ACT_T = 0
        CEN = [6, 7, 8, 9]
        SG = [10, 11]
        def part1(tt2, gi, jj):
            t0 = tt2 * 1024
            j = GROUPS[gi][jj]
            wu, wuk = self.wget()
            cbuf = jj % 2
            for gv in range(2):
                ch = j + 22 * gv
                bm = 2 * gv
                bh = 4 + gv
                for half in range(2):
                    for c in range(8):
                        self.mm(self.ps(bm + half), wu[:, gv * 1024 + c * 128: gv * 1024 + c * 128 + 128],
                                self.ht(c, t0 + half * 512, 512), c == 0, c == 7,
                                [wuk, ('H%d' % c, tt2 * 2 + half)], ['PS%d' % (bm + half)])
                tcol = t0 - 1 if tt2 == 1 else t0 + 1024
                for c in range(8):
                    self.mm(self.ps(bh, 2), wu[:, gv * 1024 + c * 128: gv * 1024 + c * 128 + 128],
                            self.ht(c, tcol - (1 if tt2 == 0 else 0), 2),
                            c == 0, c == 7, [wuk] + self.hk(c, tcol - (1 if tt2 == 0 else 0), 2), ['PS%d' % bh])
                hcol = 1 if tt2 == 0 else 0
                ct = CEN[cbuf * 2 + gv]
                cen = self.gf(ct)
                mainp = self.PS[:, bm * 512: bm * 512 + 1024]
                mkeys = ['PS%d' % bm, 'PS%d' % (bm + 1)]
                self.act(cen, mainp, AF.Identity, mkeys + ['VEC'], ['G%d' % ct],
                         scale=self.vec(l, 63 + ch), bias=self.vec(l, 151 + ch))
                self.stt(cen[:, 1:1024], mainp[:, 0:1023], self.vec(l, 19 + ch), cen[:, 1:1024], ALU.mult, ALU.add,
                         mkeys + ['VEC', 'G%d' % ct], ['G%d' % ct])
                self.stt(cen[:, 0:1023], mainp[:, 1:1024], self.vec(l, 107 + ch), cen[:, 0:1023], ALU.mult, ALU.add,
                         mkeys + ['VEC', 'G%d' % ct], ['G%d' % ct])
                if tt2 == 1:
                    self.stt(cen[:, 0:1], self.ps(bh, 1, 0, 128, hcol), self.vec(l, 19 + ch), cen[:, 0:1],
                             ALU.mult, ALU.add, ['PS%d' % bh, 'VEC', 'G%d' % ct], ['G%d' % ct])
                else:
                    self.stt(cen[:, 1023:1024], self.ps(bh, 1, 0, 128, hcol), self.vec(l, 107 + ch),
                             cen[:, 1023:1024], ALU.mult, ALU.add, ['PS%d' % bh, 'VEC', 'G%d' % ct], ['G%d' % ct])

        def part2(jj):
            cbuf = jj % 2
            cg = self.gf(CEN[cbuf * 2 + 0])
            cv = self.gf(CEN[cbuf * 2 + 1])
            sg = self.gf(SG[cbuf])
            self.act(sg, cg, AF.Silu, ['G%d' % CEN[cbuf * 2]], ['G%d' % SG[cbuf]])
            at = self.G[:, jj * 1024:(jj + 1) * 1024]
            self.tt('pool', at, cv, sg, ALU.mult, ['G%d' % CEN[cbuf * 2 + 1], 'G%d' % SG[cbuf]],
                    [('G%d' % (jj // 2), ('a', jj % 2))])

        def down(tt2, gi):
            n = len(GROUPS[gi])
            for oc in range(8):
                wd, wdk = self.wget()
                for half in range(2):
                    b = 6 + (oc * 2 + half) % 2
                    for jj in range(n):
                        self.mm(self.ps(b), wd[:, jj * 128:(jj + 1) * 128],
                                self.G[:, jj * 1024 + half * 512: jj * 1024 + half * 512 + 512],
                                jj == 0, jj == n - 1, [wdk, ('G%d' % (jj // 2), ('a', jj % 2))], ['PS%d' % b])
                    tt = tt2 * 2 + half
                    self.tt('dve', self.xt(oc, tt * 512, 512), self.ps(b), self.xt(oc, tt * 512, 512), ALU.add,
                            ['PS%d' % b, ('XT', (oc, tt))], [('XT', (oc, tt))])

        groups = [(tt2, gi) for tt2 in range(2) for gi in range(2)]
        pre = False
        for k, (tt2, gi) in enumerate(groups):
            for jj in range(len(GROUPS[gi])):
                if not (jj == 0 and pre):
                    part1(tt2, gi, jj)
                part2(jj)
            pre = False
            if k + 1 < len(groups):
                part1(groups[k + 1][0], groups[k + 1][1], 0)
                pre = True
            down(tt2, gi)

    def build(self):
        depth, nseq = self.depth, self.nseq
        nc = bass.Bass("TRN2", target_bir_lowering=False)
        self.nc = nc
        self.xTd = nc.dram_tensor("xT", [nseq, D, S], F32, kind="ExternalInput").ap()
        self.wst = nc.dram_tensor("wst", [depth * SLOTS_PER_LAYER, 128, 2048], F32, kind="ExternalInput").ap()
        self.vecd = nc.dram_tensor("vecs", [128, NV], F32, kind="ExternalInput").ap()
        self.c16 = nc.dram_tensor("c16", [128, NC16], BF16, kind="ExternalInput").ap()
        self.outT = nc.dram_tensor("outT", [nseq, D, S], F32, kind="ExternalOutput").ap()
        sched = []
        for b in range(nseq):
            for l in range(depth):
                base = l * SLOTS_PER_LAYER
                sizes = [2048, 2048, 2048, 2048, 2048, 2048, 2048, 2048, 2048, 2048, 2048, 1536, 2048, 1024, 2048, 2048]
                for i in [0, 1, 2, 3, 4, 6, 5] + list(range(7, 16)):
                    sched.append((base + i, sizes[i]))
                groups = [(tt2, gi) for tt2 in range(2) for gi in range(2)]
                for k, (tt2, gi) in enumerate(groups):
                    n = len(GROUPS[gi])
                    ubase = base + 16 + (20 if gi == 1 else 0)
                    for jj in range(n):
                        if not (jj == 0 and k > 0):
                            sched.append((ubase + jj, 2048))
                    if k + 1 < len(groups):
                        sched.append((base + 16 + (20 if groups[k + 1][1] == 1 else 0), 2048))
                    for oc in range(8):
                        sched.append((ubase + n + oc, n * 128))
        self.ws_init(sched)
        self.obrot = 0
        self.sqrot = 0
        self.rsrot = 0
        from contextlib import ExitStack
        with ExitStack() as es:
            def sb(name, shape, dt):
                return es.enter_context(nc.sbuf_tensor(name, shape, dt))
            XT = sb("XT", [128, 8 * S], F32)
            HT = sb("HT", [128, 8 * S], BF16)
            G = sb("G", [128, 12 * 2048], BF16)
            VT = sb("VT", [128, 16 * 4 * 65], BF16)
            WST = sb("WST", [128, 2 * 2048], F32)
            WR = sb("WR", [128, R_RING * 2048], BF16)
            VEC = sb("VEC", [128, NV], F32)
            C16 = sb("C16", [128, C_ROPE], BF16)
            EPSV = sb("EPSV", [128, 2], F32)
            LAMS = sb("LAMS", [128, 4], F32)
            LT = sb("LT", [128, 64], F32)
            SQ = sb("SQ", [128, 3 * 512], BF16)
            LNV = sb("LNV", [128, 512], F32)
            RSTD = sb("RSTD", [128, 512], F32)
            RT1 = sb("RT1", [128, 1024], F32)
            OF = sb("OF", [128, 512], F32)
            OB = sb("OB", [128, 512], BF16)
            SM = sb("SM", [128, 32], F32)
            PS = es.enter_context(nc.psum_tensor("PS", [128, 8 * 512], F32))
            s_pe = es.enter_context(nc.semaphore("s_pe"))
            s_act = es.enter_context(nc.semaphore("s_act"))
            s_dve = es.enter_context(nc.semaphore("s_dve"))
            s_pool = es.enter_context(nc.semaphore("s_pool"))
            dsl = [es.enter_context(nc.semaphore("d%d" % i)) for i in range(8)]
            block = es.enter_context(nc.Block())
            self.XT, self.HT, self.G, self.VT, self.WST, self.WR, self.VEC, self.C16 = XT, HT, G, VT, WST, WR, VEC, C16
            self.EPSV, self.LAMS, self.SQ, self.LNV, self.RSTD, self.RT1, self.OF, self.OB, self.SM, self.PS = \
                EPSV, LAMS, SQ, LNV, RSTD, RT1, OF, OB, SM, PS
            self.IDENT = C16[:, C_ID:C_ID + 128]
            self.ONES = C16[:, C_ONES:C_ONES + 128]

            self.dma(VEC[:, :], self.vecd[:, :], [], ['VEC'])
            self.dma(C16[:, :], self.c16[:, 0:C_ROPE], [], ['C16'])
            self.op('pool', lambda: nc.gpsimd.memset(EPSV[:, :], EPS), [], ['EPS'])
            self.op('pool', lambda: nc.gpsimd.memset(VT[:, :], 1.0), [], ['VT'])
            self.tr.retire(['VT'])
            for l in range(depth):
                lam_init = 0.8 - 0.6 * math.exp(-0.3 * l)
                for k in range(2):
                    a = self.vec(l, 259 + 64 * k, 32)
                    b_ = self.vec(l, 291 + 64 * k, 32)
                    self.tt('dve', LT[:, 0:32], a, b_, ALU.mult, ['VEC'], [('LT', 0)])
                    self.op('dve', lambda k=k: nc.vector.tensor_reduce(out=LT[:, 32 + k:33 + k], in_=LT[:, 0:32],
                                                                       axis=AX.X, op=ALU.add), [('LT', 0)], [('LT', 1 + k)])
                    self.act(LT[:, 34 + k:35 + k], LT[:, 32 + k:33 + k], AF.Exp, [('LT', 1 + k)], [('LT', 3 + k)])
                self.stt(LAMS[:, l:l + 1], LT[:, 35:36], float(-lam_init), LT[:, 34:35], ALU.add, ALU.subtract,
                         [('LT', 3), ('LT', 4)], ['LAMS'])

            for b in range(nseq):
                for tt in range(4):
                    for c in range(8):
                        self.dma(XT[:, c * S + tt * 512: c * S + (tt + 1) * 512],
                                 self.xTd[b, c * 128:(c + 1) * 128, tt * 512:(tt + 1) * 512], [], [('XT', (c, tt))])
                for l in range(depth):
                    self.tr.retire(['H%d' % i for i in range(8)])
                    self.cast_eng = 'dve'
                    self.rmsnorm_x(l, 0)
                    fa = self.mixer_a2(l)
                    self.mixer_ab(l, 'B', pre=fa)
                    self.mixer_c(l)
                    self.tr.retire(['H%d' % i for i in range(8)])
                    self.rmsnorm_x(l, 8)
                    self.ffn(l)
                self.tr.retire(['G%d' % i for i in range(12)])
                self.rmsnorm_x(0, 0, final=True, seq=b)

            @block.sync
            def _(sync):
                self.tr.emit(nc, {'pe': s_pe, 'act': s_act, 'dve': s_dve, 'pool': s_pool},
                             dsl)
        return nc


def _bf16(a):
    return np.asarray(a, dtype=np.float32).astype(ml_dtypes.bfloat16)


def make_consts():
    c = np.zeros((128, NC16), dtype=np.float32)
    c[:, C_ID:C_ID + 128] = np.eye(128, dtype=np.float32)
    c[:, C_ONES:C_ONES + 128] = 1.0
    ki = np.arange(128)[:, None]
    qi = np.arange(128)[None, :]
    for s in range(8):
        slope = 2.0 ** -(s + 1)
        c[:, C_ED + s * 128: C_ED + (s + 1) * 128] = np.exp(-slope * np.abs(qi - ki))
    pos = np.arange(S, dtype=np.float32)
    inv_freq = (10000.0 ** (-np.arange(0, 32, 2, dtype=np.float32) / 32)).astype(np.float32)
    ang = (pos[:, None] * inv_freq[None, :]).astype(np.float32)
    cos = np.cos(ang).astype(np.float32).T
    sin = np.sin(ang).astype(np.float32).T
    c[64:80, C_ROPE:C_ROPE + S] = cos
    c[80:96, C_ROPE:C_ROPE + S] = cos
    c[64:80, C_ROPE + S:C_ROPE + 2 * S] = -sin
    c[80:96, C_ROPE + S:C_ROPE + 2 * S] = sin
    p = np.arange(S)
    hi = (64 * (p // 64)).astype(np.float32)
    lo = (p % 64).astype(np.float32)
    for (off, bases) in ((C_RA, (0, 32, 64, 96)), (C_RB, (0, 64))):
        for b in bases:
            c[b + 0, off:off + S] = hi
            c[b + 1, off:off + S] = lo
            c[b + 2, off:off + S] = -1.0
            c[b + 3, off:off + S] = -1.0
            if off == C_RB and b == 64:
                c[b:b + 4, off:off + S] *= 0.25
            c[b:b + 4, off + S:off + 2 * S] = -c[b:b + 4, off:off + S]
            c[b + 0, off + 2 * S:off + 3 * S] = 1.0
            c[b + 1, off + 2 * S:off + 3 * S] = 1.0
            c[b + 2, off + 2 * S:off + 3 * S] = hi
            c[b + 3, off + 2 * S:off + 3 * S] = lo
    m = np.arange(CBW)[None, :]
    d = m - np.arange(128)[:, None] - OFFB
    ad = np.abs(d)
    cb = (ad <= 64).astype(np.float32) + ((d % 4 == 0) & (ad <= 256)) + ((d % 16 == 0) & (ad <= 1024))
    c[:, C_CB:C_CB + CBW] = cb
    return _bf16(c)


def fm(w, M=None):
    k = w.shape[0] // 128
    return np.ascontiguousarray(w.reshape(k, 128, w.shape[1]).transpose(1, 0, 2))


def make_wstream(inp, depth):
    out = np.zeros((depth * SLOTS_PER_LAYER, 128, 2048), dtype=np.float32)

    def put(idx, arr):
        a = np.ascontiguousarray(arr, dtype=np.float32).reshape(128, -1)
        out[idx, :, :a.shape[1]] = a

    for l in range(depth):
        b = l * SLOTS_PER_LAYER
        w_in = np.asarray(inp['w_in'][l])
        w_out = np.asarray(inp['w_out'][l])

        def fm2(cols0):
            return np.stack([fm(w_in[:, cols0 + i * 128: cols0 + (i + 1) * 128]) for i in range(2)], axis=1)
        def fmA(cols0, hp):
            outs = []
            for hh in range(2):
                c0 = cols0 + (2 * hp + hh) * 64
                m0 = w_in[:, c0:c0 + 32]
                m1 = w_in[:, c0 + 32:c0 + 64]
                outs.append(fm(np.concatenate([m0, m1, m1, m0], axis=1)))
            return np.stack(outs, axis=1)
        put(b + 0, fm(w_in[:, 512:768]))
        put(b + 1, fmA(0, 0))
        put(b + 2, fmA(256, 0))
        put(b + 3, fmA(0, 1))
        put(b + 4, fmA(256, 1))
        put(b + 5, fm(w_out[0:256, :]))
        put(b + 6, fm2(768))
        put(b + 7, fm2(1024))
        put(b + 8, fm(w_in[:, 1280:1536]))
        put(b + 9, fm(w_out[256:512, :]))
        put(b + 10, fm2(1536))
        kr = w_in[:, 1920:1952]
        kr_sw = np.concatenate([kr[:, 16:32], kr[:, 0:16]], axis=1)
        put(b + 11, np.concatenate([fm(w_in[:, 1792:1920]).reshape(128, -1), fm(kr).reshape(128, -1),
                                    fm(kr_sw).reshape(128, -1)], axis=1))
        uq = np.asarray(inp['c_w_uq'][l])
        uq_h = uq.reshape(256, 8, 96)
        uqs = np.concatenate([uq_h[:, :, 80:96], uq_h[:, :, 64:80]], axis=2).reshape(256, 256)
        put(b + 12, np.concatenate([fm(uq).reshape(128, -1), fm(uqs).reshape(128, -1)], axis=1))
        put(b + 13, np.asarray(inp['c_w_ukv'][l]))
        put(b + 14, fm(w_out[512:768, :]))
        put(b + 15, fm(w_out[768:1024, :]))
        w_up = np.asarray(inp['w_up'][l])
        w_down = np.asarray(inp['w_down'][l])
        o = b + 16
        for grp in GROUPS:
            for j in grp:
                put(o, np.concatenate([fm(w_up[:, j * 128:(j + 1) * 128]).reshape(128, -1),
                                       fm(w_up[:, DFF + j * 128: DFF + (j + 1) * 128]).reshape(128, -1)], axis=1))
                o += 1
            for oc in range(8):
                rows = np.stack([w_down[j * 128:(j + 1) * 128, oc * 128:(oc + 1) * 128] for j in grp], axis=1)
                put(o, rows)
                o += 1
    return out


def make_vecs(inp, depth):
    v = np.zeros((128, NV), dtype=np.float32)

    def colz(a):
        a = np.asarray(a, dtype=np.float32)
        return a.reshape(-1, 128).T

    for l in range(depth):
        o = l * VL
        v[:, o + 0:o + 8] = colz(inp['g_attn'][l])
        v[:, o + 8:o + 16] = colz(inp['g_ffn'][l])
        v[:, o + 16:o + 18] = colz(inp['c_g_q'][l])
        v[:, o + 18:o + 19] = colz(inp['c_g_kv'][l])
        for k in range(3):
            v[:, o + 19 + 44 * k: o + 19 + 44 * (k + 1)] = colz(inp['conv_w'][l][k])
        v[:, o + 151:o + 195] = colz(inp['conv_b'][l])
        v[:, o + 195:o + 259] = np.broadcast_to(np.asarray(inp['a_subln'][l], dtype=np.float32)[None, :], (128, 64))
        for i, nm in enumerate(('a_lq1', 'a_lk1', 'a_lq2', 'a_lk2')):
            v[:, o + 259 + 32 * i: o + 259 + 32 * (i + 1)] = np.broadcast_to(
                np.asarray(inp[nm][l], dtype=np.float32)[None, :], (128, 32))
    v[:, DEPTH * VL: DEPTH * VL + 8] = colz(inp['g_final'])
    return v


_CACHE = {}


def kernel(**inputs):
    x = np.asarray(inputs['x'], dtype=np.float32)
    nb = x.shape[0]
    assert nb == NCORES * NSEQ
    wst = make_wstream(inputs, DEPTH)
    vecs = make_vecs(inputs, DEPTH)
    c16 = make_consts()
    if 'nc' not in _CACHE:
        _CACHE['nc'] = Builder().build()
    nc = _CACHE['nc']
    in_maps = []
    for i in range(NCORES):
        xT = np.ascontiguousarray(x[i * NSEQ:(i + 1) * NSEQ].transpose(0, 2, 1))
        in_maps.append({"xT": xT, "wst": wst, "vecs": vecs, "c16": c16})
    res = run_bass_kernel_spmd(nc, in_maps, core_ids=list(range(NCORES)))
    outs = [np.asarray(r["outT"]).transpose(0, 2, 1) for r in res.results]
    return np.ascontiguousarray(np.concatenate(outs, axis=0)).astype(np.float32)
```
